# Optimizing a Trainium2 kernel written in Bass

```python
import math
import jax, jax.numpy as jnp
from jax import lax
import numpy as np

D_MODEL = 1024
BATCH = 2
SEQ = 8192
DEPTH = 2

CTX_LEN = 256
GRID_W = 64
H_A = 8
DK_A = 128
DV_A = 128
W_A = H_A * DV_A
QKV_A = 2 * H_A * DK_A + W_A
CONV_W = 5
H_B = 4
DQK_B = 128
DV_B = 256
W_B = H_B * DV_B
CHUNK = 64
EPS = 1e-6

COL_NAMES = ('a_q', 'a_k', 'a_v', 'a_z', 'a_beta', 'a_alpha',
             'b_q', 'b_k', 'b_v', 'b_o', 'b_z', 'b_i', 'b_f',
             'g_a', 'g_b')
COL_SIZES = (H_A * DK_A, H_A * DK_A, W_A, W_A, 2 * H_A, 2 * H_A,
             H_B * DQK_B, H_B * DQK_B, W_B, W_B, W_B, 2 * H_B, 2 * H_B,
             D_MODEL, D_MODEL)
COL_OFFSETS = tuple(int(o) for o in np.cumsum(COL_SIZES)[:-1])
N_COLS = int(sum(COL_SIZES))

kernel_name = 'hybrid_deltanet_mlstm_prefix_dit'


def rmsnorm(x, g):
    x32 = x.astype(jnp.float32)
    y = x32 * lax.rsqrt(jnp.mean(x32 * x32, axis=-1, keepdims=True) + EPS)
    return (y * g.astype(jnp.float32)).astype(x.dtype)


def l2norm(x):
    return x * lax.rsqrt(jnp.sum(x * x, axis=-1, keepdims=True) + EPS)


def split_cols(p):
    return dict(zip(COL_NAMES, jnp.split(p, COL_OFFSETS, axis=-1)))


def centred_dwconv(x, w):
    return lax.conv_general_dilated(
        x, w[:, None, :].astype(x.dtype), window_strides=(1,),
        padding=((CONV_W // 2, CONV_W // 2),),
        dimension_numbers=('NWC', 'WIO', 'NWC'),
        feature_group_count=x.shape[-1])


def flip_t(a):
    return jnp.flip(a, axis=1)


def to_chunks(a):
    b, t, h = a.shape[:3]
    a = a.reshape(b, t // CHUNK, CHUNK, h, *a.shape[3:])
    return jnp.moveaxis(a, (1, 3), (0, 2))


def from_chunks(a):
    a = jnp.moveaxis(a, (0, 2), (1, 3))
    b, n, l, h = a.shape[:4]
    return a.reshape(b, n * l, h, *a.shape[4:])


def to_colmajor(a, rows):
    b, t = a.shape[:2]
    rest = a.shape[2:]
    return a.reshape(b, rows, GRID_W, *rest).swapaxes(1, 2).reshape(b, t, *rest)


def from_colmajor(a, rows):
    b, t = a.shape[:2]
    rest = a.shape[2:]
    return a.reshape(b, GRID_W, rows, *rest).swapaxes(1, 2).reshape(b, t, *rest)


def gated_delta_scan(q, k, v, beta, g, s0):
    qc, kc, vc, bc, gc = map(to_chunks, (q, k, v, beta, g))
    gcum = jnp.cumsum(gc, axis=-1)
    idx = jnp.arange(CHUNK)
    incl = idx[:, None] >= idx[None, :]
    strict = idx[:, None] > idx[None, :]
    dmask = jnp.exp(jnp.where(incl, gcum[..., :, None] - gcum[..., None, :], -jnp.inf))
    kb = kc * bc[..., None]
    a = jnp.where(strict, jnp.einsum('nbhid,nbhjd->nbhij', kb, kc) * dmask, 0.0)
    eye = jnp.eye(CHUNK, dtype=a.dtype)
    tinv = lax.linalg.triangular_solve(a + eye, jnp.broadcast_to(eye, a.shape),
                                       left_side=True, lower=True)
    u = jnp.einsum('nbhij,nbhjd->nbhid', tinv, vc * bc[..., None])
    w = jnp.einsum('nbhij,nbhjd->nbhid', tinv, kb * jnp.exp(gcum)[..., None])
    qk = jnp.einsum('nbhid,nbhjd->nbhij', qc, kc) * dmask
    q_dec = qc * jnp.exp(gcum)[..., None]
    g_last = gcum[..., -1]
    k_dec = kc * jnp.exp(g_last[..., None] - gcum)[..., None]

    def step(s, xs):
        u_c, w_c, qk_c, qd_c, kd_c, gl_c = xs
        v_new = u_c - jnp.einsum('bhld,bhde->bhle', w_c, s)
        o = jnp.einsum('bhld,bhde->bhle', qd_c, s) + jnp.einsum('bhls,bhse->bhle', qk_c, v_new)
        s = s * jnp.exp(gl_c)[..., None, None] + jnp.einsum('bhld,bhle->bhde', kd_c, v_new)
        return s, o

    s_fin, o = lax.scan(step, s0, (u, w, qk, q_dec, k_dec, g_last))
    return from_chunks(o), s_fin


def mlstm_scan(q, k, v, ig, lf, state0):
    qc, kc, vc, ic, fc = map(to_chunks, (q, k, v, ig, lf))
    b = jnp.cumsum(fc, axis=-1)
    idx = jnp.arange(CHUNK)
    incl = idx[:, None] >= idx[None, :]
    dlog = jnp.where(incl, b[..., :, None] - b[..., None, :] + ic[..., None, :], -jnp.inf)
    dmax = jnp.max(dlog, axis=-1)
    qk = jnp.einsum('nbhid,nbhjd->nbhij', qc, kc)
    a_end = b[..., -1:] - b + ic
    a_end_max = jnp.max(a_end, axis=-1)

    def step(carry, xs):
        cm, nv, m = carry
        q_c, k_c, v_c, b_c, dl_c, dm_c, qk_c, ae_c, aem_c = xs
        m_t = jnp.maximum(b_c + m[..., None], dm_c)
        inter = jnp.exp(b_c + m[..., None] - m_t)
        wts = jnp.exp(dl_c - m_t[..., None]) * qk_c
        num = inter[..., None] * jnp.einsum('bhld,bhde->bhle', q_c, cm) \
            + jnp.einsum('bhls,bhse->bhle', wts, v_c)
        den = inter * jnp.einsum('bhld,bhd->bhl', q_c, nv) + jnp.sum(wts, axis=-1)
        h = num / jnp.maximum(jnp.abs(den), jnp.exp(-m_t))[..., None]
        m_new = jnp.maximum(b_c[..., -1] + m, aem_c)
        kw = k_c * jnp.exp(ae_c - m_new[..., None])[..., None]
        sc = jnp.exp(b_c[..., -1] + m - m_new)
        cm = sc[..., None, None] * cm + jnp.einsum('bhld,bhle->bhde', kw, v_c)
        nv = sc[..., None] * nv + jnp.sum(kw, axis=-2)
        return (cm, nv, m_new), h

    state, h = lax.scan(step, state0, (qc, kc, vc, b, dlog, dmax, qk, a_end, a_end_max))
    return from_chunks(h), state


def delta_inputs(p, conv_w, a_log, dt_bias):
    bsz, t = p['a_q'].shape[:2]
    qkv = jnp.concatenate([p['a_q'], p['a_k'], p['a_v']], axis=-1)
    qkv = jax.nn.silu(centred_dwconv(qkv, conv_w)).astype(jnp.float32)
    q, k, v = jnp.split(qkv, (H_A * DK_A, 2 * H_A * DK_A), axis=-1)
    q = l2norm(q.reshape(bsz, t, H_A, DK_A)) * (DK_A ** -0.5)
    k = l2norm(k.reshape(bsz, t, H_A, DK_A))
    v = v.reshape(bsz, t, H_A, DV_A)
    beta = jax.nn.sigmoid(p['a_beta'].astype(jnp.float32).reshape(bsz, t, 2, H_A))
    g = -jnp.exp(a_log.astype(jnp.float32)) * jax.nn.softplus(
        p['a_alpha'].astype(jnp.float32).reshape(bsz, t, 2, H_A) + dt_bias.astype(jnp.float32))
    return q, k, v, beta, g


def delta_bidir(q, k, v, beta, g, s_f0, s_b0):
    o_f, s_f = gated_delta_scan(q, k, v, beta[:, :, 0], g[:, :, 0], s_f0)
    o_b, s_b = gated_delta_scan(flip_t(q), flip_t(k), flip_t(v),
                                flip_t(beta[:, :, 1]), flip_t(g[:, :, 1]), s_b0)
    return o_f + flip_t(o_b), s_f, s_b


def mlstm_inputs(p, i_bias, f_bias):
    bsz, t = p['b_q'].shape[:2]
    q = p['b_q'].astype(jnp.float32).reshape(bsz, t, H_B, DQK_B)
    k = p['b_k'].astype(jnp.float32).reshape(bsz, t, H_B, DQK_B) * (DQK_B ** -0.5)
    v = p['b_v'].astype(jnp.float32).reshape(bsz, t, H_B, DV_B)
    ig = p['b_i'].astype(jnp.float32).reshape(bsz, t, 2, H_B) + i_bias.astype(jnp.float32)
    lf = jax.nn.log_sigmoid(p['b_f'].astype(jnp.float32).reshape(bsz, t, 2, H_B)
                            + f_bias.astype(jnp.float32))
    return q, k, v, ig, lf


def mlstm_bidir(q, k, v, ig, lf, st_f0, st_b0):
    h_f, st_f = mlstm_scan(q, k, v, ig[:, :, 0], lf[:, :, 0], st_f0)
    h_b, st_b = mlstm_scan(flip_t(q), flip_t(k), flip_t(v),
                           flip_t(ig[:, :, 1]), flip_t(lf[:, :, 1]), st_b0)
    return h_f + flip_t(h_b), st_f, st_b


def merge_branches(p, oa, hb, norm_a_g, norm_b_g, w_pa, w_pb, w_out):
    bsz, t = oa.shape[:2]
    dt = p['a_z'].dtype
    ya = rmsnorm(oa, norm_a_g).reshape(bsz, t, W_A).astype(dt) * jax.nn.silu(p['a_z'])
    yb = (rmsnorm(hb, norm_b_g).reshape(bsz, t, W_B)
          * jax.nn.sigmoid(p['b_o'].astype(jnp.float32))).astype(dt) * jax.nn.silu(p['b_z'])
    y = jax.nn.sigmoid(p['g_a']) * (ya @ w_pa) + jax.nn.sigmoid(p['g_b']) * (yb @ w_pb)
    return y @ w_out


def hybrid_layer(x, xc, mod, mod_c, norm_g, w_in, conv_w, a_log, dt_bias, norm_a_g,
                 i_bias, f_bias, norm_b_g, w_pa, w_pb, w_out, need_ctx):
    bsz, t, _ = x.shape
    rows = t // GRID_W
    shift, scale, gate = jnp.split(mod, 3, axis=-1)
    shift_c, scale_c, gate_c = jnp.split(mod_c, 3, axis=-1)
    h = rmsnorm(x, norm_g) * (1.0 + scale[:, None, :]) + shift[:, None, :]
    hc = rmsnorm(xc, norm_g) * (1.0 + scale_c) + shift_c
    p = split_cols(h @ w_in)
    pc = split_cols(hc @ w_in)

    s0 = jnp.zeros((bsz, H_A, DK_A, DV_A), jnp.float32)
    oa_c, s_f, s_b = delta_bidir(*delta_inputs(pc, conv_w, a_log, dt_bias), s0, s0)
    oa, _, _ = delta_bidir(*delta_inputs(p, conv_w, a_log, dt_bias), s_f, s_b)

    st0 = (jnp.zeros((bsz, H_B, DQK_B, DV_B), jnp.float32),
           jnp.zeros((bsz, H_B, DQK_B), jnp.float32),
           jnp.zeros((bsz, H_B), jnp.float32))
    hb_c, st_f, st_b = mlstm_bidir(*mlstm_inputs(pc, i_bias, f_bias), st0, st0)
    qb, kb, vb, igb, lfb = (to_colmajor(a, rows) for a in mlstm_inputs(p, i_bias, f_bias))
    hb, _, _ = mlstm_bidir(qb, kb, vb, igb, lfb, st_f, st_b)
    hb = from_colmajor(hb, rows)

    x = x + gate[:, None, :] * merge_branches(p, oa, hb, norm_a_g, norm_b_g, w_pa, w_pb, w_out)
    if need_ctx:
        xc = xc + gate_c * merge_branches(pc, oa_c, hb_c, norm_a_g, norm_b_g, w_pa, w_pb, w_out)
    return x, xc


def setup_inputs(seed: int = 0) -> dict:
    key = jax.random.key(seed)
    ks = jax.random.split(key, 20)
    f32 = jnp.float32

    def nrm(k, shape, s):
        return jax.random.normal(k, shape, f32) * s

    x = nrm(ks[0], (BATCH, SEQ, D_MODEL), 1.0)
    c = nrm(ks[1], (BATCH, D_MODEL), 1.0)
    ctx = nrm(ks[2], (BATCH, CTX_LEN, D_MODEL), 1.0)
    c_ctx = nrm(ks[3], (D_MODEL,), 1.0)
    ada_w = nrm(ks[4], (DEPTH, D_MODEL, 3 * D_MODEL), 0.5 * D_MODEL ** -0.5)
    ada_b = nrm(ks[5], (DEPTH, 3 * D_MODEL), 0.02)
    norm_g = 1.0 + nrm(ks[6], (DEPTH, D_MODEL), 0.02)
    w_in = nrm(ks[7], (DEPTH, D_MODEL, N_COLS), D_MODEL ** -0.5)
    conv_w = nrm(ks[8], (DEPTH, CONV_W, QKV_A), CONV_W ** -0.5)
    a_log = jnp.log(jax.random.uniform(ks[9], (DEPTH, 2, H_A), f32, 1.0, 16.0))
    dt = jnp.exp(jax.random.uniform(ks[10], (DEPTH, 2, H_A), f32,
                                    math.log(1e-3), math.log(1e-1)))
    dt_bias = dt + jnp.log(-jnp.expm1(-dt))
    norm_a_g = 1.0 + nrm(ks[11], (DEPTH, DV_A), 0.02)
    i_bias = nrm(ks[12], (DEPTH, 2, H_B), 0.1)
    f_bias = jnp.linspace(3.0, 6.0, H_B, dtype=f32) + nrm(ks[13], (DEPTH, 2, H_B), 0.1)
    norm_b_g = 1.0 + nrm(ks[14], (DEPTH, DV_B), 0.02)
    w_pa = nrm(ks[15], (DEPTH, W_A, D_MODEL), W_A ** -0.5)
    w_pb = nrm(ks[16], (DEPTH, W_B, D_MODEL), W_B ** -0.5)
    w_out = nrm(ks[17], (DEPTH, D_MODEL, D_MODEL), D_MODEL ** -0.5)
    final_g = 1.0 + nrm(ks[18], (D_MODEL,), 0.02)
    return {'x': x, 'c': c, 'ctx': ctx, 'c_ctx': c_ctx, 'ada_w': ada_w, 'ada_b': ada_b,
            'norm_g': norm_g, 'w_in': w_in, 'conv_w': conv_w, 'a_log': a_log,
            'dt_bias': dt_bias, 'norm_a_g': norm_a_g, 'i_bias': i_bias, 'f_bias': f_bias,
            'norm_b_g': norm_b_g, 'w_pa': w_pa, 'w_pb': w_pb, 'w_out': w_out,
            'final_g': final_g}


def reference(x, c, ctx, c_ctx, ada_w, ada_b, norm_g, w_in, conv_w, a_log, dt_bias,
              norm_a_g, i_bias, f_bias, norm_b_g, w_pa, w_pb, w_out, final_g):
    sc = jax.nn.silu(c)
    sc_ctx = jax.nn.silu(c_ctx)
    xc = ctx
    for l in range(DEPTH):
        mod = sc @ ada_w[l] + ada_b[l]
        mod_c = sc_ctx @ ada_w[l] + ada_b[l]
        x, xc = hybrid_layer(x, xc, mod, mod_c, norm_g[l], w_in[l], conv_w[l], a_log[l],
                             dt_bias[l], norm_a_g[l], i_bias[l], f_bias[l], norm_b_g[l],
                             w_pa[l], w_pb[l], w_out[l], need_ctx=(l < DEPTH - 1))
    return rmsnorm(x, final_g)
```

```python
import numpy as np
from contextlib import ExitStack
import concourse.bass as bass
import concourse.mybir as mybir
from concourse.bass_utils import run_bass_kernel_spmd

F32 = mybir.dt.float32
BF16 = mybir.dt.bfloat16
AF = mybir.ActivationFunctionType
ALU = mybir.AluOpType
AX = mybir.AxisListType

EP = 20000


class Buf:
    def __init__(self, prog, t, name, space):
        self.p = prog
        self.t = t
        self.name = name
        self.space = space
        self.w = None
        self.r = []
        self.sem_in = None
        self.n_in = 0
        self.sem_out = None
        self.n_out = 0
        self.acc = {}

    def __getitem__(self, idx):
        return self.t[idx]


class Prog:
    ENGS = ("pe", "act", "dve", "pool", "sp")

    def __init__(self):
        self.nc = bass.Bass("TRN2", target_bir_lowering=False)
        self.st = ExitStack()
        self.ops = {e: [] for e in self.ENGS}
        self.cnt = {e: 0 for e in self.ENGS}
        self.sems = {e: [] for e in self.ENGS}
        self.known = {e: {} for e in self.ENGS}
        self.nsem = 0
        self.out_deps = []
        self.psum_banks = []
        self.psum_i = 0

    def sem(self, name):
        self.nsem += 1
        return self.st.enter_context(self.nc.semaphore(name))

    def dram(self, name, shape, dtype, kind):
        return self.nc.dram_tensor(name, list(shape), dtype, kind=kind).ap()

    def dram_buf(self, name, shape, dtype):
        t = self.nc.dram_tensor(name, list(shape), dtype, kind="Internal").ap()
        return Buf(self, t, name, "dram")

    def sbuf(self, name, shape, dtype=F32):
        t = self.st.enter_context(self.nc.sbuf_tensor(name, list(shape), dtype))
        return Buf(self, t, name, "sbuf")

    def psum(self, name, shape, dtype=F32):
        t = self.st.enter_context(self.nc.psum_tensor(name, list(shape), dtype))
        return Buf(self, t, name, "psum")

    def alloc_psum_banks(self, n=8):
        self.psum_banks = [self.psum(f"bank{i}", [128, 512], F32) for i in range(n)]

    def bank(self):
        b = self.psum_banks[self.psum_i % len(self.psum_banks)]
        self.psum_i += 1
        return b

    def _eng_sem(self, e, idx):
        ep = (idx - 1) // EP
        while len(self.sems[e]) <= ep:
            self.sems[e].append(self.sem(f"s_{e}_{len(self.sems[e])}"))
        return self.sems[e][ep], (idx - 1) % EP + 1

    def _collect(self, eng, reads, writes):
        deps = []
        for b in reads:
            if b.w is not None:
                deps.append(b.w)
        for b in writes:
            if b.w is not None:
                deps.append(b.w)
            deps.extend(b.r)
        for b in list(reads) + list(writes):
            if b.space == "psum":
                for e2, i2 in b.acc.items():
                    if e2 != eng:
                        deps.append(("eng", e2, i2, "p"))
        waits = {}
        for d in deps:
            if d[0] == "eng":
                _, e, idx, kind = d
                if e == eng:
                    continue
                s, v = self._eng_sem(e, idx)
            else:
                _, s, v = d
            key = id(s)
            if key not in waits or waits[key][1] < v:
                waits[key] = (s, v)
        if eng in ("act", "dve", "pool"):
            m = 0
            for b in reads:
                if b.w is not None and b.w[0] == "eng" and b.w[1] == eng:
                    m = max(m, b.w[2])
            if m:
                s, v = self._eng_sem(eng, m)
                key = id(s)
                if key not in waits or waits[key][1] < v:
                    waits[key] = (s, v)
        out = []
        kn = self.known[eng]
        for key, (s, v) in waits.items():
            if kn.get(key, 0) >= v:
                continue
            kn[key] = v
            out.append((s, v))
        return out

    def op(self, eng, fn, reads=(), writes=()):
        waits = self._collect(eng, reads, writes)
        self.cnt[eng] += 1
        idx = self.cnt[eng]
        s, v = self._eng_sem(eng, idx)
        self.ops[eng].append((waits, fn, (s, 1)))
        dep = ("eng", eng, idx, "c")
        for b in list(reads) + list(writes):
            if b.space == "psum":
                b.acc[eng] = idx
        for b in writes:
            b.w = dep
            b.r = []
        for b in reads:
            if b not in writes:
                b.r.append(dep)
        return idx

    def dma(self, out_ap, in_ap, dst=None, src=None, q="sp"):
        reads = [src] if src is not None else []
        writes = [dst] if dst is not None else []
        waits = self._collect(q, reads, writes)
        if dst is not None:
            if dst.sem_in is None:
                dst.sem_in = self.sem(f"di_{dst.name}")
            dst.n_in += 1
            s, v = dst.sem_in, 16 * dst.n_in
        elif src is not None:
            if src.sem_out is None:
                src.sem_out = self.sem(f"do_{src.name}")
            src.n_out += 1
            s, v = src.sem_out, 16 * src.n_out
        else:
            raise ValueError("dma needs a tracked side")
        dep = ("dma", s, v)
        second = None
        if dst is not None and src is not None:
            pass
        self.ops[q].append((waits, lambda e, o=out_ap, i=in_ap: e.dma_start(out=o, in_=i), (s, 16)))
        if dst is not None:
            dst.w = dep
            dst.r = []
        if src is not None:
            src.r.append(dep)
        if dst is None:
            self.out_deps.append(dep)
        return dep

    def mm(self, out, lhsT, rhs, start, stop, reads, writes):
        return self.op("pe", lambda e: e.matmul(out, lhsT, rhs, start=start, stop=stop),
                       reads, writes)

    def transpose(self, out, in_, ident, reads, writes):
        return self.op("pe", lambda e: e.transpose(out, in_, ident), reads, writes)

    def act(self, out, in_, func, reads, writes, bias=None, scale=None, accum_out=None, eng="act"):
        kw = {}
        if bias is not None:
            kw["bias"] = bias
        if scale is not None:
            kw["scale"] = scale
        if accum_out is not None:
            kw["accum_out"] = accum_out
        return self.op("act", lambda e: e.activation(out, in_, func, **kw), reads, writes)

    def tt(self, out, in0, in1, op, reads, writes, eng="dve"):
        return self.op(eng, lambda e: e.tensor_tensor(out, in0, in1, op), reads, writes)

    def ts(self, out, in0, s1, s2, op0, op1, reads, writes, eng="dve"):
        if op1 is None:
            return self.op(eng, lambda e: e.tensor_scalar(out, in0, s1, None, op0), reads, writes)
        return self.op(eng, lambda e: e.tensor_scalar(out, in0, s1, s2, op0, op1), reads, writes)

    def stt(self, out, in0, scalar, in1, op0, op1, reads, writes):
        return self.op("dve", lambda e: e.scalar_tensor_tensor(out, in0, scalar, in1, op0, op1),
                       reads, writes)

    def copy(self, out, in_, reads, writes, eng="dve"):
        if eng == "act":
            return self.op("act", lambda e: e.copy(out, in_), reads, writes)
        return self.op(eng, lambda e: e.tensor_copy(out, in_), reads, writes)

    def memset(self, out, val, writes, eng="dve"):
        return self.op(eng, lambda e: e.memset(out, val), (), writes)

    def finish(self):
        nc = self.nc
        fin_waits = {}
        for d in self.out_deps:
            _, s, v = d
            if id(s) not in fin_waits or fin_waits[id(s)][1] < v:
                fin_waits[id(s)] = (s, v)
        for e in self.ENGS:
            if e == "sp" or self.cnt[e] == 0:
                continue
            s, v = self._eng_sem(e, self.cnt[e])
            fin_waits[id(s)] = (s, v)
        engmap = {"pe": "tensor", "act": "scalar", "dve": "vector", "pool": "gpsimd", "sp": "sync"}
        with nc.Block() as block:
            for e in self.ENGS:
                ops = self.ops[e]
                last = (e == "sp")
                if not ops and not last:
                    continue

                def body(eng, ops=ops, last=last):
                    for waits, fn, (s, n) in ops:
                        for (ws, wv) in waits:
                            eng.wait_ge(ws, wv)
                        fn(eng).then_inc(s, n)
                    if last:
                        for (ws, wv) in fin_waits.values():
                            eng.wait_ge(ws, wv)

                getattr(block, engmap[e])(body)
        self.st.close()
        return nc


D = 1024
BATCH = 2
SEQ = 8192
CTX = 256
TT = CTX + SEQ
H_A, DK_A, DV_A = 8, 128, 128
H_B, DQK_B, DV_B = 4, 128, 256
W_A = 1024
W_B = 1024
GRID_W = 64
EPS = 1e-6
NCORES = 8
O_AQ, O_AK, O_AV, O_AZ = 0, 1024, 2048, 3072
O_ABETA, O_AALPHA = 4096, 4112
O_BQ, O_BK, O_BV, O_BO, O_BZ = 4128, 4640, 5152, 6176, 7200
O_BI, O_BF = 8224, 8232
O_GA, O_GB = 8240, 9264
NCH = 20
NGATE = 12


def core_cols(g):
    cols = []
    hA = (2 * g, 2 * g + 1)
    for base in (O_AQ, O_AK, O_AV, O_AZ):
        for h in hA:
            cols.extend(range(base + h * 128, base + (h + 1) * 128))
    for base in (O_BQ, O_BK):
        cols.extend(range(base + g * 128, base + (g + 1) * 128))
    for base in (O_BV, O_BO, O_BZ):
        cols.extend(range(base + g * 256, base + (g + 1) * 256))
    for base in (O_GA, O_GB):
        cols.extend(range(base + g * 256, base + (g + 1) * 256))
    assert len(cols) == NCH * 128
    gates = []
    for base in (O_ABETA, O_AALPHA):
        for d in range(2):
            for h in hA:
                gates.append(base + d * H_A + h)
    for base in (O_BI, O_BF):
        for d in range(2):
            gates.append(base + d * H_B + g)
    assert len(gates) == NGATE
    return cols, gates


CH_AQ, CH_AK, CH_AV, CH_AZ = (0, 1), (2, 3), (4, 5), (6, 7)
CH_BQ, CH_BK = 8, 9
CH_BV, CH_BO, CH_BZ = (10, 11), (12, 13), (14, 15)
CH_GA, CH_GB = (16, 17), (18, 19)


def run(nc, in_maps):
    res = run_bass_kernel_spmd(nc, in_maps, core_ids=list(range(NCORES)))
    return res.results


def build_mod():
    P = Prog()
    W = P.dram("W", [6, D, 128], F32, "ExternalInput")
    cc = P.dram("cc", [128, 8, 4], F32, "ExternalInput")
    bias = P.dram("bias", [128, 6], F32, "ExternalInput")
    out = P.dram("out", [6, 128, 4], F32, "ExternalOutput")
    P.alloc_psum_banks(2)
    ccs = P.sbuf("ccs", [128, 8, 4])
    scs = P.sbuf("scs", [128, 8, 4])
    bs = P.sbuf("bs", [128, 6])
    os_ = P.sbuf("os", [128, 6, 4])
    P.dma(ccs[:], cc[:, :, :], dst=ccs)
    P.dma(bs[:], bias[:, :], dst=bs)
    P.act(scs[:], ccs[:], AF.Silu, [ccs], [scs])
    wts = [P.sbuf(f"w{i}", [128, 8, 128]) for i in range(2)]
    for i in range(6):
        wt = wts[i % 2]
        P.dma(wt[:], W[i].rearrange("(k p) c -> p k c", p=128), dst=wt)
        ps = P.bank()
        for k in range(8):
            P.mm(ps[:, 0:4], wt[:, k, :], scs[:, k, :], k == 0, k == 7, [wt, scs], [ps])
        P.ts(os_[:, i, :], ps[:, 0:4], bs[:, i:i + 1], None, ALU.add, None, [ps, bs], [os_])
    P.dma(out.rearrange("i p n -> p i n"), os_[:], src=os_)
    return P.finish()


def stage_mod(inp):
    nc = build_mod()
    c4 = np.stack([inp["c"][0], inp["c"][1], inp["c_ctx"], inp["c_ctx"]], axis=-1)
    cc = np.ascontiguousarray(c4.reshape(8, 128, 4).transpose(1, 0, 2))
    maps = []
    for j in range(NCORES):
        Ws, bs = [], []
        for i in range(6):
            job = j * 6 + i
            l, fc = divmod(job, 24)
            Ws.append(inp["ada_w"][l][:, fc * 128:(fc + 1) * 128])
            bs.append(inp["ada_b"][l][fc * 128:(fc + 1) * 128])
        maps.append({"W": np.ascontiguousarray(np.stack(Ws)), "cc": cc,
                     "bias": np.ascontiguousarray(np.stack(bs, axis=1))})
    res = run(nc, maps)
    mod = np.zeros((2, 3, 3 * D), np.float32)
    for j in range(NCORES):
        o = res[j]["out"]
        for i in range(6):
            job = j * 6 + i
            l, fc = divmod(job, 24)
            for n in range(3):
                mod[l, n, fc * 128:(fc + 1) * 128] = o[i, :, n]
    return mod


P_TILES = [(0, CTX)] + [(CTX + i * 512, CTX + (i + 1) * 512) for i in range(SEQ // 512)]


def build_proj():
    P = Prog()
    xT = P.dram("xT", [D, TT], F32, "ExternalInput")
    Wg = P.dram("Wg", [D, NCH * 128 + 16], F32, "ExternalInput")
    prm = P.dram("prm", [128, 5, 8], F32, "ExternalInput")
    pT = P.dram("pT", [NCH * 128, TT], F32, "ExternalOutput")
    gT = P.dram("gT", [16, TT], F32, "ExternalOutput")
    P.alloc_psum_banks(8)
    NW = NCH * 128 + 16
    W16 = P.sbuf("W16", [128, 8, NW], BF16)
    ones = P.sbuf("ones", [128, 128])
    P.memset(ones[:], 1.0, [ones])
    prms = P.sbuf("prms", [128, 5, 8])
    P.dma(prms[:], prm[:, :, :], dst=prms)
    GS = P.sbuf("GS", [128, 4, 8])
    P.stt(GS[:, 0, :], prms[:, 1, :], 1.0, prms[:, 0, :], ALU.add, ALU.mult, [prms], [GS])
    P.copy(GS[:, 1, :], prms[:, 2, :], [prms], [GS])
    P.stt(GS[:, 2, :], prms[:, 3, :], 1.0, prms[:, 0, :], ALU.add, ALU.mult, [prms], [GS])
    P.copy(GS[:, 3, :], prms[:, 4, :], [prms], [GS])
    Wv = Wg.rearrange("(k p) c -> p k c", p=128)
    stg = [P.sbuf(f"wst{i}", [128, 8, 512]) for i in range(2)]
    c0 = 0
    i = 0
    while c0 < NW:
        c1 = min(NW, c0 + 512)
        s = stg[i % 2]
        P.dma(s[:, :, 0:c1 - c0], Wv[:, :, c0:c1], dst=s)
        P.copy(W16[:, :, c0:c1], s[:, :, 0:c1 - c0], [s], [W16], eng=("dve" if i % 2 == 0 else "pool"))
        c0 = c1
        i += 1
    xv = xT.rearrange("(k p) t -> p k t", p=128)
    xts = [P.sbuf(f"xt{i}", [128, 8, 512]) for i in range(2)]
    sq = P.sbuf("sq", [128, 8, 512])
    hTs = [P.sbuf(f"hT{i}", [128, 8, 512], BF16) for i in range(2)]
    rstd = P.sbuf("rstd", [128, 512])
    tmp = [P.sbuf(f"tmp{i}", [128, 512]) for i in range(2)]
    evs = [P.sbuf(f"ev{i}", [128, 512]) for i in range(4)]
    gev = P.sbuf("gev", [16, 512])
    nev = 0
    for ti, (t0, t1) in enumerate(P_TILES):
        n = t1 - t0
        xt = xts[ti % 2]
        hT = hTs[ti % 2]
        P.dma(xt[:, :, 0:n], xv[:, :, t0:t1], dst=xt)
        P.act(sq[:, :, 0:n], xt[:, :, 0:n], AF.Square, [xt], [sq])
        ss = P.bank()
        for k in range(8):
            P.mm(ss[:, 0:n], ones[:], sq[:, k, 0:n], k == 0, k == 7, [ones, sq], [ss])
        P.ts(rstd[:, 0:n], ss[:, 0:n], 1.0 / D, EPS, ALU.mult, ALU.add, [ss], [rstd])
        P.act(rstd[:, 0:n], rstd[:, 0:n], AF.Sqrt, [rstd], [rstd])
        P.op("dve", lambda e, o=rstd[:, 0:n], i_=rstd[:, 0:n]: e.reciprocal(o, i_), [rstd], [rstd])
        gi = 2 if ti == 0 else 0
        for k in range(8):
            tm = tmp[k % 2]
            P.tt(tm[:, 0:n], xt[:, k, 0:n], rstd[:, 0:n], ALU.mult, [xt, rstd], [tm])
            P.act(hT[:, k, 0:n], tm[:, 0:n], AF.Identity, [tm, GS], [hT],
                  scale=GS[:, gi, k:k + 1], bias=GS[:, gi + 1, k:k + 1])
        for c in range(NCH):
            ps = P.bank()
            for k in range(8):
                P.mm(ps[:, 0:n], W16[:, k, c * 128:(c + 1) * 128], hT[:, k, 0:n], k == 0, k == 7,
                     [W16, hT], [ps])
            ev = evs[nev % 4]
            P.copy(ev[:, 0:n], ps[:, 0:n], [ps], [ev], eng=("act" if nev % 2 else "dve"))
            nev += 1
            P.dma(pT[c * 128:(c + 1) * 128, t0:t1], ev[:, 0:n], src=ev)
        ps = P.bank()
        for k in range(8):
            P.mm(ps[0:16, 0:n], W16[:, k, NCH * 128:NCH * 128 + 16], hT[:, k, 0:n], k == 0, k == 7,
                 [W16, hT], [ps])
        P.copy(gev[:, 0:n], ps[0:16, 0:n], [ps], [gev])
        P.dma(gT[:, t0:t1], gev[:, 0:n], src=gev)
    return P.finish()


def feat_major(v):
    return np.ascontiguousarray(v.reshape(8, 128).T)


def stage_proj(nc, xT_b, w_in_l, norm_g_l, mod_l):
    maps = []
    for j in range(NCORES):
        b, g = divmod(j, 4)
        cols, gates = core_cols(g)
        Wg = np.zeros((D, NCH * 128 + 16), np.float32)
        Wg[:, :NCH * 128] = w_in_l[:, cols]
        Wg[:, NCH * 128:NCH * 128 + NGATE] = w_in_l[:, gates]
        shift, scale = mod_l[b][0:D], mod_l[b][D:2 * D]
        shift_c, scale_c = mod_l[2][0:D], mod_l[2][D:2 * D]
        prm = np.stack([feat_major(norm_g_l), feat_major(scale), feat_major(shift),
                        feat_major(scale_c), feat_major(shift_c)], axis=1)
        maps.append({"xT": xT_b[b], "Wg": Wg, "prm": np.ascontiguousarray(prm)})
    return run(nc, maps)


CPAD = (CTX + 4) + (SEQ + 4)
C_TILES = [(0, 0, CTX)] + [(CTX + 4, i * 512, 512) for i in range(SEQ // 512)]


def build_conv():
    P = Prog()
    pre = P.dram("pre", [6, 128, CPAD], F32, "ExternalInput")
    cw = P.dram("cw", [128, 6, 5], F32, "ExternalInput")
    gin = {nm: P.dram(nm, [r, TT], F32, "ExternalInput") for nm, r in
           (("betaT", 4), ("alphaT", 4), ("iT", 2), ("fT", 2))}
    gprm = P.dram("gprm", [4, 4], F32, "ExternalInput")
    qkv = P.dram("qkv", [6, 128, TT], F32, "ExternalOutput")
    gout = {nm: P.dram(nm, [r, TT], F32, "ExternalOutput") for nm, r in
            (("beta", 4), ("g", 4), ("ig", 2), ("lf", 2))}
    P.alloc_psum_banks(4)
    ones = P.sbuf("ones", [128, 128])
    P.memset(ones[:], 1.0, [ones])
    cws = P.sbuf("cws", [128, 6, 5])
    P.dma(cws[:], cw[:, :, :], dst=cws)
    gp = P.sbuf("gp", [4, 4])
    P.dma(gp[:], gprm[:, :], dst=gp)
    gp2 = P.sbuf("gp2", [4, 4])
    P.act(gp2[:, 0:1], gp[:, 0:1], AF.Exp, [gp], [gp2])
    P.ts(gp2[:, 1:2], gp2[:, 0:1], -1.0, None, ALU.mult, None, [gp2], [gp2])
    P.ts(gp2[0:2, 2:3], gp[0:2, 3:4], -1.0, None, ALU.mult, None, [gp], [gp2])
    ga = P.sbuf("ga", [4, TT])
    gb = P.sbuf("gb", [4, TT])
    gc = P.sbuf("gc", [4, TT])
    gd = P.sbuf("gd", [4, TT])
    P.dma(ga[:], gin["betaT"][:, :], dst=ga)
    P.act(gb[:], ga[:], AF.Sigmoid, [ga], [gb])
    P.dma(gout["beta"][:, :], gb[:], src=gb)
    P.dma(gc[:], gin["alphaT"][:, :], dst=gc)
    P.act(gd[:], gc[:], AF.Exp, [gc, gp], [gd], bias=gp[:, 1:2])
    P.act(gd[:], gd[:], AF.Ln, [gd], [gd], bias=1.0)
    P.ts(gc[:], gd[:], gp2[:, 1:2], None, ALU.mult, None, [gd, gp2], [gc])
    P.dma(gout["g"][:, :], gc[:], src=gc)
    P.dma(ga[0:2, :], gin["iT"][:, :], dst=ga)
    P.ts(gb[0:2, :], ga[0:2, :], gp[0:2, 2:3], None, ALU.add, None, [ga, gp], [gb])
    P.dma(gout["ig"][:, :], gb[0:2, :], src=gb)
    P.dma(gd[0:2, :], gin["fT"][:, :], dst=gd)
    P.act(gc[0:2, :], gd[0:2, :], AF.Exp, [gd, gp2], [gc], bias=gp2[0:2, 2:3], scale=-1.0)
    P.act(gc[0:2, :], gc[0:2, :], AF.Ln, [gc], [gc], bias=1.0)
    P.ts(gd[0:2, :], gc[0:2, :], -1.0, None, ALU.mult, None, [gc], [gd])
    P.dma(gout["lf"][:, :], gd[0:2, :], src=gd)
    xin = [P.sbuf(f"xin{i}", [128, 516]) for i in range(3)]
    acc = [P.sbuf(f"acc{i}", [128, 512]) for i in range(2)]
    ys = [P.sbuf(f"y{i}", [128, 512]) for i in range(2)]
    sqs = [P.sbuf(f"sq{i}", [128, 512]) for i in range(2)]
    rs = [P.sbuf(f"r{i}", [128, 512]) for i in range(2)]
    outs = [P.sbuf(f"o{i}", [128, 512]) for i in range(3)]
    it = 0
    for (off, t0, n) in C_TILES:
        for c in range(6):
            xi = xin[it % 3]
            a = acc[it % 2]
            y = ys[it % 2]
            o = outs[it % 3]
            P.dma(xi[:, 0:n + 4], pre[c, :, off + t0:off + t0 + n + 4], dst=xi)
            P.ts(a[:, 0:n], xi[:, 0:n], cws[:, c, 0:1], None, ALU.mult, None, [xi, cws], [a])
            for s in range(1, 5):
                P.stt(a[:, 0:n], xi[:, s:s + n], cws[:, c, s:s + 1], a[:, 0:n], ALU.mult, ALU.add,
                      [xi, cws, a], [a])
            tok0 = (0 if off == 0 else CTX) + t0
            if c < 4:
                P.act(y[:, 0:n], a[:, 0:n], AF.Silu, [a], [y])
                sq = sqs[it % 2]
                r = rs[it % 2]
                P.tt(sq[:, 0:n], y[:, 0:n], y[:, 0:n], ALU.mult, [y], [sq], eng="pool")
                ps = P.bank()
                P.mm(ps[:, 0:n], ones[:], sq[:, 0:n], True, True, [ones, sq], [ps])
                P.ts(r[:, 0:n], ps[:, 0:n], EPS, None, ALU.add, None, [ps], [r])
                P.act(r[:, 0:n], r[:, 0:n], AF.Sqrt, [r], [r])
                P.op("dve", lambda e, o_=r[:, 0:n], i_=r[:, 0:n]: e.reciprocal(o_, i_), [r], [r])
                sc = (DK_A ** -0.5) if c < 2 else 1.0
                P.stt(o[:, 0:n], y[:, 0:n], sc, r[:, 0:n], ALU.mult, ALU.mult, [y, r], [o])
            else:
                P.act(o[:, 0:n], a[:, 0:n], AF.Silu, [a], [o])
            P.dma(qkv[c, :, tok0:tok0 + n], o[:, 0:n], src=o)
            it += 1
    return P.finish()


def pad_seq(a):
    z = np.zeros(a.shape[:-1] + (2,), a.dtype)
    return np.ascontiguousarray(np.concatenate([z, a[..., :CTX], z, z, a[..., CTX:], z], axis=-1))


def stage_conv(nc, proj_res, conv_w_l, a_log_l, dt_bias_l, i_bias_l, f_bias_l):
    maps = []
    for j in range(NCORES):
        b, g = divmod(j, 4)
        pT = proj_res[j]["pT"]
        gT = proj_res[j]["gT"]
        pre = pad_seq(pT[0:768].reshape(6, 128, TT))
        hA = (2 * g, 2 * g + 1)
        chans = []
        for base in (0, 1024, 2048):
            for h in hA:
                chans.append(np.arange(base + h * 128, base + (h + 1) * 128))
        cw = np.stack([conv_w_l[:, ch].T for ch in chans], axis=1)
        gprm = np.zeros((4, 4), np.float32)
        k = 0
        for d in range(2):
            for h in hA:
                gprm[k, 0] = a_log_l[d, h]
                gprm[k, 1] = dt_bias_l[d, h]
                k += 1
        for d in range(2):
            gprm[d, 2] = i_bias_l[d, g]
            gprm[d, 3] = f_bias_l[d, g]
        maps.append({"pre": pre, "cw": np.ascontiguousarray(cw), "gprm": gprm,
                     "betaT": np.ascontiguousarray(gT[0:4]), "alphaT": np.ascontiguousarray(gT[4:8]),
                     "iT": np.ascontiguousarray(gT[8:10]), "fT": np.ascontiguousarray(gT[10:12])})
    return run(nc, maps)


class View:
    def __init__(self, buf, ap):
        self.buf = buf
        self.ap = ap

    def __getitem__(self, idx):
        return View(self.buf, self.ap[idx])


def V(buf, *idx):
    if not idx:
        return View(buf, buf.t[:])
    return View(buf, buf.t[idx if len(idx) > 1 else idx[0]])


def _u(x):
    return x.ap if isinstance(x, View) else x


def _b(*xs):
    out = []
    for x in xs:
        if isinstance(x, View) and x.buf not in out:
            out.append(x.buf)
    return out


class AProg(Prog):
    def quarters(self, nbanks=8):
        self.alloc_psum_banks(nbanks)
        self.qtiles = []
        for b in self.psum_banks:
            for q in range(4):
                self.qtiles.append(Buf(self, b.t[:, q * 128:(q + 1) * 128], f"{b.name}q{q}", "psum"))
        self.qi = 0

    def q(self):
        t = self.qtiles[self.qi % len(self.qtiles)]
        self.qi += 1
        return V(t)

    def amm(self, out, lhsT, rhs, start=True, stop=True):
        return self.mm(_u(out), _u(lhsT), _u(rhs), start, stop, _b(lhsT, rhs), _b(out))

    def atr(self, out, in_, ident):
        return self.transpose(_u(out), _u(in_), _u(ident), _b(in_, ident), _b(out))

    def aact(self, out, in_, func, bias=None, scale=None):
        return self.act(_u(out), _u(in_), func, _b(in_, bias, scale), _b(out),
                        bias=_u(bias) if bias is not None else None,
                        scale=_u(scale) if scale is not None else None)

    def att(self, out, in0, in1, op, eng="dve"):
        return self.tt(_u(out), _u(in0), _u(in1), op, _b(in0, in1), _b(out), eng=eng)

    def ats(self, out, in0, s1, op0, s2=None, op1=None, eng="dve"):
        return self.ts(_u(out), _u(in0), _u(s1), _u(s2), op0, op1, _b(in0, s1, s2), _b(out), eng=eng)

    def astt(self, out, in0, scalar, in1, op0, op1):
        return self.stt(_u(out), _u(in0), _u(scalar), _u(in1), op0, op1, _b(in0, scalar, in1), _b(out))

    def acopy(self, out, in_, eng="dve"):
        return self.copy(_u(out), _u(in_), _b(in_), _b(out), eng=eng)

    def amemset(self, out, val, eng="dve"):
        return self.memset(_u(out), val, _b(out), eng=eng)

    def aload(self, dst, src_ap, q="sp"):
        return self.dma(_u(dst), src_ap, dst=dst.buf, q=q)

    def astore(self, dst_ap, src, q="sp"):
        return self.dma(dst_ap, _u(src), src=src.buf, q=q)


def run_round_robin(gens):
    gens = list(gens)
    while gens:
        nxt = []
        for g in gens:
            try:
                next(g)
                nxt.append(g)
            except StopIteration:
                pass
        gens = nxt


BIG = 30000.0
NSC = TT // 128


def make_masks():
    i = np.arange(128)
    same = (i[:, None] // 64) == (i[None, :] // 64)
    U = ((i[:, None] <= i[None, :]) & same).astype(np.float32)
    BD = same.astype(np.float32)
    PL = np.where((i[None, :] < i[:, None]) & same, 0.0, BIG).astype(np.float32)
    NU = np.where((i[:, None] <= i[None, :]) & same, 0.0, -BIG).astype(np.float32)
    ID = np.eye(128, dtype=np.float32)
    NL = np.where((i[None, :] <= i[:, None]) & same, 0.0, -BIG).astype(np.float32)
    ON = np.ones((128, 128), np.float32)
    return np.ascontiguousarray(np.stack([U, BD, PL, NU, ID, NL, ON, -U], axis=1))


M_U, M_BD, M_PL, M_NU, M_ID, M_NL, M_ONES, M_NEGU = range(8)
NMSK = 8


def delta_problem(P, pi, msk, dr, S, o_out, banks, nsc=NSC):
    f = lambda nm, shape=(128, 128): P.sbuf(f"{nm}_{pi}", list(shape))
    qA = lambda: V(banks[0])[:, 0:128]
    qB = lambda: V(banks[1])[:, 0:128]
    qTs = [f(f"qT{i}") for i in range(2)]
    kTs = [f(f"kT{i}") for i in range(2)]
    kts = [f(f"ktm{i}") for i in range(2)]
    vts = [f(f"vtm{i}") for i in range(2)]
    bcol = f("bcol", (128, NSC))
    nbcol = f("nbcol", (128, NSC))
    gcol = f("gcol", (128, NSC, 2))
    cols = [f(f"cols{i}", (128, 8)) for i in range(2)]
    gbc = f("gbc"); RL = f("RL"); RU = f("RU"); dL = f("dL"); dT = f("dT"); eg = [f("eg0"), f("eg1")]
    BT = f("BT"); Bm = f("Bm")
    Pa = [f("Pa0"), f("Pa1")]; PaT = [f("PaT0"), f("PaT1")]
    X = [f("X0"), f("X1")]
    kbg = f("kbg"); vb = f("vb"); kdec = [f("kdec0"), f("kdec1")]
    wT = f("wT"); u = f("u"); qkT = [f("qkT0"), f("qkT1")]; qdT = [f("qdT0"), f("qdT1")]
    vnew = f("vnew")
    osb = [f("osb0"), f("osb1")]
    P.amemset(V(vnew), 0.0)
    P.aload(V(bcol), dr["bcol"][:, pi, :])
    P.aload(V(gcol), dr["gcol"][:, pi, :, :])
    P.ats(V(nbcol), V(bcol), -1.0, ALU.mult)
    U = V(msk, slice(None), M_U, slice(None))
    BDm = V(msk, slice(None), M_BD, slice(None))
    PLm = V(msk, slice(None), M_PL, slice(None))
    NUm = V(msk, slice(None), M_NU, slice(None))
    ID = V(msk, slice(None), M_ID, slice(None))
    ONESm = V(msk, slice(None), M_ONES, slice(None))
    yield
    for sc in range(nsc):
        t0 = sc * 128
        qT = V(qTs[sc % 2]); kT = V(kTs[sc % 2]); ktm = V(kts[sc % 2]); vtm = V(vts[sc % 2])
        P.aload(qT, dr["qT"][pi, :, t0:t0 + 128])
        P.aload(kT, dr["kT"][pi, :, t0:t0 + 128])
        P.aload(ktm, dr["ktm"][pi, t0:t0 + 128, :])
        P.aload(vtm, dr["vtm"][pi, t0:t0 + 128, :])
        cl = V(cols[sc % 2])
        gps = qA()
        P.amm(gps[:, 0:2], U, V(gcol)[:, sc, :])
        P.amm(gps[:, 2:4], BDm, V(gcol)[:, sc, :])
        P.ats(V(gbc), ONESm, V(gcol)[:, sc, 0:1], ALU.mult, eng="pool")
        yield
        grow = qB()
        P.amm(grow, V(gbc), U)
        P.acopy(cl[:, 0:1], gps[:, 0:1])
        P.ats(cl[:, 1:2], gps[:, 0:1], -1.0, ALU.mult)
        yield
        P.aact(cl[:, 2:3], gps[:, 0:1], AF.Exp)
        P.aact(cl[:, 3:4], gps[:, 2:3], AF.Exp, bias=cl[:, 1:2])
        P.att(V(RL), grow, PLm, ALU.add)
        P.att(V(RU), grow, NUm, ALU.add)
        egr = V(eg[sc % 2])
        P.aact(egr, grow, AF.Exp)
        yield
        P.aact(V(dL), V(RL), AF.Exp, bias=cl[:, 0:1], scale=-1.0)
        P.aact(V(dT), V(RU), AF.Exp, bias=cl[:, 1:2])
        P.att(cl[:, 4:5], cl[:, 2:3], V(bcol)[:, sc:sc + 1], ALU.mult)
        kk = qA()
        P.amm(kk, kT, kT)
        qk = qB()
        P.amm(qk, kT, qT)
        yield
        P.astt(V(BT), kk, V(nbcol)[:, sc:sc + 1], V(dL), ALU.mult, ALU.mult)
        P.ats(V(kbg), ktm, cl[:, 4:5], ALU.mult, eng="pool")
        P.ats(V(vb), vtm, V(bcol)[:, sc:sc + 1], ALU.mult, eng="pool")
        kd = V(kdec[sc % 2])
        P.ats(kd, ktm, cl[:, 3:4], ALU.mult, eng="pool")
        qkTs = V(qkT[sc % 2])
        P.att(qkTs, qk, V(dT), ALU.mult)
        qd = V(qdT[sc % 2])
        P.att(qd, qT, egr, ALU.mult, eng="pool")
        yield
        bps = qA()
        P.atr(bps, V(BT), ID)
        yield
        P.acopy(V(Bm), bps, eng="act")
        P.att(V(X[0]), bps, ID, ALU.add)
        yield
        cur, curT = V(Bm), V(BT)
        xi = 0
        for lev in range(5):
            last = lev == 4
            p2T = qA()
            P.amm(p2T, cur, curT)
            if not last:
                p2 = qB()
                P.amm(p2, curT, cur)
            yield
            nT = V(PaT[lev % 2])
            P.acopy(nT, p2T, eng="act")
            if not last:
                n_ = V(Pa[lev % 2])
                P.acopy(n_, p2)
            yield
            xp = qA()
            P.amm(xp, nT, V(X[xi]))
            yield
            P.att(V(X[1 - xi]), xp, V(X[xi]), ALU.add)
            xi = 1 - xi
            if not last:
                cur, curT = n_, nT
            yield
        TinvT = V(X[xi])
        wps = qA()
        P.amm(wps, V(kbg), TinvT)
        ups = qB()
        P.amm(ups, TinvT, V(vb))
        yield
        P.acopy(V(wT), wps, eng="act")
        P.acopy(V(u), ups)
        yield
        ob = V(osb[sc % 2])
        for c in range(2):
            r = slice(c * 64, c * 64 + 64)
            vps = qA()
            P.amm(vps[r, :], V(wT)[:, r], V(S))
            yield
            P.att(V(vnew)[r, :], V(u)[r, :], vps[r, :], ALU.subtract)
            yield
            ops_ = qB()
            P.amm(ops_[r, :], qd[:, r], V(S), True, False)
            P.amm(ops_[r, :], qkTs[:, r], V(vnew), False, True)
            sps = qA()
            P.amm(sps, kd[r, :], V(vnew)[r, :])
            yield
            P.astt(V(S), V(S), egr[:, c * 64 + 63:c * 64 + 64], sps, ALU.mult, ALU.add)
            P.acopy(ob[r, :], ops_[r, :], eng="act")
            yield
        P.astore(o_out[pi, t0:t0 + 128, :], ob)
        yield


def build_delta(nsc=NSC, nprob=4):
    P = AProg()
    dr = {
        "qT": P.dram("qT", [4, 128, TT], F32, "ExternalInput"),
        "kT": P.dram("kT", [4, 128, TT], F32, "ExternalInput"),
        "ktm": P.dram("ktm", [4, TT, 128], F32, "ExternalInput"),
        "vtm": P.dram("vtm", [4, TT, 128], F32, "ExternalInput"),
        "bcol": P.dram("bcol", [128, 4, NSC], F32, "ExternalInput"),
        "gcol": P.dram("gcol", [128, 4, NSC, 2], F32, "ExternalInput"),
    }
    mskd = P.dram("msk", [128, NMSK, 128], F32, "ExternalInput")
    o_out = P.dram("o", [4, TT, 128], F32, "ExternalOutput")
    P.alloc_psum_banks(8)
    msk = P.sbuf("msk_sb", [128, NMSK, 128])
    P.aload(V(msk), mskd[:, :, :])
    gens = []
    for pi in range(nprob):
        S = P.sbuf(f"S_{pi}", [128, 128])
        P.amemset(V(S), 0.0)
        gens.append(delta_problem(P, pi, msk, dr, S, o_out,
                                  (P.psum_banks[2 * pi], P.psum_banks[2 * pi + 1]), nsc))
    run_round_robin(gens)
    return P.finish()


def prob_order(d):
    ci = np.arange(CTX)
    li = CTX + np.arange(SEQ)
    if d == 1:
        ci = ci[::-1]
        li = li[::-1]
    return np.concatenate([ci, li])


def col_layout(v):
    return np.ascontiguousarray(v.reshape(NSC, 128).T)


def stage_delta(nc, conv_res):
    msk = make_masks()
    maps = []
    for j in range(NCORES):
        r = conv_res[j]
        qT, kT, ktm, vtm = [], [], [], []
        bcol = np.zeros((128, 4, NSC), np.float32)
        gcol = np.zeros((128, 4, NSC, 2), np.float32)
        for pi in range(4):
            d, hh = divmod(pi, 2)
            idx = prob_order(d)
            qT.append(r["qkv"][0 + hh][:, idx])
            kk = r["qkv"][2 + hh][:, idx]
            kT.append(kk)
            ktm.append(kk.T)
            vtm.append(r["qkv"][4 + hh][:, idx].T)
            bcol[:, pi, :] = col_layout(r["beta"][pi][idx])
            gc = col_layout(r["g"][pi][idx])
            gcol[:, pi, :, 0] = gc
            gcol[:, pi, :, 1] = gc
        maps.append({"qT": np.ascontiguousarray(np.stack(qT)), "kT": np.ascontiguousarray(np.stack(kT)),
                     "ktm": np.ascontiguousarray(np.stack(ktm)), "vtm": np.ascontiguousarray(np.stack(vtm)),
                     "bcol": bcol, "gcol": gcol, "msk": msk})
    return run(nc, maps)


KSC = DQK_B ** -0.5
NEGBIG = -1.0e30


def mlstm_problem(P, pi, msk, dr, h_out, banks, nsc=NSC):
    f = lambda nm, shape=(128, 128): P.sbuf(f"{nm}_m{pi}", list(shape))
    qA = lambda: V(banks[0])
    qB = lambda: V(banks[1])
    qC = lambda: V(banks[2])
    qD = lambda: V(banks[3])
    qTs = [f(f"qT{i}") for i in range(2)]
    kTs = [f(f"kT{i}") for i in range(2)]
    kts = [f(f"ktm{i}") for i in range(2)]
    vas = [f(f"vaug{i}", (128, 258)) for i in range(2)]
    igc = f("igc", (128, NSC))
    lfc = f("lfc", (128, NSC, 2))
    cols = [f(f"cols{i}", (128, 8)) for i in range(2)]
    lfb = f("lfb"); igb = f("igb"); brs = [f("brs0"), f("brs1")]; Rm = f("Rm")
    pmx = [f("pmx0"), f("pmx1")]; dmr = f("dmr")
    mtr = f("mtr", (128, 64)); Er = f("Er", (128, 64)); inr = f("inr", (128, 64)); tmpw = f("tmpw", (128, 64))
    W1 = f("W1", (128, 64))
    qd = f("qd")
    wts = [f("wts0"), f("wts1")]
    kw = f("kw")
    Cn = f("Cn", (128, 258))
    ms = [f("m0", (128, 1)), f("m1", (128, 1))]
    sm = [f("sm0", (128, 8)), f("sm1", (128, 8))]
    hsb = [f("hsb0", (128, 256)), f("hsb1", (128, 256))]
    P.amemset(V(Cn), 0.0)
    P.amemset(V(ms[0]), 0.0)
    for w_ in wts:
        P.amemset(V(w_), 0.0, eng="pool")
    for va in vas:
        P.amemset(V(va)[:, 256:258], 1.0, eng="pool")
    P.aload(V(igc), dr["igc"][:, pi, :])
    P.aload(V(lfc), dr["lfc"][:, pi, :, :])
    U = V(msk, slice(None), M_U, slice(None))
    NUm = V(msk, slice(None), M_NU, slice(None))
    ID = V(msk, slice(None), M_ID, slice(None))
    NLm = V(msk, slice(None), M_NL, slice(None))
    ONESm = V(msk, slice(None), M_ONES, slice(None))
    NEGU = V(msk, slice(None), M_NEGU, slice(None))
    mi = 0
    yield
    for sc in range(nsc):
        t0 = sc * 128
        qT = V(qTs[sc % 2]); kT = V(kTs[sc % 2]); ktm = V(kts[sc % 2]); va = V(vas[sc % 2])
        P.aload(qT, dr["qT"][pi, :, t0:t0 + 128])
        P.aload(kT, dr["kT"][pi, :, t0:t0 + 128])
        P.aload(ktm, dr["ktm"][pi, t0:t0 + 128, :])
        P.aload(va[:, 0:256], dr["vtm"][pi, t0:t0 + 128, :])
        cl = V(cols[sc % 2])
        bps = qA()
        P.amm(bps[:, 0:2], U, V(lfc)[:, sc, :])
        P.ats(V(lfb), ONESm, V(lfc)[:, sc, 0:1], ALU.mult, eng="pool")
        P.ats(V(igb), ONESm, V(igc)[:, sc:sc + 1], ALU.mult, eng="pool")
        yield
        brow = qB()
        P.amm(brow[:, 0:128], V(lfb), U)
        rrow = qC()
        P.amm(rrow[:, 0:128], V(igb), ID, True, False)
        P.amm(rrow[:, 0:128], V(lfb), NEGU, False, True)
        qk = qD()
        P.amm(qk[:, 0:128], kT, qT)
        P.acopy(cl[:, 0:1], bps[:, 0:1])
        yield
        br = V(brs[sc % 2])
        P.acopy(br, brow[:, 0:128], eng="act")
        P.att(cl[:, 1:2], V(igc)[:, sc:sc + 1], cl[:, 0:1], ALU.subtract)
        P.att(V(Rm), rrow[:, 0:128], NLm, ALU.add)
        pm = V(pmx[sc % 2])
        for c in range(2):
            r = slice(c * 64, c * 64 + 64)
            P.op("dve", lambda e, o=_u(pm[:, r]), d0=_u(ONESm[:, r]), d1=_u(rrow[:, r]):
                 e.tensor_tensor_scan(o, d0, d1, NEGBIG, ALU.mult, ALU.max),
                 _b(ONESm, rrow), _b(pm))
        yield
        P.op("dve", lambda e, o=_u(cl[:, 2:3]), i_=_u(V(Rm)): e.tensor_reduce(o, i_, AX.X, ALU.max),
             _b(V(Rm)), _b(cl))
        P.att(V(dmr), br, pm, ALU.add)
        yield
        P.att(cl[:, 3:4], cl[:, 0:1], cl[:, 2:3], ALU.add)
        yield
        hb_ = V(hsb[sc % 2])
        for c in range(2):
            r = slice(c * 64, c * 64 + 64)
            last = c * 64 + 63
            m_old = V(ms[mi]); m_new = V(ms[1 - mi])
            s_ = V(sm[c])
            P.astt(V(mtr), br[:, r], m_old[:, 0:1], V(dmr)[:, r], ALU.add, ALU.max)
            P.astt(s_[:, 0:1], cl[:, 0:1], m_old[:, 0:1], cl[:, 3:4], ALU.add, ALU.max)
            P.att(s_[:, 1:2], m_old[:, 0:1], pm[:, last:last + 1], ALU.max)
            yield
            P.att(V(Er), br[:, r], V(mtr), ALU.subtract)
            P.aact(s_[:, 2:3], s_[:, 0:1], AF.Exp, scale=-1.0)
            P.att(m_new[:, 0:1], s_[:, 1:2], br[:, last:last + 1], ALU.add)
            yield
            P.aact(V(inr), V(Er), AF.Exp, bias=m_old[:, 0:1])
            P.att(V(tmpw)[r, :], V(Er)[r, :], NUm[r, r], ALU.add, eng="pool")
            P.att(s_[:, 3:4], br[:, last:last + 1], m_new[:, 0:1], ALU.subtract)
            yield
            P.att(V(qd)[:, r], qT[:, r], V(inr), ALU.mult, eng="pool")
            P.aact(V(W1)[r, :], V(tmpw)[r, :], AF.Exp, bias=cl[r, 1:2])
            P.aact(s_[:, 4:5], cl[:, 1:2], AF.Exp, bias=s_[:, 3:4])
            P.aact(s_[:, 5:6], s_[:, 3:4], AF.Exp, bias=m_old[:, 0:1])
            yield
            wt = V(wts[c])
            P.astt(wt[r, r], qk[r, r], KSC, V(W1)[r, :], ALU.mult, ALU.mult)
            P.ats(V(kw)[r, :], ktm[r, :], s_[r, 4:5], ALU.mult, KSC, ALU.mult)
            yield
            nps = qA()
            P.amm(nps[r, 0:258], V(qd)[:, r], V(Cn), True, False)
            P.amm(nps[r, 0:258], wt[:, r], va, False, True)
            cps = qB()
            P.amm(cps[:, 0:258], V(kw)[r, :], va[r, :])
            yield
            P.ats(s_[r, 7:8], nps[r, 256:257], -1.0, ALU.mult)
            P.astt(s_[r, 6:7], nps[r, 256:257], s_[r, 2:3], s_[r, 7:8], ALU.max, ALU.max)
            P.astt(V(Cn), V(Cn), s_[:, 5:6], cps[:, 0:258], ALU.mult, ALU.add)
            yield
            P.op("dve", lambda e, o=_u(s_[r, 7:8]), i_=_u(s_[r, 6:7]): e.reciprocal(o, i_), _b(s_), _b(s_))
            yield
            P.ats(hb_[r, :], nps[r, 0:256], s_[r, 7:8], ALU.mult)
            mi = 1 - mi
            yield
        P.astore(h_out[pi, t0:t0 + 128, :], hb_)
        yield


def build_mlstm(nsc=NSC, nprob=2):
    P = AProg()
    dr = {
        "qT": P.dram("qT", [2, 128, TT], F32, "ExternalInput"),
        "kT": P.dram("kT", [2, 128, TT], F32, "ExternalInput"),
        "ktm": P.dram("ktm", [2, TT, 128], F32, "ExternalInput"),
        "vtm": P.dram("vtm", [2, TT, 256], F32, "ExternalInput"),
        "igc": P.dram("igc", [128, 2, NSC], F32, "ExternalInput"),
        "lfc": P.dram("lfc", [128, 2, NSC, 2], F32, "ExternalInput"),
    }
    mskd = P.dram("msk", [128, NMSK, 128], F32, "ExternalInput")
    h_out = P.dram("h", [2, TT, 256], F32, "ExternalOutput")
    P.alloc_psum_banks(8)
    msk = P.sbuf("msk_sb", [128, NMSK, 128])
    P.aload(V(msk), mskd[:, :, :])
    gens = []
    for pi in range(nprob):
        gens.append(mlstm_problem(P, pi, msk, dr, h_out, P.psum_banks[4 * pi:4 * pi + 4], nsc))
    run_round_robin(gens)
    return P.finish()


def colmajor_perm():
    rows = SEQ // GRID_W
    return np.arange(SEQ).reshape(rows, GRID_W).T.reshape(-1)


def mlstm_order(d):
    ci = np.arange(CTX)
    li = CTX + colmajor_perm()
    if d == 1:
        ci = ci[::-1]
        li = li[::-1]
    return np.concatenate([ci, li])


def stage_mlstm(nc, proj_res, conv_res):
    msk = make_masks()
    maps = []
    for j in range(NCORES):
        pT = proj_res[j]["pT"]
        r = conv_res[j]
        q = pT[CH_BQ * 128:(CH_BQ + 1) * 128]
        k = pT[CH_BK * 128:(CH_BK + 1) * 128]
        v = pT[CH_BV[0] * 128:(CH_BV[1] + 1) * 128]
        qT, kT, ktm, vtm = [], [], [], []
        igc = np.zeros((128, 2, NSC), np.float32)
        lfc = np.zeros((128, 2, NSC, 2), np.float32)
        for d in range(2):
            idx = mlstm_order(d)
            qT.append(q[:, idx])
            kk = k[:, idx]
            kT.append(kk)
            ktm.append(kk.T)
            vtm.append(v[:, idx].T)
            igc[:, d, :] = col_layout(r["ig"][d][idx])
            lc = col_layout(r["lf"][d][idx])
            lfc[:, d, :, 0] = lc
            lfc[:, d, :, 1] = lc
        maps.append({"qT": np.ascontiguousarray(np.stack(qT)), "kT": np.ascontiguousarray(np.stack(kT)),
                     "ktm": np.ascontiguousarray(np.stack(ktm)), "vtm": np.ascontiguousarray(np.stack(vtm)),
                     "igc": igc, "lfc": lfc, "msk": msk})
    return run(nc, maps)


O_ARR = ("oaf", "oab", "hbf", "hbb", "az", "bo", "bz", "ga", "gb", "x")


def build_merge(has_ctx, final, ntiles=None):
    P = AProg()
    NT = (64 if has_ctx else 0) + 2048
    tiles = ([(0, 64, 1)] if has_ctx else []) + [((64 if has_ctx else 0) + i * 512, 512, 0) for i in range(4)]
    if ntiles is not None:
        tiles = tiles[:ntiles]
    dr = {nm: P.dram(nm, [D, NT], F32, "ExternalInput").rearrange("(k p) t -> p k t", p=128) for nm in O_ARR}
    Wd = {nm: P.dram(nm, [D, D], F32, "ExternalInput").rearrange("(k p) c -> p k c", p=128)
          for nm in ("w_pa", "w_pb", "w_out")}
    prm = P.dram("prm", [128, 4, 8], F32, "ExternalInput")
    outT = P.dram("outT", [D, NT], F32, "ExternalOutput").rearrange("(k p) t -> p k t", p=128)
    P.alloc_psum_banks(8)
    ones = P.sbuf("ones", [128, 128])
    P.amemset(V(ones), 1.0)
    prms = P.sbuf("prms", [128, 4, 8])
    P.aload(V(prms), prm[:, :, :])
    W16 = {}
    stg = [P.sbuf(f"wst{i}", [128, 8, 256]) for i in range(2)]
    si = 0
    for nm in ("w_pa", "w_pb", "w_out"):
        W16[nm] = P.sbuf(f"{nm}16", [128, 8, D], BF16)
        for half in range(4):
            s_ = V(stg[si % 2])
            P.aload(s_, Wd[nm][:, :, half * 256:(half + 1) * 256])
            P.acopy(V(W16[nm])[:, :, half * 256:(half + 1) * 256], s_, eng=("dve" if si % 2 == 0 else "pool"))
            si += 1
    yaT = P.sbuf("yaT", [128, 8, 512], BF16)
    ybT = P.sbuf("ybT", [128, 8, 512], BF16)
    yT = P.sbuf("yT", [128, 8, 512], BF16)
    xn = P.sbuf("xn", [128, 8, 512])
    NB = 2
    bufs = {nm: [P.sbuf(f"in_{nm}{i}", [128, 512]) for i in range(NB)] for nm in O_ARR}
    cnt = {nm: 0 for nm in O_ARR}

    def load(nm, k, t0, n):
        b = V(bufs[nm][cnt[nm] % NB])[:, 0:n]
        cnt[nm] += 1
        P.aload(b, dr[nm][:, k, t0:t0 + n])
        return b

    tmpn = [0]

    def tmp(tag, dtype=F32):
        key = f"t_{tag}"
        if key not in bufs:
            bufs[key] = [P.sbuf(f"{key}{i}", [128, 512], dtype) for i in range(2)]
            cnt[key] = 0
        b = V(bufs[key][cnt[key] % 2])
        cnt[key] += 1
        return b

    def rstd_from(ss_ps, n, dim):
        r = tmp("rstd")[:, 0:n]
        P.ats(r, ss_ps, 1.0 / dim, ALU.mult, EPS, ALU.add)
        P.aact(r, r, AF.Sqrt)
        P.op("dve", lambda e, o=_u(r), i_=_u(r): e.reciprocal(o, i_), _b(r), _b(r))
        return r

    for (t0, n, is_ctx) in tiles:
        for k in range(8):
            f_ = load("oaf", k, t0, n)
            b_ = load("oab", k, t0, n)
            oa = tmp("oa")[:, 0:n]
            P.att(oa, f_, b_, ALU.add)
            sq = tmp("sq")[:, 0:n]
            P.aact(sq, oa, AF.Square)
            ss = V(P.bank())[:, 0:n]
            P.amm(ss, V(ones), sq)
            r = rstd_from(ss, n, 128)
            az = load("az", k, t0, n)
            sz = tmp("sz")[:, 0:n]
            P.aact(sz, az, AF.Silu)
            t1 = tmp("t1")[:, 0:n]
            P.att(t1, oa, r, ALU.mult)
            P.astt(V(yaT)[:, k, 0:n], sz, V(prms)[:, 3, 0:1], t1, ALU.mult, ALU.mult)
        for hd in range(4):
            hbs = []
            ss = V(P.bank())[:, 0:n]
            for c in range(2):
                k = hd * 2 + c
                f_ = load("hbf", k, t0, n)
                b_ = load("hbb", k, t0, n)
                hb = tmp(f"hb{c}")[:, 0:n]
                P.att(hb, f_, b_, ALU.add)
                sq = tmp("sq")[:, 0:n]
                P.aact(sq, hb, AF.Square)
                P.amm(ss, V(ones), sq, c == 0, c == 1)
                hbs.append(hb)
            r = rstd_from(ss, n, 256)
            for c in range(2):
                k = hd * 2 + c
                bo = load("bo", k, t0, n)
                bz = load("bz", k, t0, n)
                so = tmp("so")[:, 0:n]
                P.aact(so, bo, AF.Sigmoid)
                sz = tmp("sz")[:, 0:n]
                P.aact(sz, bz, AF.Silu)
                t4 = tmp("t4")[:, 0:n]
                P.att(t4, so, sz, ALU.mult, eng="pool")
                t1 = tmp("t1")[:, 0:n]
                P.att(t1, hbs[c], r, ALU.mult)
                P.astt(V(ybT)[:, k, 0:n], t4, V(prms)[:, 3, 1 + c:2 + c], t1, ALU.mult, ALU.mult)
        for m in range(8):
            za = V(P.bank())[:, 0:n]
            for k in range(8):
                P.amm(za, V(W16["w_pa"])[:, k, m * 128:(m + 1) * 128], V(yaT)[:, k, 0:n], k == 0, k == 7)
            zb = V(P.bank())[:, 0:n]
            for k in range(8):
                P.amm(zb, V(W16["w_pb"])[:, k, m * 128:(m + 1) * 128], V(ybT)[:, k, 0:n], k == 0, k == 7)
            ga = load("ga", m, t0, n)
            gb = load("gb", m, t0, n)
            sga = tmp("sga")[:, 0:n]
            P.aact(sga, ga, AF.Sigmoid)
            sgb = tmp("sgb")[:, 0:n]
            P.aact(sgb, gb, AF.Sigmoid)
            ta = tmp("ta")[:, 0:n]
            P.att(ta, za, sga, ALU.mult)
            tb = tmp("tb")[:, 0:n]
            P.att(tb, zb, sgb, ALU.mult)
            P.att(V(yT)[:, m, 0:n], ta, tb, ALU.add)
        for m in range(8):
            ops_ = V(P.bank())[:, 0:n]
            for k in range(8):
                P.amm(ops_, V(W16["w_out"])[:, k, m * 128:(m + 1) * 128], V(yT)[:, k, 0:n], k == 0, k == 7)
            xk = load("x", m, t0, n)
            P.astt(V(xn)[:, m, 0:n], ops_, V(prms)[:, 1 if is_ctx else 0, m:m + 1], xk, ALU.mult, ALU.add)
        if final:
            sqf = V(yT)
            ss = V(P.bank())[:, 0:n]
            for k in range(8):
                sq = tmp("sq")[:, 0:n]
                P.aact(sq, V(xn)[:, k, 0:n], AF.Square)
                P.amm(ss, V(ones), sq, k == 0, k == 7)
            r = rstd_from(ss, n, D)
            for k in range(8):
                t1 = tmp("t1")[:, 0:n]
                P.att(t1, V(xn)[:, k, 0:n], r, ALU.mult)
                o_ = tmp("fo")[:, 0:n]
                P.ats(o_, t1, V(prms)[:, 2, k:k + 1], ALU.mult)
                P.astore(outT[:, k, t0:t0 + n], o_)
        else:
            for k in range(8):
                P.astore(outT[:, k, t0:t0 + n], V(xn)[:, k, 0:n])
    return P.finish()


def core_tokens(qtr, has_ctx):
    lat = CTX + np.arange(qtr * 2048, (qtr + 1) * 2048)
    if has_ctx:
        return np.concatenate([np.arange(qtr * 64, (qtr + 1) * 64), lat])
    return lat


def stage_merge(nc, has_ctx, proj_res, delta_res, mlstm_res, xT_b, mod_l, inp, l, final):
    full = []
    for b in range(BATCH):
        A = {nm: np.empty((D, TT), np.float32) for nm in O_ARR if nm != "x"}
        for g in range(4):
            j = b * 4 + g
            pT = proj_res[j]["pT"]
            for hh in range(2):
                h = 2 * g + hh
                A["az"][h * 128:(h + 1) * 128] = pT[CH_AZ[hh] * 128:(CH_AZ[hh] + 1) * 128]
                for d, nm in ((0, "oaf"), (1, "oab")):
                    o = delta_res[j]["o"][d * 2 + hh]
                    nat = np.empty_like(o)
                    nat[prob_order(d)] = o
                    A[nm][h * 128:(h + 1) * 128] = nat.T
            for nm, ch in (("bo", CH_BO), ("bz", CH_BZ), ("ga", CH_GA), ("gb", CH_GB)):
                A[nm][g * 256:(g + 1) * 256] = pT[ch[0] * 128:(ch[1] + 1) * 128]
            for d, nm in ((0, "hbf"), (1, "hbb")):
                hv = mlstm_res[j]["h"][d]
                nat = np.empty_like(hv)
                nat[mlstm_order(d)] = hv
                A[nm][g * 256:(g + 1) * 256] = nat.T
        A["x"] = xT_b[b]
        full.append(A)
    maps = []
    for j in range(NCORES):
        b, qtr = divmod(j, 4)
        tok = core_tokens(qtr, has_ctx)
        mp = {nm: np.ascontiguousarray(full[b][nm][:, tok]) for nm in O_ARR}
        prm = np.zeros((128, 4, 8), np.float32)
        prm[:, 0, :] = feat_major(mod_l[b][2 * D:3 * D])
        prm[:, 1, :] = feat_major(mod_l[2][2 * D:3 * D])
        prm[:, 2, :] = feat_major(inp["final_g"])
        prm[:, 3, 0] = inp["norm_a_g"][l]
        prm[:, 3, 1] = inp["norm_b_g"][l][0:128]
        prm[:, 3, 2] = inp["norm_b_g"][l][128:256]
        mp["prm"] = prm
        for nm in ("w_pa", "w_pb", "w_out"):
            mp[nm] = np.ascontiguousarray(inp[nm][l])
        maps.append(mp)
    res = run(nc, maps)
    out = [np.array(xT_b[b]) for b in range(BATCH)]
    for j in range(NCORES):
        b, qtr = divmod(j, 4)
        tok = core_tokens(qtr, has_ctx)
        out[b][:, tok] = res[j]["outT"]
    return out


def kernel(x, c, ctx, c_ctx, ada_w, ada_b, norm_g, w_in, conv_w, a_log, dt_bias, norm_a_g,
           i_bias, f_bias, norm_b_g, w_pa, w_pb, w_out, final_g):
    inp = {k: np.asarray(v, dtype=np.float32) for k, v in dict(
        x=x, c=c, ctx=ctx, c_ctx=c_ctx, ada_w=ada_w, ada_b=ada_b, norm_g=norm_g, w_in=w_in,
        conv_w=conv_w, a_log=a_log, dt_bias=dt_bias, norm_a_g=norm_a_g, i_bias=i_bias,
        f_bias=f_bias, norm_b_g=norm_b_g, w_pa=w_pa, w_pb=w_pb, w_out=w_out, final_g=final_g).items()}
    mod = stage_mod(inp)
    xT_b = [np.ascontiguousarray(np.concatenate([inp["ctx"][b], inp["x"][b]], axis=0).T)
            for b in range(BATCH)]
    nc_proj = build_proj()
    nc_conv = build_conv()
    nc_delta = build_delta()
    nc_mlstm = build_mlstm()
    for l in range(2):
        proj_res = stage_proj(nc_proj, xT_b, inp["w_in"][l], inp["norm_g"][l], mod[l])
        conv_res = stage_conv(nc_conv, proj_res, inp["conv_w"][l], inp["a_log"][l], inp["dt_bias"][l],
                              inp["i_bias"][l], inp["f_bias"][l])
        delta_res = stage_delta(nc_delta, conv_res)
        mlstm_res = stage_mlstm(nc_mlstm, proj_res, conv_res)
        last = l == 1
        nc_merge = build_merge(not last, last)
        xT_b = stage_merge(nc_merge, not last, proj_res, delta_res, mlstm_res, xT_b, mod[l], inp, l, last)
        del proj_res, conv_res, delta_res, mlstm_res
    out = np.stack([np.ascontiguousarray(xT_b[b][:, CTX:].T) for b in range(BATCH)])
    return out.astype(np.float32)
```

```python
import numpy as np
from contextlib import ExitStack
import concourse.bass as bass
import concourse.mybir as mybir
from concourse.bass_utils import run_bass_kernel_spmd

F32 = mybir.dt.float32
BF16 = mybir.dt.bfloat16
F32R = mybir.dt.float32r
AF = mybir.ActivationFunctionType
ALU = mybir.AluOpType
AX = mybir.AxisListType

EP = 20000


class Buf:
    def __init__(self, prog, t, name, space):
        self.p = prog
        self.t = t
        self.name = name
        self.space = space
        self.w = None
        self.r = []
        self.sem_in = None
        self.n_in = 0
        self.sem_out = None
        self.n_out = 0
        self.acc = {}

    def __getitem__(self, idx):
        return self.t[idx]


class Prog:
    ENGS = ("pe", "act", "dve", "pool", "sp")

    def __init__(self):
        self.nc = bass.Bass("TRN2", target_bir_lowering=False)
        self.st = ExitStack()
        self.ops = {e: [] for e in self.ENGS}
        self.cnt = {e: 0 for e in self.ENGS}
        self.sems = {e: [] for e in self.ENGS}
        self.known = {e: {} for e in self.ENGS}
        self.nsem = 0
        self.out_deps = []
        self.psum_banks = []
        self.psum_i = 0

    def sem(self, name):
        self.nsem += 1
        return self.st.enter_context(self.nc.semaphore(name))

    def dram(self, name, shape, dtype, kind):
        return self.nc.dram_tensor(name, list(shape), dtype, kind=kind).ap()

    def dram_buf(self, name, shape, dtype, addr_space=None):
        if addr_space is not None:
            t = self.nc.dram_tensor(name, list(shape), dtype, kind="Internal", addr_space=addr_space).ap()
        else:
            t = self.nc.dram_tensor(name, list(shape), dtype, kind="Internal").ap()
        return Buf(self, t, name, "dram")

    def sbuf(self, name, shape, dtype=F32):
        t = self.st.enter_context(self.nc.sbuf_tensor(name, list(shape), dtype))
        return Buf(self, t, name, "sbuf")

    def psum(self, name, shape, dtype=F32):
        t = self.st.enter_context(self.nc.psum_tensor(name, list(shape), dtype))
        return Buf(self, t, name, "psum")

    def alloc_psum_banks(self, n=8):
        self.psum_banks = [self.psum(f"bank{i}", [128, 512], F32) for i in range(n)]

    def bank(self):
        b = self.psum_banks[self.psum_i % len(self.psum_banks)]
        self.psum_i += 1
        return b

    def _eng_sem(self, e, idx):
        ep = (idx - 1) // EP
        while len(self.sems[e]) <= ep:
            self.sems[e].append(self.sem(f"s_{e}_{len(self.sems[e])}"))
        return self.sems[e][ep], (idx - 1) % EP + 1

    def _collect(self, eng, reads, writes):
        deps = []
        for b in reads:
            if b.w is not None:
                deps.append(b.w)
        for b in writes:
            if b.w is not None:
                deps.append(b.w)
            deps.extend(b.r)
        for b in list(reads) + list(writes):
            if b.space == "psum":
                for e2, i2 in b.acc.items():
                    if e2 != eng:
                        deps.append(("eng", e2, i2, "p"))
        waits = {}
        for d in deps:
            if d[0] == "eng":
                _, e, idx, kind = d
                if e == eng:
                    continue
                s, v = self._eng_sem(e, idx)
            else:
                _, s, v = d
            key = id(s)
            if key not in waits or waits[key][1] < v:
                waits[key] = (s, v)
        if eng in ("act", "dve", "pool"):
            m = 0
            for b in reads:
                if b.w is not None and b.w[0] == "eng" and b.w[1] == eng:
                    m = max(m, b.w[2])
            if m:
                s, v = self._eng_sem(eng, m)
                key = id(s)
                if key not in waits or waits[key][1] < v:
                    waits[key] = (s, v)
        out = []
        kn = self.known[eng]
        for key, (s, v) in waits.items():
            if kn.get(key, 0) >= v:
                continue
            kn[key] = v
            out.append((s, v))
        return out

    def op(self, eng, fn, reads=(), writes=()):
        waits = self._collect(eng, reads, writes)
        self.cnt[eng] += 1
        idx = self.cnt[eng]
        s, v = self._eng_sem(eng, idx)
        self.ops[eng].append((waits, fn, (s, 1)))
        dep = ("eng", eng, idx, "c")
        for b in list(reads) + list(writes):
            if b.space == "psum":
                b.acc[eng] = idx
        for b in writes:
            b.w = dep
            b.r = []
        for b in reads:
            if b not in writes:
                b.r.append(dep)
        return idx

    def dma(self, out_ap, in_ap, dst=None, src=None, q="sp"):
        reads = [src] if src is not None else []
        writes = [dst] if dst is not None else []
        waits = self._collect(q, reads, writes)
        if dst is not None:
            if dst.sem_in is None:
                dst.sem_in = self.sem(f"di_{dst.name}")
            dst.n_in += 1
            s, v = dst.sem_in, 16 * dst.n_in
        elif src is not None:
            if src.sem_out is None:
                src.sem_out = self.sem(f"do_{src.name}")
            src.n_out += 1
            s, v = src.sem_out, 16 * src.n_out
        else:
            raise ValueError("dma needs a tracked side")
        dep = ("dma", s, v)
        self.ops[q].append((waits, lambda e, o=out_ap, i=in_ap: e.dma_start(out=o, in_=i), (s, 16)))
        if dst is not None:
            dst.w = dep
            dst.r = []
        if src is not None:
            src.r.append(dep)
        if dst is None:
            self.out_deps.append(dep)
        return dep

    def collective(self, kind, src, dst, replica_groups, q="pool"):
        waits = self._collect(q, [src], [dst])
        if dst.sem_in is None:
            dst.sem_in = self.sem(f"di_{dst.name}")
        dst.n_in += 1
        s, v = dst.sem_in, 16 * dst.n_in
        dep = ("dma", s, v)
        self.ops[q].append((waits, lambda e: e.collective_compute(
            kind, ALU.bypass, replica_groups=replica_groups, ins=[src.t[:]], outs=[dst.t[:]]), (s, 16)))
        dst.w = dep
        dst.r = []
        src.r.append(dep)
        return dep

    def mm(self, out, lhsT, rhs, start, stop, reads, writes):
        return self.op("pe", lambda e: e.matmul(out, lhsT, rhs, start=start, stop=stop),
                       reads, writes)

    def transpose(self, out, in_, ident, reads, writes):
        return self.op("pe", lambda e: e.transpose(out, in_, ident), reads, writes)

    def act(self, out, in_, func, reads, writes, bias=None, scale=None, accum_out=None, eng="act"):
        kw = {}
        if bias is not None:
            kw["bias"] = bias
        if scale is not None:
            kw["scale"] = scale
        if accum_out is not None:
            kw["accum_out"] = accum_out
        return self.op("act", lambda e: e.activation(out, in_, func, **kw), reads, writes)

    def tt(self, out, in0, in1, op, reads, writes, eng="dve"):
        return self.op(eng, lambda e: e.tensor_tensor(out, in0, in1, op), reads, writes)

    def ts(self, out, in0, s1, s2, op0, op1, reads, writes, eng="dve"):
        if op1 is None:
            return self.op(eng, lambda e: e.tensor_scalar(out, in0, s1, None, op0), reads, writes)
        return self.op(eng, lambda e: e.tensor_scalar(out, in0, s1, s2, op0, op1), reads, writes)

    def stt(self, out, in0, scalar, in1, op0, op1, reads, writes):
        return self.op("dve", lambda e: e.scalar_tensor_tensor(out, in0, scalar, in1, op0, op1),
                       reads, writes)

    def copy(self, out, in_, reads, writes, eng="dve"):
        if eng == "act":
            return self.op("act", lambda e: e.copy(out, in_), reads, writes)
        return self.op(eng, lambda e: e.tensor_copy(out, in_), reads, writes)

    def memset(self, out, val, writes, eng="dve"):
        return self.op(eng, lambda e: e.memset(out, val), (), writes)

    def finish(self):
        nc = self.nc
        fin_waits = {}
        for d in self.out_deps:
            _, s, v = d
            if id(s) not in fin_waits or fin_waits[id(s)][1] < v:
                fin_waits[id(s)] = (s, v)
        for e in self.ENGS:
            if e == "sp" or self.cnt[e] == 0:
                continue
            s, v = self._eng_sem(e, self.cnt[e])
            fin_waits[id(s)] = (s, v)
        engmap = {"pe": "tensor", "act": "scalar", "dve": "vector", "pool": "gpsimd", "sp": "sync"}
        with nc.Block() as block:
            for e in self.ENGS:
                ops = self.ops[e]
                last = (e == "sp")
                if not ops and not last:
                    continue

                def body(eng, ops=ops, last=last):
                    for waits, fn, (s, n) in ops:
                        for (ws, wv) in waits:
                            eng.wait_ge(ws, wv)
                        fn(eng).then_inc(s, n)
                    if last:
                        for (ws, wv) in fin_waits.values():
                            eng.wait_ge(ws, wv)

                getattr(block, engmap[e])(body)
        self.st.close()
        return nc


D = 1024
BATCH = 2
SEQ = 8192
CTX = 256
TT = CTX + SEQ
H_A, DK_A, DV_A = 8, 128, 128
H_B, DQK_B, DV_B = 4, 128, 256
W_A = 1024
W_B = 1024
GRID_W = 64
EPS = 1e-6
NCORES = 8
O_AQ, O_AK, O_AV, O_AZ = 0, 1024, 2048, 3072
O_ABETA, O_AALPHA = 4096, 4112
O_BQ, O_BK, O_BV, O_BO, O_BZ = 4128, 4640, 5152, 6176, 7200
O_BI, O_BF = 8224, 8232
O_GA, O_GB = 8240, 9264
NCH = 20
NGATE = 12


def core_cols(g):
    cols = []
    hA = (2 * g, 2 * g + 1)
    for base in (O_AQ, O_AK, O_AV, O_AZ):
        for h in hA:
            cols.extend(range(base + h * 128, base + (h + 1) * 128))
    for base in (O_BQ, O_BK):
        cols.extend(range(base + g * 128, base + (g + 1) * 128))
    for base in (O_BV, O_BO, O_BZ):
        cols.extend(range(base + g * 256, base + (g + 1) * 256))
    for base in (O_GA, O_GB):
        cols.extend(range(base + g * 256, base + (g + 1) * 256))
    assert len(cols) == NCH * 128
    gates = []
    for base in (O_ABETA, O_AALPHA):
        for d in range(2):
            for h in hA:
                gates.append(base + d * H_A + h)
    for base in (O_BI, O_BF):
        for d in range(2):
            gates.append(base + d * H_B + g)
    assert len(gates) == NGATE
    return cols, gates


CH_AQ, CH_AK, CH_AV, CH_AZ = (0, 1), (2, 3), (4, 5), (6, 7)
CH_BQ, CH_BK = 8, 9
CH_BV, CH_BO, CH_BZ = (10, 11), (12, 13), (14, 15)
CH_GA, CH_GB = (16, 17), (18, 19)


LAST_EXEC_NS = [None]


def run(nc, in_maps, trace=False):
    if trace:
        res = run_bass_kernel_spmd(nc, in_maps, core_ids=list(range(NCORES)), trace=True)
    else:
        res = run_bass_kernel_spmd(nc, in_maps, core_ids=list(range(NCORES)))
    LAST_EXEC_NS[0] = getattr(res, "exec_time_ns", None)
    return res.results


def build_mod():
    P = Prog()
    W = P.dram("W", [6, D, 128], F32, "ExternalInput")
    cc = P.dram("cc", [128, 8, 4], F32, "ExternalInput")
    bias = P.dram("bias", [128, 6], F32, "ExternalInput")
    out = P.dram("out", [6, 128, 4], F32, "ExternalOutput")
    P.alloc_psum_banks(2)
    ccs = P.sbuf("ccs", [128, 8, 4])
    scs = P.sbuf("scs", [128, 8, 4])
    bs = P.sbuf("bs", [128, 6])
    os_ = P.sbuf("os", [128, 6, 4])
    P.dma(ccs[:], cc[:, :, :], dst=ccs)
    P.dma(bs[:], bias[:, :], dst=bs)
    P.act(scs[:], ccs[:], AF.Silu, [ccs], [scs])
    wts = [P.sbuf(f"w{i}", [128, 8, 128]) for i in range(2)]
    for i in range(6):
        wt = wts[i % 2]
        P.dma(wt[:], W[i].rearrange("(k p) c -> p k c", p=128), dst=wt)
        ps = P.bank()
        for k in range(8):
            P.mm(ps[:, 0:4], wt[:, k, :], scs[:, k, :], k == 0, k == 7, [wt, scs], [ps])
        P.ts(os_[:, i, :], ps[:, 0:4], bs[:, i:i + 1], None, ALU.add, None, [ps, bs], [os_])
    P.dma(out.rearrange("i p n -> p i n"), os_[:], src=os_)
    return P.finish()


def stage_mod(inp):
    nc = build_mod()
    c4 = np.stack([inp["c"][0], inp["c"][1], inp["c_ctx"], inp["c_ctx"]], axis=-1)
    cc = np.ascontiguousarray(c4.reshape(8, 128, 4).transpose(1, 0, 2))
    maps = []
    for j in range(NCORES):
        Ws, bs = [], []
        for i in range(6):
            job = j * 6 + i
            l, fc = divmod(job, 24)
            Ws.append(inp["ada_w"][l][:, fc * 128:(fc + 1) * 128])
            bs.append(inp["ada_b"][l][fc * 128:(fc + 1) * 128])
        maps.append({"W": np.ascontiguousarray(np.stack(Ws)), "cc": cc,
                     "bias": np.ascontiguousarray(np.stack(bs, axis=1))})
    res = run(nc, maps)
    mod = np.zeros((2, 3, 3 * D), np.float32)
    for j in range(NCORES):
        o = res[j]["out"]
        for i in range(6):
            job = j * 6 + i
            l, fc = divmod(job, 24)
            for n in range(3):
                mod[l, n, fc * 128:(fc + 1) * 128] = o[i, :, n]
    return mod


P_TILES = [(0, CTX)] + [(CTX + i * 512, CTX + (i + 1) * 512) for i in range(SEQ // 512)]


def build_proj(ntiles=None):
    P = Prog()
    xT = P.dram("xT", [D, TT], F32, "ExternalInput")
    Wg = P.dram("Wg", [D, NCH * 128 + 16], F32, "ExternalInput")
    prm = P.dram("prm", [128, 5, 8], F32, "ExternalInput")
    pT = P.dram("pT", [NCH * 128, TT], F32, "ExternalOutput")
    gT = P.dram("gT", [16, TT], F32, "ExternalOutput")
    P.alloc_psum_banks(8)
    NW = NCH * 128 + 16
    W16 = P.sbuf("W16", [128, 8, NW], BF16)
    ones = P.sbuf("ones", [128, 128])
    P.memset(ones[:], 1.0, [ones])
    prms = P.sbuf("prms", [128, 5, 8])
    P.dma(prms[:], prm[:, :, :], dst=prms)
    epsb = P.sbuf("epsb", [128, 1])
    P.memset(epsb[:], EPS, [epsb])
    GS = P.sbuf("GS", [128, 4, 8])
    P.stt(GS[:, 0, :], prms[:, 1, :], 1.0, prms[:, 0, :], ALU.add, ALU.mult, [prms], [GS])
    P.copy(GS[:, 1, :], prms[:, 2, :], [prms], [GS])
    P.stt(GS[:, 2, :], prms[:, 3, :], 1.0, prms[:, 0, :], ALU.add, ALU.mult, [prms], [GS])
    P.copy(GS[:, 3, :], prms[:, 4, :], [prms], [GS])
    Wv = Wg.rearrange("(k p) c -> p k c", p=128)
    stg = [P.sbuf(f"wst{i}", [128, 8, 512]) for i in range(2)]
    c0 = 0
    i = 0
    while c0 < NW:
        c1 = min(NW, c0 + 512)
        s = stg[i % 2]
        P.dma(s[:, :, 0:c1 - c0], Wv[:, :, c0:c1], dst=s)
        P.copy(W16[:, :, c0:c1], s[:, :, 0:c1 - c0], [s], [W16], eng=("dve" if i % 2 == 0 else "pool"))
        c0 = c1
        i += 1
    xv = xT.rearrange("(k p) t -> p k t", p=128)
    xts = [P.sbuf(f"xt{i}", [128, 8, 512]) for i in range(2)]
    sq = P.sbuf("sq", [128, 8, 512])
    hTs = [P.sbuf(f"hT{i}", [128, 8, 512], BF16) for i in range(2)]
    rstd = P.sbuf("rstd", [128, 512])
    tmp = [P.sbuf(f"tmp{i}", [128, 512]) for i in range(2)]
    evs = [P.sbuf(f"ev{i}", [128, 512]) for i in range(4)]
    gev = P.sbuf("gev", [16, 512])
    nev = 0
    for ti, (t0, t1) in enumerate(P_TILES[:ntiles] if ntiles else P_TILES):
        n = t1 - t0
        xt = xts[ti % 2]
        hT = hTs[ti % 2]
        P.dma(xt[:, :, 0:n], xv[:, :, t0:t1], dst=xt)
        P.act(sq[:, :, 0:n], xt[:, :, 0:n], AF.Square, [xt], [sq])
        ss = P.bank()
        for k in range(8):
            P.mm(ss[:, 0:n], ones[:], sq[:, k, 0:n], k == 0, k == 7, [ones, sq], [ss])
        P.act(rstd[:, 0:n], ss[:, 0:n], AF.Ln, [ss, epsb], [rstd], bias=epsb[:, 0:1], scale=1.0 / D)
        P.act(rstd[:, 0:n], rstd[:, 0:n], AF.Exp, [rstd], [rstd], scale=-0.5)
        gi = 2 if ti == 0 else 0
        for k in range(8):
            tm = tmp[k % 2]
            P.tt(tm[:, 0:n], xt[:, k, 0:n], rstd[:, 0:n], ALU.mult, [xt, rstd], [tm])
            P.act(hT[:, k, 0:n], tm[:, 0:n], AF.Identity, [tm, GS], [hT],
                  scale=GS[:, gi, k:k + 1], bias=GS[:, gi + 1, k:k + 1])
        for c in range(NCH):
            ps = P.bank()
            for k in range(8):
                P.mm(ps[:, 0:n], W16[:, k, c * 128:(c + 1) * 128], hT[:, k, 0:n], k == 0, k == 7,
                     [W16, hT], [ps])
            ev = evs[nev % 4]
            P.copy(ev[:, 0:n], ps[:, 0:n], [ps], [ev], eng=("act" if nev % 2 else "dve"))
            nev += 1
            P.dma(pT[c * 128:(c + 1) * 128, t0:t1], ev[:, 0:n], src=ev)
        ps = P.bank()
        for k in range(8):
            P.mm(ps[0:16, 0:n], W16[:, k, NCH * 128:NCH * 128 + 16], hT[:, k, 0:n], k == 0, k == 7,
                 [W16, hT], [ps])
        P.copy(gev[:, 0:n], ps[0:16, 0:n], [ps], [gev])
        P.dma(gT[:, t0:t1], gev[:, 0:n], src=gev)
    return P.finish()


def feat_major(v):
    return np.ascontiguousarray(v.reshape(8, 128).T)


def stage_proj(nc, xT_b, w_in_l, norm_g_l, mod_l):
    maps = []
    for j in range(NCORES):
        b, g = divmod(j, 4)
        cols, gates = core_cols(g)
        Wg = np.zeros((D, NCH * 128 + 16), np.float32)
        Wg[:, :NCH * 128] = w_in_l[:, cols]
        Wg[:, NCH * 128:NCH * 128 + NGATE] = w_in_l[:, gates]
        shift, scale = mod_l[b][0:D], mod_l[b][D:2 * D]
        shift_c, scale_c = mod_l[2][0:D], mod_l[2][D:2 * D]
        prm = np.stack([feat_major(norm_g_l), feat_major(scale), feat_major(shift),
                        feat_major(scale_c), feat_major(shift_c)], axis=1)
        maps.append({"xT": xT_b[b], "Wg": Wg, "prm": np.ascontiguousarray(prm)})
    return run(nc, maps)


CPAD = (CTX + 4) + (SEQ + 4)
C_TILES = [(0, 0, CTX)] + [(CTX + 4, i * 512, 512) for i in range(SEQ // 512)]


def build_conv():
    P = Prog()
    pre = P.dram("pre", [6, 128, CPAD], F32, "ExternalInput")
    cw = P.dram("cw", [128, 6, 5], F32, "ExternalInput")
    gin = {nm: P.dram(nm, [r, TT], F32, "ExternalInput") for nm, r in
           (("betaT", 4), ("alphaT", 4), ("iT", 2), ("fT", 2))}
    gprm = P.dram("gprm", [4, 4], F32, "ExternalInput")
    qkv = P.dram("qkv", [6, 128, TT], F32, "ExternalOutput")
    gout = {nm: P.dram(nm, [r, TT], F32, "ExternalOutput") for nm, r in
            (("beta", 4), ("g", 4), ("ig", 2), ("lf", 2))}
    P.alloc_psum_banks(4)
    ones = P.sbuf("ones", [128, 128])
    P.memset(ones[:], 1.0, [ones])
    cws = P.sbuf("cws", [128, 6, 5])
    P.dma(cws[:], cw[:, :, :], dst=cws)
    epsb = P.sbuf("epsb", [128, 1])
    P.memset(epsb[:], EPS, [epsb])
    gp = P.sbuf("gp", [4, 4])
    P.dma(gp[:], gprm[:, :], dst=gp)
    gp2 = P.sbuf("gp2", [4, 4])
    P.act(gp2[:, 0:1], gp[:, 0:1], AF.Exp, [gp], [gp2])
    P.ts(gp2[:, 1:2], gp2[:, 0:1], -1.0, None, ALU.mult, None, [gp2], [gp2])
    P.ts(gp2[0:2, 2:3], gp[0:2, 3:4], -1.0, None, ALU.mult, None, [gp], [gp2])
    g1 = P.sbuf("g1", [4, TT])
    g2 = P.sbuf("g2", [4, TT])
    P.dma(g1[:], gin["betaT"][:, :], dst=g1)
    P.act(g2[:], g1[:], AF.Sigmoid, [g1], [g2])
    P.dma(gout["beta"][:, :], g2[:], src=g2)
    P.dma(g1[:], gin["alphaT"][:, :], dst=g1)
    P.act(g1[:], g1[:], AF.Exp, [g1, gp], [g1], bias=gp[:, 1:2])
    P.act(g1[:], g1[:], AF.Ln, [g1], [g1], bias=1.0)
    P.ts(g2[:], g1[:], gp2[:, 1:2], None, ALU.mult, None, [g1, gp2], [g2])
    P.dma(gout["g"][:, :], g2[:], src=g2)
    P.dma(g1[0:2, :], gin["iT"][:, :], dst=g1)
    P.ts(g2[0:2, :], g1[0:2, :], gp[0:2, 2:3], None, ALU.add, None, [g1, gp], [g2])
    P.dma(gout["ig"][:, :], g2[0:2, :], src=g2)
    P.dma(g1[0:2, :], gin["fT"][:, :], dst=g1)
    P.act(g1[0:2, :], g1[0:2, :], AF.Exp, [g1, gp2], [g1], bias=gp2[0:2, 2:3], scale=-1.0)
    P.act(g1[0:2, :], g1[0:2, :], AF.Ln, [g1], [g1], bias=1.0)
    P.ts(g2[0:2, :], g1[0:2, :], -1.0, None, ALU.mult, None, [g1], [g2])
    P.dma(gout["lf"][:, :], g2[0:2, :], src=g2)
    NBUF = 2
    xin = [P.sbuf(f"xin{i}", [128, 516]) for i in range(6 * NBUF)]
    acc = [P.sbuf(f"acc{i}", [128, 512]) for i in range(3)]
    ys = [P.sbuf(f"y{i}", [128, 512]) for i in range(6 * NBUF)]
    sqs = [P.sbuf(f"sq{i}", [128, 512]) for i in range(4)]
    rs = [P.sbuf(f"r{i}", [128, 512]) for i in range(4)]
    outs = [P.sbuf(f"o{i}", [128, 512]) for i in range(6)]
    it = 0
    io = 0
    for ti, (off, t0, n) in enumerate(C_TILES):
        tok0 = (0 if off == 0 else CTX) + t0
        for c in range(6):
            xi = xin[(ti % NBUF) * 6 + c]
            a = acc[it % 3]
            y = ys[(ti % NBUF) * 6 + c]
            P.dma(xi[:, 0:n + 4], pre[c, :, off + t0:off + t0 + n + 4], dst=xi)
            P.ts(a[:, 0:n], xi[:, 0:n], cws[:, c, 0:1], None, ALU.mult, None, [xi, cws], [a])
            for s in range(1, 5):
                P.stt(a[:, 0:n], xi[:, s:s + n], cws[:, c, s:s + 1], a[:, 0:n], ALU.mult, ALU.add,
                      [xi, cws, a], [a])
            if c < 4:
                P.act(y[:, 0:n], a[:, 0:n], AF.Silu, [a], [y])
            else:
                o = outs[io % 6]
                io += 1
                P.act(o[:, 0:n], a[:, 0:n], AF.Silu, [a], [o])
                P.dma(qkv[c, :, tok0:tok0 + n], o[:, 0:n], src=o)
            it += 1
        for c in range(4):
            y = ys[(ti % NBUF) * 6 + c]
            sq = sqs[c]
            r = rs[c]
            o = outs[io % 6]
            io += 1
            P.tt(sq[:, 0:n], y[:, 0:n], y[:, 0:n], ALU.mult, [y], [sq], eng="pool")
            ps = P.bank()
            P.mm(ps[:, 0:n], ones[:], sq[:, 0:n], True, True, [ones, sq], [ps])
            P.act(r[:, 0:n], ps[:, 0:n], AF.Ln, [ps, epsb], [r], bias=epsb[:, 0:1])
            P.act(r[:, 0:n], r[:, 0:n], AF.Exp, [r], [r], scale=-0.5)
            sc = (DK_A ** -0.5) if c < 2 else 1.0
            P.stt(o[:, 0:n], y[:, 0:n], sc, r[:, 0:n], ALU.mult, ALU.mult, [y, r], [o])
            P.dma(qkv[c, :, tok0:tok0 + n], o[:, 0:n], src=o)
    return P.finish()


def pad_seq(a):
    z = np.zeros(a.shape[:-1] + (2,), a.dtype)
    return np.ascontiguousarray(np.concatenate([z, a[..., :CTX], z, z, a[..., CTX:], z], axis=-1))


def stage_conv(nc, proj_res, conv_w_l, a_log_l, dt_bias_l, i_bias_l, f_bias_l):
    maps = []
    for j in range(NCORES):
        b, g = divmod(j, 4)
        pT = proj_res[j]["pT"]
        gT = proj_res[j]["gT"]
        pre = pad_seq(pT[0:768].reshape(6, 128, TT))
        hA = (2 * g, 2 * g + 1)
        chans = []
        for base in (0, 1024, 2048):
            for h in hA:
                chans.append(np.arange(base + h * 128, base + (h + 1) * 128))
        cw = np.stack([conv_w_l[:, ch].T for ch in chans], axis=1)
        gprm = np.zeros((4, 4), np.float32)
        k = 0
        for d in range(2):
            for h in hA:
                gprm[k, 0] = a_log_l[d, h]
                gprm[k, 1] = dt_bias_l[d, h]
                k += 1
        for d in range(2):
            gprm[d, 2] = i_bias_l[d, g]
            gprm[d, 3] = f_bias_l[d, g]
        maps.append({"pre": pre, "cw": np.ascontiguousarray(cw), "gprm": gprm,
                     "betaT": np.ascontiguousarray(gT[0:4]), "alphaT": np.ascontiguousarray(gT[4:8]),
                     "iT": np.ascontiguousarray(gT[8:10]), "fT": np.ascontiguousarray(gT[10:12])})
    return run(nc, maps)


class View:
    def __init__(self, buf, ap):
        self.buf = buf
        self.ap = ap

    def __getitem__(self, idx):
        return View(self.buf, self.ap[idx])


def V(buf, *idx):
    if not idx:
        return View(buf, buf.t[:])
    return View(buf, buf.t[idx if len(idx) > 1 else idx[0]])


def _u(x):
    return x.ap if isinstance(x, View) else x


def _b(*xs):
    out = []
    for x in xs:
        if isinstance(x, View) and x.buf not in out:
            out.append(x.buf)
    return out


class AProg(Prog):
    def quarters(self, nbanks=8):
        self.alloc_psum_banks(nbanks)
        self.qtiles = []
        for b in self.psum_banks:
            for q in range(4):
                self.qtiles.append(Buf(self, b.t[:, q * 128:(q + 1) * 128], f"{b.name}q{q}", "psum"))
        self.qi = 0

    def q(self):
        t = self.qtiles[self.qi % len(self.qtiles)]
        self.qi += 1
        return V(t)

    def amm(self, out, lhsT, rhs, start=True, stop=True):
        return self.mm(_u(out), _u(lhsT), _u(rhs), start, stop, _b(lhsT, rhs), _b(out))

    def atr(self, out, in_, ident):
        return self.transpose(_u(out), _u(in_), _u(ident), _b(in_, ident), _b(out))

    def aact(self, out, in_, func, bias=None, scale=None):
        return self.act(_u(out), _u(in_), func, _b(in_, bias, scale), _b(out),
                        bias=_u(bias) if bias is not None else None,
                        scale=_u(scale) if scale is not None else None)

    def att(self, out, in0, in1, op, eng="dve"):
        return self.tt(_u(out), _u(in0), _u(in1), op, _b(in0, in1), _b(out), eng=eng)

    def ats(self, out, in0, s1, op0, s2=None, op1=None, eng="dve"):
        return self.ts(_u(out), _u(in0), _u(s1), _u(s2), op0, op1, _b(in0, s1, s2), _b(out), eng=eng)

    def astt(self, out, in0, scalar, in1, op0, op1):
        return self.stt(_u(out), _u(in0), _u(scalar), _u(in1), op0, op1, _b(in0, scalar, in1), _b(out))

    def acopy(self, out, in_, eng="dve"):
        return self.copy(_u(out), _u(in_), _b(in_), _b(out), eng=eng)

    def amemset(self, out, val, eng="dve"):
        return self.memset(_u(out), val, _b(out), eng=eng)

    def aload(self, dst, src_ap, q="sp"):
        return self.dma(_u(dst), src_ap, dst=dst.buf, q=q)

    def astore(self, dst_ap, src, q="sp"):
        return self.dma(dst_ap, _u(src), src=src.buf, q=q)


def run_round_robin(gens):
    gens = list(gens)
    while gens:
        nxt = []
        for g in gens:
            try:
                next(g)
                nxt.append(g)
            except StopIteration:
                pass
        gens = nxt


BIG = 30000.0
NSC = TT // 128


def make_masks():
    i = np.arange(128)
    same = (i[:, None] // 64) == (i[None, :] // 64)
    U = ((i[:, None] <= i[None, :]) & same).astype(np.float32)
    BD = same.astype(np.float32)
    PL = np.where((i[None, :] < i[:, None]) & same, 0.0, BIG).astype(np.float32)
    NU = np.where((i[:, None] <= i[None, :]) & same, 0.0, -BIG).astype(np.float32)
    ID = np.eye(128, dtype=np.float32)
    NL = np.where((i[None, :] <= i[:, None]) & same, 0.0, -BIG).astype(np.float32)
    ON = np.ones((128, 128), np.float32)
    return np.ascontiguousarray(np.stack([U, BD, PL, NU, ID, NL, ON, -U], axis=1))


M_U, M_BD, M_PL, M_NU, M_ID, M_NL, M_ONES, M_NEGU = range(8)
NMSK = 8


def delta_problem(P, pi, msk, dr, S, o_out, banks, nsc=NSC):
    f = lambda nm, shape=(128, 128): P.sbuf(f"{nm}_{pi}", list(shape))
    fr = lambda nm, shape=(128, 128): P.sbuf(f"{nm}_{pi}", list(shape), F32R)
    qA = lambda: V(banks[0])[:, 0:128]
    qB = lambda: V(banks[1])[:, 0:128]
    qTs = [f(f"qT{i}") for i in range(2)]
    kTs = [f(f"kT{i}") for i in range(2)]
    kts = [f(f"ktm{i}") for i in range(2)]
    vts = [f(f"vtm{i}") for i in range(2)]
    bcol = f("bcol", (128, NSC))
    nbcol = f("nbcol", (128, NSC))
    gcol = f("gcol", (128, NSC, 2))
    cols = [f(f"cols{i}", (128, 8)) for i in range(2)]
    gbc = f("gbc"); RL = f("RL"); RU = f("RU"); dL = f("dL"); dT = f("dT"); eg = [f("eg0"), f("eg1")]
    BT = f("BT"); Bm = f("Bm")
    Pa = [fr("Pa0"), fr("Pa1")]; PaT = [fr("PaT0"), fr("PaT1")]
    X = [fr("X0"), fr("X1")]
    qTr = fr("qTr"); kTr = fr("kTr")
    kbg = fr("kbg"); vb = fr("vb"); kdec = [f("kdec0"), f("kdec1")]
    wT = f("wT"); u = f("u"); qkT = [f("qkT0"), f("qkT1")]; qdT = [f("qdT0"), f("qdT1")]
    vnew = f("vnew")
    osb = [f("osb0"), f("osb1")]
    P.amemset(V(vnew), 0.0)
    P.aload(V(bcol), dr["bcol"][:, pi, :])
    P.aload(V(gcol), dr["gcol"][:, pi, :, :])
    P.ats(V(nbcol), V(bcol), -1.0, ALU.mult)
    U = V(msk, slice(None), M_U, slice(None))
    BDm = V(msk, slice(None), M_BD, slice(None))
    PLm = V(msk, slice(None), M_PL, slice(None))
    NUm = V(msk, slice(None), M_NU, slice(None))
    ID = V(msk, slice(None), M_ID, slice(None))
    ONESm = V(msk, slice(None), M_ONES, slice(None))
    yield
    for sc in range(nsc):
        t0 = sc * 128
        qT = V(qTs[sc % 2]); kT = V(kTs[sc % 2]); ktm = V(kts[sc % 2]); vtm = V(vts[sc % 2])
        P.aload(qT, dr["qT"][pi, :, t0:t0 + 128])
        P.aload(kT, dr["kT"][pi, :, t0:t0 + 128])
        P.aload(ktm, dr["ktm"][pi, t0:t0 + 128, :])
        P.aload(vtm, dr["vtm"][pi, t0:t0 + 128, :])
        cl = V(cols[sc % 2])
        gps = qA()
        P.amm(gps[:, 0:2], U, V(gcol)[:, sc, :])
        P.amm(gps[:, 2:4], BDm, V(gcol)[:, sc, :])
        P.ats(V(gbc), ONESm, V(gcol)[:, sc, 0:1], ALU.mult, eng="pool")
        yield
        grow = qB()
        P.amm(grow, V(gbc), U)
        P.acopy(cl[:, 0:1], gps[:, 0:1])
        P.ats(cl[:, 1:2], gps[:, 0:1], -1.0, ALU.mult)
        yield
        P.aact(cl[:, 2:3], gps[:, 0:1], AF.Exp)
        P.aact(cl[:, 3:4], gps[:, 2:3], AF.Exp, bias=cl[:, 1:2])
        P.att(V(RL), grow, PLm, ALU.add)
        P.att(V(RU), grow, NUm, ALU.add)
        egr = V(eg[sc % 2])
        P.aact(egr, grow, AF.Exp)
        yield
        P.aact(V(dL), V(RL), AF.Exp, bias=cl[:, 0:1], scale=-1.0)
        P.aact(V(dT), V(RU), AF.Exp, bias=cl[:, 1:2])
        P.att(cl[:, 4:5], cl[:, 2:3], V(bcol)[:, sc:sc + 1], ALU.mult)
        P.acopy(V(kTr), kT, eng="act")
        P.acopy(V(qTr), qT, eng="act")
        kk = qA()
        P.amm(kk, V(kTr), V(kTr))
        qk = qB()
        P.amm(qk, V(kTr), V(qTr))
        yield
        P.astt(V(BT), kk, V(nbcol)[:, sc:sc + 1], V(dL), ALU.mult, ALU.mult)
        P.aact(V(kbg), ktm, AF.Identity, scale=cl[:, 4:5])
        P.aact(V(vb), vtm, AF.Identity, scale=V(bcol)[:, sc:sc + 1])
        kd = V(kdec[sc % 2])
        P.ats(kd, ktm, cl[:, 3:4], ALU.mult, eng="pool")
        qkTs = V(qkT[sc % 2])
        P.att(qkTs, qk, V(dT), ALU.mult)
        qd = V(qdT[sc % 2])
        P.att(qd, qT, egr, ALU.mult, eng="pool")
        yield
        bps = qA()
        P.atr(bps, V(BT), ID)
        yield
        P.acopy(V(Bm), bps, eng="act")
        P.att(V(X[0]), bps, ID, ALU.add)
        yield
        cur, curT = V(Bm), V(BT)
        xi = 0
        for lev in range(5):
            last = lev == 4
            p2T = qA()
            P.amm(p2T, cur, curT)
            if not last:
                p2 = qB()
                P.amm(p2, curT, cur)
            yield
            nT = V(PaT[lev % 2])
            P.acopy(nT, p2T, eng="act")
            if not last:
                n_ = V(Pa[lev % 2])
                P.acopy(n_, p2)
            yield
            xp = qA()
            P.amm(xp, nT, V(X[xi]))
            yield
            P.att(V(X[1 - xi]), xp, View(X[xi], X[xi].t[:].bitcast(F32)), ALU.add)
            xi = 1 - xi
            if not last:
                cur, curT = n_, nT
            yield
        TinvT = V(X[xi])
        wps = qA()
        P.amm(wps, V(kbg), TinvT)
        ups = qB()
        P.amm(ups, TinvT, V(vb))
        yield
        P.acopy(V(wT), wps, eng="act")
        P.acopy(V(u), ups)
        yield
        ob = V(osb[sc % 2])
        for c in range(2):
            r = slice(c * 64, c * 64 + 64)
            vps = qA()
            P.amm(vps[r, :], V(wT)[:, r], V(S))
            yield
            P.att(V(vnew)[r, :], V(u)[r, :], vps[r, :], ALU.subtract)
            yield
            ops_ = qB()
            P.amm(ops_[r, :], qd[:, r], V(S), True, False)
            P.amm(ops_[r, :], qkTs[:, r], V(vnew), False, True)
            sps = qA()
            P.amm(sps, kd[r, :], V(vnew)[r, :])
            yield
            P.astt(V(S), V(S), egr[:, c * 64 + 63:c * 64 + 64], sps, ALU.mult, ALU.add)
            P.acopy(ob[r, :], ops_[r, :], eng="act")
            yield
        P.astore(o_out[pi, t0:t0 + 128, :], ob)
        yield


def build_delta(nsc=NSC, nprob=4):
    P = AProg()
    dr = {
        "qT": P.dram("qT", [4, 128, TT], F32, "ExternalInput"),
        "kT": P.dram("kT", [4, 128, TT], F32, "ExternalInput"),
        "ktm": P.dram("ktm", [4, TT, 128], F32, "ExternalInput"),
        "vtm": P.dram("vtm", [4, TT, 128], F32, "ExternalInput"),
        "bcol": P.dram("bcol", [128, 4, NSC], F32, "ExternalInput"),
        "gcol": P.dram("gcol", [128, 4, NSC, 2], F32, "ExternalInput"),
    }
    mskd = P.dram("msk", [128, NMSK, 128], F32, "ExternalInput")
    o_out = P.dram("o", [4, TT, 128], F32, "ExternalOutput")
    P.alloc_psum_banks(8)
    msk = P.sbuf("msk_sb", [128, NMSK, 128])
    P.aload(V(msk), mskd[:, :, :])
    gens = []
    for pi in range(nprob):
        S = P.sbuf(f"S_{pi}", [128, 128])
        P.amemset(V(S), 0.0)
        gens.append(delta_problem(P, pi, msk, dr, S, o_out,
                                  (P.psum_banks[2 * pi], P.psum_banks[2 * pi + 1]), nsc))
    run_round_robin(gens)
    return P.finish()


def prob_order(d):
    ci = np.arange(CTX)
    li = CTX + np.arange(SEQ)
    if d == 1:
        ci = ci[::-1]
        li = li[::-1]
    return np.concatenate([ci, li])


def col_layout(v):
    return np.ascontiguousarray(v.reshape(NSC, 128).T)


def stage_delta(nc, conv_res):
    msk = make_masks()
    maps = []
    for j in range(NCORES):
        r = conv_res[j]
        qT, kT, ktm, vtm = [], [], [], []
        bcol = np.zeros((128, 4, NSC), np.float32)
        gcol = np.zeros((128, 4, NSC, 2), np.float32)
        for pi in range(4):
            d, hh = divmod(pi, 2)
            idx = prob_order(d)
            qT.append(r["qkv"][0 + hh][:, idx])
            kk = r["qkv"][2 + hh][:, idx]
            kT.append(kk)
            ktm.append(kk.T)
            vtm.append(r["qkv"][4 + hh][:, idx].T)
            bcol[:, pi, :] = col_layout(r["beta"][pi][idx])
            gc = col_layout(r["g"][pi][idx])
            gcol[:, pi, :, 0] = gc
            gcol[:, pi, :, 1] = gc
        maps.append({"qT": np.ascontiguousarray(np.stack(qT)), "kT": np.ascontiguousarray(np.stack(kT)),
                     "ktm": np.ascontiguousarray(np.stack(ktm)), "vtm": np.ascontiguousarray(np.stack(vtm)),
                     "bcol": bcol, "gcol": gcol, "msk": msk})
    return run(nc, maps)


KSC = DQK_B ** -0.5
NEGBIG = -1.0e30


def mlstm_problem(P, pi, msk, dr, h_out, banks, nsc=NSC):
    f = lambda nm, shape=(128, 128): P.sbuf(f"{nm}_m{pi}", list(shape))
    qA = lambda: V(banks[0])
    qB = lambda: V(banks[1])
    qC = lambda: V(banks[2])
    qD = lambda: V(banks[3])
    qTs = [f(f"qT{i}") for i in range(2)]
    kTs = [f(f"kT{i}") for i in range(2)]
    kts = [f(f"ktm{i}") for i in range(2)]
    vas = [f(f"vaug{i}", (128, 258)) for i in range(2)]
    igc = f("igc", (128, NSC))
    lfc = f("lfc", (128, NSC, 2))
    cols = [f(f"cols{i}", (128, 8)) for i in range(2)]
    lfb = f("lfb"); igb = f("igb"); brs = [f("brs0"), f("brs1")]; Rm = f("Rm")
    pmx = [f("pmx0"), f("pmx1")]; dmr = f("dmr")
    mtr = f("mtr", (128, 64)); Er = f("Er", (128, 64)); inr = f("inr", (128, 64)); tmpw = f("tmpw", (128, 64))
    W1 = f("W1", (128, 64))
    qd = f("qd")
    wts = [f("wts0"), f("wts1")]
    kw = f("kw")
    Cn = f("Cn", (128, 258))
    ms = [f("m0", (128, 1)), f("m1", (128, 1))]
    sm = [f("sm0", (128, 8)), f("sm1", (128, 8))]
    hsb = [f("hsb0", (128, 256)), f("hsb1", (128, 256))]
    P.amemset(V(Cn), 0.0)
    P.amemset(V(ms[0]), 0.0)
    for w_ in wts:
        P.amemset(V(w_), 0.0, eng="pool")
    for va in vas:
        P.amemset(V(va)[:, 256:258], 1.0, eng="pool")
    P.aload(V(igc), dr["igc"][:, pi, :])
    P.aload(V(lfc), dr["lfc"][:, pi, :, :])
    U = V(msk, slice(None), M_U, slice(None))
    NUm = V(msk, slice(None), M_NU, slice(None))
    ID = V(msk, slice(None), M_ID, slice(None))
    NLm = V(msk, slice(None), M_NL, slice(None))
    ONESm = V(msk, slice(None), M_ONES, slice(None))
    NEGU = V(msk, slice(None), M_NEGU, slice(None))
    mi = 0
    yield
    for sc in range(nsc):
        t0 = sc * 128
        qT = V(qTs[sc % 2]); kT = V(kTs[sc % 2]); ktm = V(kts[sc % 2]); va = V(vas[sc % 2])
        P.aload(qT, dr["qT"][pi, :, t0:t0 + 128])
        P.aload(kT, dr["kT"][pi, :, t0:t0 + 128])
        P.aload(ktm, dr["ktm"][pi, t0:t0 + 128, :])
        P.aload(va[:, 0:256], dr["vtm"][pi, t0:t0 + 128, :])
        cl = V(cols[sc % 2])
        bps = qA()
        P.amm(bps[:, 0:2], U, V(lfc)[:, sc, :])
        P.ats(V(lfb), ONESm, V(lfc)[:, sc, 0:1], ALU.mult, eng="pool")
        P.ats(V(igb), ONESm, V(igc)[:, sc:sc + 1], ALU.mult, eng="pool")
        yield
        brow = qB()
        P.amm(brow[:, 0:128], V(lfb), U)
        rrow = qC()
        P.amm(rrow[:, 0:128], V(igb), ID, True, False)
        P.amm(rrow[:, 0:128], V(lfb), NEGU, False, True)
        qk = qD()
        P.amm(qk[:, 0:128], kT, qT)
        P.acopy(cl[:, 0:1], bps[:, 0:1])
        yield
        br = V(brs[sc % 2])
        P.acopy(br, brow[:, 0:128], eng="act")
        P.att(cl[:, 1:2], V(igc)[:, sc:sc + 1], cl[:, 0:1], ALU.subtract)
        P.att(V(Rm), rrow[:, 0:128], NLm, ALU.add)
        pm = V(pmx[sc % 2])
        for c in range(2):
            r = slice(c * 64, c * 64 + 64)
            P.op("dve", lambda e, o=_u(pm[:, r]), d0=_u(ONESm[:, r]), d1=_u(rrow[:, r]):
                 e.tensor_tensor_scan(o, d0, d1, NEGBIG, ALU.mult, ALU.max),
                 _b(ONESm, rrow), _b(pm))
        yield
        P.op("dve", lambda e, o=_u(cl[:, 2:3]), i_=_u(V(Rm)): e.tensor_reduce(o, i_, AX.X, ALU.max),
             _b(V(Rm)), _b(cl))
        P.att(V(dmr), br, pm, ALU.add)
        yield
        P.att(cl[:, 3:4], cl[:, 0:1], cl[:, 2:3], ALU.add)
        yield
        hb_ = V(hsb[sc % 2])
        for c in range(2):
            r = slice(c * 64, c * 64 + 64)
            last = c * 64 + 63
            m_old = V(ms[mi]); m_new = V(ms[1 - mi])
            s_ = V(sm[c])
            P.astt(V(mtr), br[:, r], m_old[:, 0:1], V(dmr)[:, r], ALU.add, ALU.max)
            P.astt(s_[:, 0:1], cl[:, 0:1], m_old[:, 0:1], cl[:, 3:4], ALU.add, ALU.max)
            P.att(s_[:, 1:2], m_old[:, 0:1], pm[:, last:last + 1], ALU.max)
            yield
            P.att(V(Er), br[:, r], V(mtr), ALU.subtract)
            P.aact(s_[:, 2:3], s_[:, 0:1], AF.Exp, scale=-1.0)
            P.att(m_new[:, 0:1], s_[:, 1:2], br[:, last:last + 1], ALU.add)
            yield
            P.aact(V(inr), V(Er), AF.Exp, bias=m_old[:, 0:1])
            P.att(V(tmpw)[r, :], V(Er)[r, :], NUm[r, r], ALU.add, eng="pool")
            P.att(s_[:, 3:4], br[:, last:last + 1], m_new[:, 0:1], ALU.subtract)
            yield
            P.att(V(qd)[:, r], qT[:, r], V(inr), ALU.mult, eng="pool")
            P.aact(V(W1)[r, :], V(tmpw)[r, :], AF.Exp, bias=cl[r, 1:2])
            P.aact(s_[:, 4:5], cl[:, 1:2], AF.Exp, bias=s_[:, 3:4])
            P.aact(s_[:, 5:6], s_[:, 3:4], AF.Exp, bias=m_old[:, 0:1])
            yield
            wt = V(wts[c])
            P.astt(wt[r, r], qk[r, r], KSC, V(W1)[r, :], ALU.mult, ALU.mult)
            P.ats(V(kw)[r, :], ktm[r, :], s_[r, 4:5], ALU.mult, KSC, ALU.mult)
            yield
            nps = qA()
            P.amm(nps[r, 0:258], V(qd)[:, r], V(Cn), True, False)
            P.amm(nps[r, 0:258], wt[:, r], va, False, True)
            cps = qB()
            P.amm(cps[:, 0:258], V(kw)[r, :], va[r, :])
            yield
            P.ats(s_[r, 7:8], nps[r, 256:257], -1.0, ALU.mult)
            P.astt(s_[r, 6:7], nps[r, 256:257], s_[r, 2:3], s_[r, 7:8], ALU.max, ALU.max)
            P.astt(V(Cn), V(Cn), s_[:, 5:6], cps[:, 0:258], ALU.mult, ALU.add)
            yield
            P.op("dve", lambda e, o=_u(s_[r, 7:8]), i_=_u(s_[r, 6:7]): e.reciprocal(o, i_), _b(s_), _b(s_))
            yield
            P.ats(hb_[r, :], nps[r, 0:256], s_[r, 7:8], ALU.mult)
            mi = 1 - mi
            yield
        P.astore(h_out[pi, t0:t0 + 128, :], hb_)
        yield


def build_mlstm(nsc=NSC, nprob=2):
    P = AProg()
    dr = {
        "qT": P.dram("qT", [2, 128, TT], F32, "ExternalInput"),
        "kT": P.dram("kT", [2, 128, TT], F32, "ExternalInput"),
        "ktm": P.dram("ktm", [2, TT, 128], F32, "ExternalInput"),
        "vtm": P.dram("vtm", [2, TT, 256], F32, "ExternalInput"),
        "igc": P.dram("igc", [128, 2, NSC], F32, "ExternalInput"),
        "lfc": P.dram("lfc", [128, 2, NSC, 2], F32, "ExternalInput"),
    }
    mskd = P.dram("msk", [128, NMSK, 128], F32, "ExternalInput")
    h_out = P.dram("h", [2, TT, 256], F32, "ExternalOutput")
    P.alloc_psum_banks(8)
    msk = P.sbuf("msk_sb", [128, NMSK, 128])
    P.aload(V(msk), mskd[:, :, :])
    gens = []
    for pi in range(nprob):
        gens.append(mlstm_problem(P, pi, msk, dr, h_out, P.psum_banks[4 * pi:4 * pi + 4], nsc))
    run_round_robin(gens)
    return P.finish()


def colmajor_perm():
    rows = SEQ // GRID_W
    return np.arange(SEQ).reshape(rows, GRID_W).T.reshape(-1)


def mlstm_order(d):
    ci = np.arange(CTX)
    li = CTX + colmajor_perm()
    if d == 1:
        ci = ci[::-1]
        li = li[::-1]
    return np.concatenate([ci, li])


def stage_mlstm(nc, proj_res, conv_res):
    msk = make_masks()
    maps = []
    for j in range(NCORES):
        pT = proj_res[j]["pT"]
        r = conv_res[j]
        q = pT[CH_BQ * 128:(CH_BQ + 1) * 128]
        k = pT[CH_BK * 128:(CH_BK + 1) * 128]
        v = pT[CH_BV[0] * 128:(CH_BV[1] + 1) * 128]
        qT, kT, ktm, vtm = [], [], [], []
        igc = np.zeros((128, 2, NSC), np.float32)
        lfc = np.zeros((128, 2, NSC, 2), np.float32)
        for d in range(2):
            idx = mlstm_order(d)
            qT.append(q[:, idx])
            kk = k[:, idx]
            kT.append(kk)
            ktm.append(kk.T)
            vtm.append(v[:, idx].T)
            igc[:, d, :] = col_layout(r["ig"][d][idx])
            lc = col_layout(r["lf"][d][idx])
            lfc[:, d, :, 0] = lc
            lfc[:, d, :, 1] = lc
        maps.append({"qT": np.ascontiguousarray(np.stack(qT)), "kT": np.ascontiguousarray(np.stack(kT)),
                     "ktm": np.ascontiguousarray(np.stack(ktm)), "vtm": np.ascontiguousarray(np.stack(vtm)),
                     "igc": igc, "lfc": lfc, "msk": msk})
    return run(nc, maps)


O_ARR = ("oaf", "oab", "hbf", "hbb", "az", "bo", "bz", "ga", "gb", "x")


def build_merge(has_ctx, final, ntiles=None):
    P = AProg()
    NT = (64 if has_ctx else 0) + 2048
    tiles = ([(0, 64, 1)] if has_ctx else []) + [((64 if has_ctx else 0) + i * 512, 512, 0) for i in range(4)]
    if ntiles is not None:
        tiles = tiles[:ntiles]
    dr = {nm: P.dram(nm, [D, NT], F32, "ExternalInput").rearrange("(k p) t -> p k t", p=128) for nm in O_ARR}
    Wd = {nm: P.dram(nm, [D, D], F32, "ExternalInput").rearrange("(k p) c -> p k c", p=128)
          for nm in ("w_pa", "w_pb", "w_out")}
    prm = P.dram("prm", [128, 4, 8], F32, "ExternalInput")
    outT = P.dram("outT", [D, NT], F32, "ExternalOutput").rearrange("(k p) t -> p k t", p=128)
    P.alloc_psum_banks(8)
    ones = P.sbuf("ones", [128, 128])
    P.amemset(V(ones), 1.0)
    prms = P.sbuf("prms", [128, 4, 8])
    P.aload(V(prms), prm[:, :, :])
    W16 = {}
    stg = [P.sbuf(f"wst{i}", [128, 8, 256]) for i in range(2)]
    si = 0
    for nm in ("w_pa", "w_pb", "w_out"):
        W16[nm] = P.sbuf(f"{nm}16", [128, 8, D], BF16)
        for half in range(4):
            s_ = V(stg[si % 2])
            P.aload(s_, Wd[nm][:, :, half * 256:(half + 1) * 256])
            P.acopy(V(W16[nm])[:, :, half * 256:(half + 1) * 256], s_, eng=("dve" if si % 2 == 0 else "pool"))
            si += 1
    yaT = P.sbuf("yaT", [128, 8, 512], BF16)
    ybT = P.sbuf("ybT", [128, 8, 512], BF16)
    yT = P.sbuf("yT", [128, 8, 512], BF16)
    xn = P.sbuf("xn", [128, 8, 512])
    NB = 2
    bufs = {nm: [P.sbuf(f"in_{nm}{i}", [128, 512]) for i in range(NB)] for nm in O_ARR}
    cnt = {nm: 0 for nm in O_ARR}

    def load(nm, k, t0, n):
        b = V(bufs[nm][cnt[nm] % NB])[:, 0:n]
        cnt[nm] += 1
        P.aload(b, dr[nm][:, k, t0:t0 + n])
        return b

    tmpn = [0]

    def tmp(tag, dtype=F32):
        key = f"t_{tag}"
        if key not in bufs:
            bufs[key] = [P.sbuf(f"{key}{i}", [128, 512], dtype) for i in range(2)]
            cnt[key] = 0
        b = V(bufs[key][cnt[key] % 2])
        cnt[key] += 1
        return b

    epsb = P.sbuf("epsb", [128, 1])
    P.amemset(V(epsb), EPS)

    def rstd_from(ss_ps, n, dim):
        r = tmp("rstd")[:, 0:n]
        P.aact(r, ss_ps, AF.Ln, bias=V(epsb)[:, 0:1], scale=1.0 / dim)
        P.aact(r, r, AF.Exp, scale=-0.5)
        return r

    for (t0, n, is_ctx) in tiles:
        for k in range(8):
            f_ = load("oaf", k, t0, n)
            b_ = load("oab", k, t0, n)
            oa = tmp("oa")[:, 0:n]
            P.att(oa, f_, b_, ALU.add)
            sq = tmp("sq")[:, 0:n]
            P.aact(sq, oa, AF.Square)
            ss = V(P.bank())[:, 0:n]
            P.amm(ss, V(ones), sq)
            r = rstd_from(ss, n, 128)
            az = load("az", k, t0, n)
            sz = tmp("sz")[:, 0:n]
            P.aact(sz, az, AF.Silu)
            t1 = tmp("t1")[:, 0:n]
            P.att(t1, oa, r, ALU.mult)
            P.astt(V(yaT)[:, k, 0:n], sz, V(prms)[:, 3, 0:1], t1, ALU.mult, ALU.mult)
        for hd in range(4):
            hbs = []
            ss = V(P.bank())[:, 0:n]
            for c in range(2):
                k = hd * 2 + c
                f_ = load("hbf", k, t0, n)
                b_ = load("hbb", k, t0, n)
                hb = tmp(f"hb{c}")[:, 0:n]
                P.att(hb, f_, b_, ALU.add)
                sq = tmp("sq")[:, 0:n]
                P.aact(sq, hb, AF.Square)
                P.amm(ss, V(ones), sq, c == 0, c == 1)
                hbs.append(hb)
            r = rstd_from(ss, n, 256)
            for c in range(2):
                k = hd * 2 + c
                bo = load("bo", k, t0, n)
                bz = load("bz", k, t0, n)
                so = tmp("so")[:, 0:n]
                P.aact(so, bo, AF.Sigmoid)
                sz = tmp("sz")[:, 0:n]
                P.aact(sz, bz, AF.Silu)
                t4 = tmp("t4")[:, 0:n]
                P.att(t4, so, sz, ALU.mult, eng="pool")
                t1 = tmp("t1")[:, 0:n]
                P.att(t1, hbs[c], r, ALU.mult)
                P.astt(V(ybT)[:, k, 0:n], t4, V(prms)[:, 3, 1 + c:2 + c], t1, ALU.mult, ALU.mult)
        for m in range(8):
            za = V(P.bank())[:, 0:n]
            for k in range(8):
                P.amm(za, V(W16["w_pa"])[:, k, m * 128:(m + 1) * 128], V(yaT)[:, k, 0:n], k == 0, k == 7)
            zb = V(P.bank())[:, 0:n]
            for k in range(8):
                P.amm(zb, V(W16["w_pb"])[:, k, m * 128:(m + 1) * 128], V(ybT)[:, k, 0:n], k == 0, k == 7)
            ga = load("ga", m, t0, n)
            gb = load("gb", m, t0, n)
            sga = tmp("sga")[:, 0:n]
            P.aact(sga, ga, AF.Sigmoid)
            sgb = tmp("sgb")[:, 0:n]
            P.aact(sgb, gb, AF.Sigmoid)
            ta = tmp("ta")[:, 0:n]
            P.att(ta, za, sga, ALU.mult)
            tb = tmp("tb")[:, 0:n]
            P.att(tb, zb, sgb, ALU.mult)
            P.att(V(yT)[:, m, 0:n], ta, tb, ALU.add)
        for m in range(8):
            ops_ = V(P.bank())[:, 0:n]
            for k in range(8):
                P.amm(ops_, V(W16["w_out"])[:, k, m * 128:(m + 1) * 128], V(yT)[:, k, 0:n], k == 0, k == 7)
            xk = load("x", m, t0, n)
            P.astt(V(xn)[:, m, 0:n], ops_, V(prms)[:, 1 if is_ctx else 0, m:m + 1], xk, ALU.mult, ALU.add)
        if final:
            sqf = V(yT)
            ss = V(P.bank())[:, 0:n]
            for k in range(8):
                sq = tmp("sq")[:, 0:n]
                P.aact(sq, V(xn)[:, k, 0:n], AF.Square)
                P.amm(ss, V(ones), sq, k == 0, k == 7)
            r = rstd_from(ss, n, D)
            for k in range(8):
                t1 = tmp("t1")[:, 0:n]
                P.att(t1, V(xn)[:, k, 0:n], r, ALU.mult)
                o_ = tmp("fo")[:, 0:n]
                P.ats(o_, t1, V(prms)[:, 2, k:k + 1], ALU.mult)
                P.astore(outT[:, k, t0:t0 + n], o_)
        else:
            for k in range(8):
                P.astore(outT[:, k, t0:t0 + n], V(xn)[:, k, 0:n])
    return P.finish()


def core_tokens(qtr, has_ctx):
    lat = CTX + np.arange(qtr * 2048, (qtr + 1) * 2048)
    if has_ctx:
        return np.concatenate([np.arange(qtr * 64, (qtr + 1) * 64), lat])
    return lat


def stage_merge(nc, has_ctx, proj_res, delta_res, mlstm_res, xT_b, mod_l, inp, l, final):
    full = []
    for b in range(BATCH):
        A = {nm: np.empty((D, TT), np.float32) for nm in O_ARR if nm != "x"}
        for g in range(4):
            j = b * 4 + g
            pT = proj_res[j]["pT"]
            for hh in range(2):
                h = 2 * g + hh
                A["az"][h * 128:(h + 1) * 128] = pT[CH_AZ[hh] * 128:(CH_AZ[hh] + 1) * 128]
                for d, nm in ((0, "oaf"), (1, "oab")):
                    o = delta_res[j]["o"][d * 2 + hh]
                    nat = np.empty_like(o)
                    nat[prob_order(d)] = o
                    A[nm][h * 128:(h + 1) * 128] = nat.T
            for nm, ch in (("bo", CH_BO), ("bz", CH_BZ), ("ga", CH_GA), ("gb", CH_GB)):
                A[nm][g * 256:(g + 1) * 256] = pT[ch[0] * 128:(ch[1] + 1) * 128]
            for d, nm in ((0, "hbf"), (1, "hbb")):
                hv = mlstm_res[j]["h"][d]
                nat = np.empty_like(hv)
                nat[mlstm_order(d)] = hv
                A[nm][g * 256:(g + 1) * 256] = nat.T
        A["x"] = xT_b[b]
        full.append(A)
    maps = []
    for j in range(NCORES):
        b, qtr = divmod(j, 4)
        tok = core_tokens(qtr, has_ctx)
        mp = {nm: np.ascontiguousarray(full[b][nm][:, tok]) for nm in O_ARR}
        prm = np.zeros((128, 4, 8), np.float32)
        prm[:, 0, :] = feat_major(mod_l[b][2 * D:3 * D])
        prm[:, 1, :] = feat_major(mod_l[2][2 * D:3 * D])
        prm[:, 2, :] = feat_major(inp["final_g"])
        prm[:, 3, 0] = inp["norm_a_g"][l]
        prm[:, 3, 1] = inp["norm_b_g"][l][0:128]
        prm[:, 3, 2] = inp["norm_b_g"][l][128:256]
        mp["prm"] = prm
        for nm in ("w_pa", "w_pb", "w_out"):
            mp[nm] = np.ascontiguousarray(inp[nm][l])
        maps.append(mp)
    res = run(nc, maps)
    out = [np.array(xT_b[b]) for b in range(BATCH)]
    for j in range(NCORES):
        b, qtr = divmod(j, 4)
        tok = core_tokens(qtr, has_ctx)
        out[b][:, tok] = res[j]["outT"]
    return out


def kernel(x, c, ctx, c_ctx, ada_w, ada_b, norm_g, w_in, conv_w, a_log, dt_bias, norm_a_g,
           i_bias, f_bias, norm_b_g, w_pa, w_pb, w_out, final_g):
    inp = {k: np.asarray(v, dtype=np.float32) for k, v in dict(
        x=x, c=c, ctx=ctx, c_ctx=c_ctx, ada_w=ada_w, ada_b=ada_b, norm_g=norm_g, w_in=w_in,
        conv_w=conv_w, a_log=a_log, dt_bias=dt_bias, norm_a_g=norm_a_g, i_bias=i_bias,
        f_bias=f_bias, norm_b_g=norm_b_g, w_pa=w_pa, w_pb=w_pb, w_out=w_out, final_g=final_g).items()}
    mod = stage_mod(inp)
    xT_b = [np.ascontiguousarray(np.concatenate([inp["ctx"][b], inp["x"][b]], axis=0).T)
            for b in range(BATCH)]
    nc_proj = build_proj()
    nc_conv = build_conv()
    nc_delta = build_delta()
    nc_mlstm = build_mlstm()
    for l in range(2):
        proj_res = stage_proj(nc_proj, xT_b, inp["w_in"][l], inp["norm_g"][l], mod[l])
        conv_res = stage_conv(nc_conv, proj_res, inp["conv_w"][l], inp["a_log"][l], inp["dt_bias"][l],
                              inp["i_bias"][l], inp["f_bias"][l])
        delta_res = stage_delta(nc_delta, conv_res)
        mlstm_res = stage_mlstm(nc_mlstm, proj_res, conv_res)
        last = l == 1
        nc_merge = build_merge(not last, last)
        xT_b = stage_merge(nc_merge, not last, proj_res, delta_res, mlstm_res, xT_b, mod[l], inp, l, last)
        del proj_res, conv_res, delta_res, mlstm_res
    out = np.stack([np.ascontiguousarray(xT_b[b][:, CTX:].T) for b in range(BATCH)])
    return out.astype(np.float32)
```

```python
import numpy as np
from contextlib import ExitStack
import concourse.bass as bass
import concourse.mybir as mybir
from concourse.bass_utils import run_bass_kernel_spmd

F32 = mybir.dt.float32
BF16 = mybir.dt.bfloat16
F32R = mybir.dt.float32r
AF = mybir.ActivationFunctionType
ALU = mybir.AluOpType
AX = mybir.AxisListType

EP = 20000


class Buf:
    def __init__(self, prog, t, name, space):
        self.p = prog
        self.t = t
        self.name = name
        self.space = space
        self.w = None
        self.r = []
        self.sem_in = None
        self.n_in = 0
        self.sem_out = None
        self.n_out = 0
        self.acc = {}

    def __getitem__(self, idx):
        return self.t[idx]


class Prog:
    ENGS = ("pe", "act", "dve", "pool", "sp")

    def __init__(self):
        self.nc = bass.Bass("TRN2", target_bir_lowering=False)
        self.st = ExitStack()
        self.ops = {e: [] for e in self.ENGS}
        self.cnt = {e: 0 for e in self.ENGS}
        self.sems = {e: [] for e in self.ENGS}
        self.known = {e: {} for e in self.ENGS}
        self.nsem = 0
        self.out_deps = []
        self.psum_banks = []
        self.psum_i = 0

    def sem(self, name):
        self.nsem += 1
        return self.st.enter_context(self.nc.semaphore(name))

    def dram(self, name, shape, dtype, kind):
        return self.nc.dram_tensor(name, list(shape), dtype, kind=kind).ap()

    def dram_buf(self, name, shape, dtype, addr_space=None):
        if addr_space is not None:
            t = self.nc.dram_tensor(name, list(shape), dtype, kind="Internal", addr_space=addr_space).ap()
        else:
            t = self.nc.dram_tensor(name, list(shape), dtype, kind="Internal").ap()
        return Buf(self, t, name, "dram")

    def sbuf(self, name, shape, dtype=F32):
        t = self.st.enter_context(self.nc.sbuf_tensor(name, list(shape), dtype))
        return Buf(self, t, name, "sbuf")

    def psum(self, name, shape, dtype=F32):
        t = self.st.enter_context(self.nc.psum_tensor(name, list(shape), dtype))
        return Buf(self, t, name, "psum")

    def alloc_psum_banks(self, n=8):
        self.psum_banks = [self.psum(f"bank{i}", [128, 512], F32) for i in range(n)]

    def bank(self):
        b = self.psum_banks[self.psum_i % len(self.psum_banks)]
        self.psum_i += 1
        return b

    def _eng_sem(self, e, idx):
        ep = (idx - 1) // EP
        while len(self.sems[e]) <= ep:
            self.sems[e].append(self.sem(f"s_{e}_{len(self.sems[e])}"))
        return self.sems[e][ep], (idx - 1) % EP + 1

    def _collect(self, eng, reads, writes):
        deps = []
        for b in reads:
            if b.w is not None:
                deps.append(b.w)
        for b in writes:
            if b.w is not None:
                deps.append(b.w)
            deps.extend(b.r)
        for b in list(reads) + list(writes):
            if b.space == "psum":
                for e2, i2 in b.acc.items():
                    if e2 != eng:
                        deps.append(("eng", e2, i2, "p"))
        waits = {}
        for d in deps:
            if d[0] == "eng":
                _, e, idx, kind = d
                if e == eng:
                    continue
                s, v = self._eng_sem(e, idx)
            else:
                _, s, v = d
            key = id(s)
            if key not in waits or waits[key][1] < v:
                waits[key] = (s, v)
        if eng in ("act", "dve", "pool"):
            m = 0
            for b in reads:
                if b.w is not None and b.w[0] == "eng" and b.w[1] == eng:
                    m = max(m, b.w[2])
            if m:
                s, v = self._eng_sem(eng, m)
                key = id(s)
                if key not in waits or waits[key][1] < v:
                    waits[key] = (s, v)
        out = []
        kn = self.known[eng]
        for key, (s, v) in waits.items():
            if kn.get(key, 0) >= v:
                continue
            kn[key] = v
            out.append((s, v))
        return out

    def op(self, eng, fn, reads=(), writes=()):
        waits = self._collect(eng, reads, writes)
        self.cnt[eng] += 1
        idx = self.cnt[eng]
        s, v = self._eng_sem(eng, idx)
        self.ops[eng].append((waits, fn, (s, 1)))
        dep = ("eng", eng, idx, "c")
        for b in list(reads) + list(writes):
            if b.space == "psum":
                b.acc[eng] = idx
        for b in writes:
            b.w = dep
            b.r = []
        for b in reads:
            if b not in writes:
                b.r.append(dep)
        return idx

    def dma(self, out_ap, in_ap, dst=None, src=None, q="sp"):
        reads = [src] if src is not None else []
        writes = [dst] if dst is not None else []
        waits = self._collect(q, reads, writes)
        if dst is not None:
            if dst.sem_in is None:
                dst.sem_in = self.sem(f"di_{dst.name}")
            dst.n_in += 1
            s, v = dst.sem_in, 16 * dst.n_in
        elif src is not None:
            if src.sem_out is None:
                src.sem_out = self.sem(f"do_{src.name}")
            src.n_out += 1
            s, v = src.sem_out, 16 * src.n_out
        else:
            raise ValueError("dma needs a tracked side")
        dep = ("dma", s, v)
        self.ops[q].append((waits, lambda e, o=out_ap, i=in_ap: e.dma_start(out=o, in_=i), (s, 16)))
        if dst is not None:
            dst.w = dep
            dst.r = []
        if src is not None:
            src.r.append(dep)
        if dst is None:
            self.out_deps.append(dep)
        return dep

    def collective(self, kind, src, dst, replica_groups, q="pool"):
        waits = self._collect(q, [src], [dst])
        if dst.sem_in is None:
            dst.sem_in = self.sem(f"di_{dst.name}")
        dst.n_in += 1
        s, v = dst.sem_in, 16 * dst.n_in
        dep = ("dma", s, v)
        self.ops[q].append((waits, lambda e: e.collective_compute(
            kind, ALU.bypass, replica_groups=replica_groups, ins=[src.t[:]], outs=[dst.t[:]]), (s, 16)))
        dst.w = dep
        dst.r = []
        src.r.append(dep)
        return dep

    def mm(self, out, lhsT, rhs, start, stop, reads, writes):
        return self.op("pe", lambda e: e.matmul(out, lhsT, rhs, start=start, stop=stop),
                       reads, writes)

    def transpose(self, out, in_, ident, reads, writes):
        return self.op("pe", lambda e: e.transpose(out, in_, ident), reads, writes)

    def act(self, out, in_, func, reads, writes, bias=None, scale=None, accum_out=None, eng="act"):
        kw = {}
        if bias is not None:
            kw["bias"] = bias
        if scale is not None:
            kw["scale"] = scale
        if accum_out is not None:
            kw["accum_out"] = accum_out
        return self.op("act", lambda e: e.activation(out, in_, func, **kw), reads, writes)

    def tt(self, out, in0, in1, op, reads, writes, eng="dve"):
        return self.op(eng, lambda e: e.tensor_tensor(out, in0, in1, op), reads, writes)

    def ts(self, out, in0, s1, s2, op0, op1, reads, writes, eng="dve"):
        if op1 is None:
            return self.op(eng, lambda e: e.tensor_scalar(out, in0, s1, None, op0), reads, writes)
        return self.op(eng, lambda e: e.tensor_scalar(out, in0, s1, s2, op0, op1), reads, writes)

    def stt(self, out, in0, scalar, in1, op0, op1, reads, writes):
        return self.op("dve", lambda e: e.scalar_tensor_tensor(out, in0, scalar, in1, op0, op1),
                       reads, writes)

    def copy(self, out, in_, reads, writes, eng="dve"):
        if eng == "act":
            return self.op("act", lambda e: e.copy(out, in_), reads, writes)
        return self.op(eng, lambda e: e.tensor_copy(out, in_), reads, writes)

    def memset(self, out, val, writes, eng="dve"):
        return self.op(eng, lambda e: e.memset(out, val), (), writes)

    def finish(self):
        nc = self.nc
        fin_waits = {}
        for d in self.out_deps:
            _, s, v = d
            if id(s) not in fin_waits or fin_waits[id(s)][1] < v:
                fin_waits[id(s)] = (s, v)
        for e in self.ENGS:
            if e == "sp" or self.cnt[e] == 0:
                continue
            s, v = self._eng_sem(e, self.cnt[e])
            fin_waits[id(s)] = (s, v)
        engmap = {"pe": "tensor", "act": "scalar", "dve": "vector", "pool": "gpsimd", "sp": "sync"}
        with nc.Block() as block:
            for e in self.ENGS:
                ops = self.ops[e]
                last = (e == "sp")
                if not ops and not last:
                    continue

                def body(eng, ops=ops, last=last):
                    for waits, fn, (s, n) in ops:
                        for (ws, wv) in waits:
                            eng.wait_ge(ws, wv)
                        fn(eng).then_inc(s, n)
                    if last:
                        for (ws, wv) in fin_waits.values():
                            eng.wait_ge(ws, wv)

                getattr(block, engmap[e])(body)
        self.st.close()
        return nc


D = 1024
BATCH = 2
SEQ = 8192
CTX = 256
TT = CTX + SEQ
H_A, DK_A, DV_A = 8, 128, 128
H_B, DQK_B, DV_B = 4, 128, 256
W_A = 1024
W_B = 1024
GRID_W = 64
EPS = 1e-6
NCORES = 8
O_AQ, O_AK, O_AV, O_AZ = 0, 1024, 2048, 3072
O_ABETA, O_AALPHA = 4096, 4112
O_BQ, O_BK, O_BV, O_BO, O_BZ = 4128, 4640, 5152, 6176, 7200
O_BI, O_BF = 8224, 8232
O_GA, O_GB = 8240, 9264
NCH = 20
NGATE = 12


def core_cols(g):
    cols = []
    hA = (2 * g, 2 * g + 1)
    for base in (O_AQ, O_AK, O_AV, O_AZ):
        for h in hA:
            cols.extend(range(base + h * 128, base + (h + 1) * 128))
    for base in (O_BQ, O_BK):
        cols.extend(range(base + g * 128, base + (g + 1) * 128))
    for base in (O_BV, O_BO, O_BZ):
        cols.extend(range(base + g * 256, base + (g + 1) * 256))
    for base in (O_GA, O_GB):
        cols.extend(range(base + g * 256, base + (g + 1) * 256))
    assert len(cols) == NCH * 128
    gates = []
    for base in (O_ABETA, O_AALPHA):
        for d in range(2):
            for h in hA:
                gates.append(base + d * H_A + h)
    for base in (O_BI, O_BF):
        for d in range(2):
            gates.append(base + d * H_B + g)
    assert len(gates) == NGATE
    return cols, gates


CH_AQ, CH_AK, CH_AV, CH_AZ = (0, 1), (2, 3), (4, 5), (6, 7)
CH_BQ, CH_BK = 8, 9
CH_BV, CH_BO, CH_BZ = (10, 11), (12, 13), (14, 15)
CH_GA, CH_GB = (16, 17), (18, 19)


LAST_EXEC_NS = [None]


def run(nc, in_maps, trace=False):
    if trace:
        res = run_bass_kernel_spmd(nc, in_maps, core_ids=list(range(NCORES)), trace=True)
    else:
        res = run_bass_kernel_spmd(nc, in_maps, core_ids=list(range(NCORES)))
    LAST_EXEC_NS[0] = getattr(res, "exec_time_ns", None)
    return res.results


def build_mod():
    P = Prog()
    W = P.dram("W", [6, D, 128], F32, "ExternalInput")
    cc = P.dram("cc", [128, 8, 4], F32, "ExternalInput")
    bias = P.dram("bias", [128, 6], F32, "ExternalInput")
    out = P.dram("out", [6, 128, 4], F32, "ExternalOutput")
    P.alloc_psum_banks(2)
    ccs = P.sbuf("ccs", [128, 8, 4])
    scs = P.sbuf("scs", [128, 8, 4])
    bs = P.sbuf("bs", [128, 6])
    os_ = P.sbuf("os", [128, 6, 4])
    P.dma(ccs[:], cc[:, :, :], dst=ccs)
    P.dma(bs[:], bias[:, :], dst=bs)
    P.act(scs[:], ccs[:], AF.Silu, [ccs], [scs])
    wts = [P.sbuf(f"w{i}", [128, 8, 128]) for i in range(2)]
    for i in range(6):
        wt = wts[i % 2]
        P.dma(wt[:], W[i].rearrange("(k p) c -> p k c", p=128), dst=wt)
        ps = P.bank()
        for k in range(8):
            P.mm(ps[:, 0:4], wt[:, k, :], scs[:, k, :], k == 0, k == 7, [wt, scs], [ps])
        P.ts(os_[:, i, :], ps[:, 0:4], bs[:, i:i + 1], None, ALU.add, None, [ps, bs], [os_])
    P.dma(out.rearrange("i p n -> p i n"), os_[:], src=os_)
    return P.finish()


def stage_mod(inp):
    nc = build_mod()
    c4 = np.stack([inp["c"][0], inp["c"][1], inp["c_ctx"], inp["c_ctx"]], axis=-1)
    cc = np.ascontiguousarray(c4.reshape(8, 128, 4).transpose(1, 0, 2))
    maps = []
    for j in range(NCORES):
        Ws, bs = [], []
        for i in range(6):
            job = j * 6 + i
            l, fc = divmod(job, 24)
            Ws.append(inp["ada_w"][l][:, fc * 128:(fc + 1) * 128])
            bs.append(inp["ada_b"][l][fc * 128:(fc + 1) * 128])
        maps.append({"W": np.ascontiguousarray(np.stack(Ws)), "cc": cc,
                     "bias": np.ascontiguousarray(np.stack(bs, axis=1))})
    res = run(nc, maps)
    mod = np.zeros((2, 3, 3 * D), np.float32)
    for j in range(NCORES):
        o = res[j]["out"]
        for i in range(6):
            job = j * 6 + i
            l, fc = divmod(job, 24)
            for n in range(3):
                mod[l, n, fc * 128:(fc + 1) * 128] = o[i, :, n]
    return mod


P_TILES = [(0, CTX)] + [(CTX + i * 512, CTX + (i + 1) * 512) for i in range(SEQ // 512)]


def build_proj(ntiles=None):
    P = Prog()
    xT = P.dram("xT", [D, TT], F32, "ExternalInput")
    Wg = P.dram("Wg", [D, NCH * 128 + 16], F32, "ExternalInput")
    prm = P.dram("prm", [128, 5, 8], F32, "ExternalInput")
    pT = P.dram("pT", [NCH * 128, TT], F32, "ExternalOutput")
    gT = P.dram("gT", [16, TT], F32, "ExternalOutput")
    P.alloc_psum_banks(8)
    NW = NCH * 128 + 16
    W16 = P.sbuf("W16", [128, 8, NW], BF16)
    ones = P.sbuf("ones", [128, 128])
    P.memset(ones[:], 1.0, [ones])
    prms = P.sbuf("prms", [128, 5, 8])
    P.dma(prms[:], prm[:, :, :], dst=prms)
    epsb = P.sbuf("epsb", [128, 1])
    P.memset(epsb[:], EPS, [epsb])
    GS = P.sbuf("GS", [128, 4, 8])
    P.stt(GS[:, 0, :], prms[:, 1, :], 1.0, prms[:, 0, :], ALU.add, ALU.mult, [prms], [GS])
    P.copy(GS[:, 1, :], prms[:, 2, :], [prms], [GS])
    P.stt(GS[:, 2, :], prms[:, 3, :], 1.0, prms[:, 0, :], ALU.add, ALU.mult, [prms], [GS])
    P.copy(GS[:, 3, :], prms[:, 4, :], [prms], [GS])
    Wv = Wg.rearrange("(k p) c -> p k c", p=128)
    stg = [P.sbuf(f"wst{i}", [128, 8, 512]) for i in range(2)]
    c0 = 0
    i = 0
    while c0 < NW:
        c1 = min(NW, c0 + 512)
        s = stg[i % 2]
        P.dma(s[:, :, 0:c1 - c0], Wv[:, :, c0:c1], dst=s)
        P.copy(W16[:, :, c0:c1], s[:, :, 0:c1 - c0], [s], [W16], eng=("dve" if i % 2 == 0 else "pool"))
        c0 = c1
        i += 1
    xv = xT.rearrange("(k p) t -> p k t", p=128)
    xts = [P.sbuf(f"xt{i}", [128, 8, 512]) for i in range(2)]
    sq = P.sbuf("sq", [128, 8, 512])
    hTs = [P.sbuf(f"hT{i}", [128, 8, 512], BF16) for i in range(2)]
    rstd = P.sbuf("rstd", [128, 512])
    tmp = [P.sbuf(f"tmp{i}", [128, 512]) for i in range(2)]
    evs = [P.sbuf(f"ev{i}", [128, 512]) for i in range(4)]
    gev = P.sbuf("gev", [16, 512])
    nev = 0
    for ti, (t0, t1) in enumerate(P_TILES[:ntiles] if ntiles else P_TILES):
        n = t1 - t0
        xt = xts[ti % 2]
        hT = hTs[ti % 2]
        P.dma(xt[:, :, 0:n], xv[:, :, t0:t1], dst=xt)
        P.act(sq[:, :, 0:n], xt[:, :, 0:n], AF.Square, [xt], [sq])
        ss = P.bank()
        for k in range(8):
            P.mm(ss[:, 0:n], ones[:], sq[:, k, 0:n], k == 0, k == 7, [ones, sq], [ss])
        P.act(rstd[:, 0:n], ss[:, 0:n], AF.Ln, [ss, epsb], [rstd], bias=epsb[:, 0:1], scale=1.0 / D)
        P.act(rstd[:, 0:n], rstd[:, 0:n], AF.Exp, [rstd], [rstd], scale=-0.5)
        gi = 2 if ti == 0 else 0
        for k in range(8):
            tm = tmp[k % 2]
            P.tt(tm[:, 0:n], xt[:, k, 0:n], rstd[:, 0:n], ALU.mult, [xt, rstd], [tm])
            P.act(hT[:, k, 0:n], tm[:, 0:n], AF.Identity, [tm, GS], [hT],
                  scale=GS[:, gi, k:k + 1], bias=GS[:, gi + 1, k:k + 1])
        for c in range(NCH):
            ps = P.bank()
            for k in range(8):
                P.mm(ps[:, 0:n], W16[:, k, c * 128:(c + 1) * 128], hT[:, k, 0:n], k == 0, k == 7,
                     [W16, hT], [ps])
            ev = evs[nev % 4]
            P.copy(ev[:, 0:n], ps[:, 0:n], [ps], [ev], eng=("act" if nev % 2 else "dve"))
            nev += 1
            P.dma(pT[c * 128:(c + 1) * 128, t0:t1], ev[:, 0:n], src=ev)
        ps = P.bank()
        for k in range(8):
            P.mm(ps[0:16, 0:n], W16[:, k, NCH * 128:NCH * 128 + 16], hT[:, k, 0:n], k == 0, k == 7,
                 [W16, hT], [ps])
        P.copy(gev[:, 0:n], ps[0:16, 0:n], [ps], [gev])
        P.dma(gT[:, t0:t1], gev[:, 0:n], src=gev)
    return P.finish()


def feat_major(v):
    return np.ascontiguousarray(v.reshape(8, 128).T)


def stage_proj(nc, xT_b, w_in_l, norm_g_l, mod_l):
    maps = []
    for j in range(NCORES):
        b, g = divmod(j, 4)
        cols, gates = core_cols(g)
        Wg = np.zeros((D, NCH * 128 + 16), np.float32)
        Wg[:, :NCH * 128] = w_in_l[:, cols]
        Wg[:, NCH * 128:NCH * 128 + NGATE] = w_in_l[:, gates]
        shift, scale = mod_l[b][0:D], mod_l[b][D:2 * D]
        shift_c, scale_c = mod_l[2][0:D], mod_l[2][D:2 * D]
        prm = np.stack([feat_major(norm_g_l), feat_major(scale), feat_major(shift),
                        feat_major(scale_c), feat_major(shift_c)], axis=1)
        maps.append({"xT": xT_b[b], "Wg": Wg, "prm": np.ascontiguousarray(prm)})
    return run(nc, maps)


CPAD = (CTX + 4) + (SEQ + 4)
C_TILES = [(0, 0, CTX)] + [(CTX + 4, i * 512, 512) for i in range(SEQ // 512)]


def build_conv():
    P = Prog()
    pre = P.dram("pre", [6, 128, CPAD], F32, "ExternalInput")
    cw = P.dram("cw", [128, 6, 5], F32, "ExternalInput")
    gin = {nm: P.dram(nm, [r, TT], F32, "ExternalInput") for nm, r in
           (("betaT", 4), ("alphaT", 4), ("iT", 2), ("fT", 2))}
    gprm = P.dram("gprm", [4, 4], F32, "ExternalInput")
    qkv = P.dram("qkv", [6, 128, TT], F32, "ExternalOutput")
    gout = {nm: P.dram(nm, [r, TT], F32, "ExternalOutput") for nm, r in
            (("beta", 4), ("g", 4), ("ig", 2), ("lf", 2))}
    P.alloc_psum_banks(4)
    ones = P.sbuf("ones", [128, 128])
    P.memset(ones[:], 1.0, [ones])
    cws = P.sbuf("cws", [128, 6, 5])
    P.dma(cws[:], cw[:, :, :], dst=cws)
    epsb = P.sbuf("epsb", [128, 1])
    P.memset(epsb[:], EPS, [epsb])
    gp = P.sbuf("gp", [4, 4])
    P.dma(gp[:], gprm[:, :], dst=gp)
    gp2 = P.sbuf("gp2", [4, 4])
    P.act(gp2[:, 0:1], gp[:, 0:1], AF.Exp, [gp], [gp2])
    P.ts(gp2[:, 1:2], gp2[:, 0:1], -1.0, None, ALU.mult, None, [gp2], [gp2])
    P.ts(gp2[0:2, 2:3], gp[0:2, 3:4], -1.0, None, ALU.mult, None, [gp], [gp2])
    g1 = P.sbuf("g1", [4, TT])
    g2 = P.sbuf("g2", [4, TT])
    P.dma(g1[:], gin["betaT"][:, :], dst=g1)
    P.act(g2[:], g1[:], AF.Sigmoid, [g1], [g2])
    P.dma(gout["beta"][:, :], g2[:], src=g2)
    P.dma(g1[:], gin["alphaT"][:, :], dst=g1)
    P.act(g1[:], g1[:], AF.Exp, [g1, gp], [g1], bias=gp[:, 1:2])
    P.act(g1[:], g1[:], AF.Ln, [g1], [g1], bias=1.0)
    P.ts(g2[:], g1[:], gp2[:, 1:2], None, ALU.mult, None, [g1, gp2], [g2])
    P.dma(gout["g"][:, :], g2[:], src=g2)
    P.dma(g1[0:2, :], gin["iT"][:, :], dst=g1)
    P.ts(g2[0:2, :], g1[0:2, :], gp[0:2, 2:3], None, ALU.add, None, [g1, gp], [g2])
    P.dma(gout["ig"][:, :], g2[0:2, :], src=g2)
    P.dma(g1[0:2, :], gin["fT"][:, :], dst=g1)
    P.act(g1[0:2, :], g1[0:2, :], AF.Exp, [g1, gp2], [g1], bias=gp2[0:2, 2:3], scale=-1.0)
    P.act(g1[0:2, :], g1[0:2, :], AF.Ln, [g1], [g1], bias=1.0)
    P.ts(g2[0:2, :], g1[0:2, :], -1.0, None, ALU.mult, None, [g1], [g2])
    P.dma(gout["lf"][:, :], g2[0:2, :], src=g2)
    NBUF = 2
    xin = [P.sbuf(f"xin{i}", [128, 516]) for i in range(6 * NBUF)]
    acc = [P.sbuf(f"acc{i}", [128, 512]) for i in range(3)]
    ys = [P.sbuf(f"y{i}", [128, 512]) for i in range(6 * NBUF)]
    sqs = [P.sbuf(f"sq{i}", [128, 512]) for i in range(4)]
    rs = [P.sbuf(f"r{i}", [128, 512]) for i in range(4)]
    outs = [P.sbuf(f"o{i}", [128, 512]) for i in range(6)]
    it = 0
    io = 0
    for ti, (off, t0, n) in enumerate(C_TILES):
        tok0 = (0 if off == 0 else CTX) + t0
        for c in range(6):
            xi = xin[(ti % NBUF) * 6 + c]
            a = acc[it % 3]
            y = ys[(ti % NBUF) * 6 + c]
            P.dma(xi[:, 0:n + 4], pre[c, :, off + t0:off + t0 + n + 4], dst=xi)
            P.ts(a[:, 0:n], xi[:, 0:n], cws[:, c, 0:1], None, ALU.mult, None, [xi, cws], [a])
            for s in range(1, 5):
                P.stt(a[:, 0:n], xi[:, s:s + n], cws[:, c, s:s + 1], a[:, 0:n], ALU.mult, ALU.add,
                      [xi, cws, a], [a])
            if c < 4:
                P.act(y[:, 0:n], a[:, 0:n], AF.Silu, [a], [y])
            else:
                o = outs[io % 6]
                io += 1
                P.act(o[:, 0:n], a[:, 0:n], AF.Silu, [a], [o])
                P.dma(qkv[c, :, tok0:tok0 + n], o[:, 0:n], src=o)
            it += 1
        for c in range(4):
            y = ys[(ti % NBUF) * 6 + c]
            sq = sqs[c]
            r = rs[c]
            o = outs[io % 6]
            io += 1
            P.tt(sq[:, 0:n], y[:, 0:n], y[:, 0:n], ALU.mult, [y], [sq], eng="pool")
            ps = P.bank()
            P.mm(ps[:, 0:n], ones[:], sq[:, 0:n], True, True, [ones, sq], [ps])
            P.act(r[:, 0:n], ps[:, 0:n], AF.Ln, [ps, epsb], [r], bias=epsb[:, 0:1])
            P.act(r[:, 0:n], r[:, 0:n], AF.Exp, [r], [r], scale=-0.5)
            sc = (DK_A ** -0.5) if c < 2 else 1.0
            P.stt(o[:, 0:n], y[:, 0:n], sc, r[:, 0:n], ALU.mult, ALU.mult, [y, r], [o])
            P.dma(qkv[c, :, tok0:tok0 + n], o[:, 0:n], src=o)
    return P.finish()


def pad_seq(a):
    z = np.zeros(a.shape[:-1] + (2,), a.dtype)
    return np.ascontiguousarray(np.concatenate([z, a[..., :CTX], z, z, a[..., CTX:], z], axis=-1))


def stage_conv(nc, proj_res, conv_w_l, a_log_l, dt_bias_l, i_bias_l, f_bias_l):
    maps = []
    for j in range(NCORES):
        b, g = divmod(j, 4)
        pT = proj_res[j]["pT"]
        gT = proj_res[j]["gT"]
        pre = pad_seq(pT[0:768].reshape(6, 128, TT))
        hA = (2 * g, 2 * g + 1)
        chans = []
        for base in (0, 1024, 2048):
            for h in hA:
                chans.append(np.arange(base + h * 128, base + (h + 1) * 128))
        cw = np.stack([conv_w_l[:, ch].T for ch in chans], axis=1)
        gprm = np.zeros((4, 4), np.float32)
        k = 0
        for d in range(2):
            for h in hA:
                gprm[k, 0] = a_log_l[d, h]
                gprm[k, 1] = dt_bias_l[d, h]
                k += 1
        for d in range(2):
            gprm[d, 2] = i_bias_l[d, g]
            gprm[d, 3] = f_bias_l[d, g]
        maps.append({"pre": pre, "cw": np.ascontiguousarray(cw), "gprm": gprm,
                     "betaT": np.ascontiguousarray(gT[0:4]), "alphaT": np.ascontiguousarray(gT[4:8]),
                     "iT": np.ascontiguousarray(gT[8:10]), "fT": np.ascontiguousarray(gT[10:12])})
    return run(nc, maps)


class View:
    def __init__(self, buf, ap):
        self.buf = buf
        self.ap = ap

    def __getitem__(self, idx):
        return View(self.buf, self.ap[idx])


def V(buf, *idx):
    if not idx:
        return View(buf, buf.t[:])
    return View(buf, buf.t[idx if len(idx) > 1 else idx[0]])


def _u(x):
    return x.ap if isinstance(x, View) else x


def _b(*xs):
    out = []
    for x in xs:
        if isinstance(x, View) and x.buf not in out:
            out.append(x.buf)
    return out


class AProg(Prog):
    def quarters(self, nbanks=8):
        self.alloc_psum_banks(nbanks)
        self.qtiles = []
        for b in self.psum_banks:
            for q in range(4):
                self.qtiles.append(Buf(self, b.t[:, q * 128:(q + 1) * 128], f"{b.name}q{q}", "psum"))
        self.qi = 0

    def q(self):
        t = self.qtiles[self.qi % len(self.qtiles)]
        self.qi += 1
        return V(t)

    def amm(self, out, lhsT, rhs, start=True, stop=True):
        return self.mm(_u(out), _u(lhsT), _u(rhs), start, stop, _b(lhsT, rhs), _b(out))

    def atr(self, out, in_, ident):
        return self.transpose(_u(out), _u(in_), _u(ident), _b(in_, ident), _b(out))

    def aact(self, out, in_, func, bias=None, scale=None):
        return self.act(_u(out), _u(in_), func, _b(in_, bias, scale), _b(out),
                        bias=_u(bias) if bias is not None else None,
                        scale=_u(scale) if scale is not None else None)

    def att(self, out, in0, in1, op, eng="dve"):
        return self.tt(_u(out), _u(in0), _u(in1), op, _b(in0, in1), _b(out), eng=eng)

    def ats(self, out, in0, s1, op0, s2=None, op1=None, eng="dve"):
        return self.ts(_u(out), _u(in0), _u(s1), _u(s2), op0, op1, _b(in0, s1, s2), _b(out), eng=eng)

    def astt(self, out, in0, scalar, in1, op0, op1):
        return self.stt(_u(out), _u(in0), _u(scalar), _u(in1), op0, op1, _b(in0, scalar, in1), _b(out))

    def acopy(self, out, in_, eng="dve"):
        return self.copy(_u(out), _u(in_), _b(in_), _b(out), eng=eng)

    def amemset(self, out, val, eng="dve"):
        return self.memset(_u(out), val, _b(out), eng=eng)

    def aload(self, dst, src_ap, q="sp"):
        return self.dma(_u(dst), src_ap, dst=dst.buf, q=q)

    def astore(self, dst_ap, src, q="sp"):
        return self.dma(dst_ap, _u(src), src=src.buf, q=q)


def run_round_robin(gens):
    gens = list(gens)
    while gens:
        nxt = []
        for g in gens:
            try:
                next(g)
                nxt.append(g)
            except StopIteration:
                pass
        gens = nxt


BIG = 30000.0
NSC = TT // 128


def make_masks():
    i = np.arange(128)
    same = (i[:, None] // 64) == (i[None, :] // 64)
    U = ((i[:, None] <= i[None, :]) & same).astype(np.float32)
    BD = same.astype(np.float32)
    PL = np.where((i[None, :] < i[:, None]) & same, 0.0, BIG).astype(np.float32)
    NU = np.where((i[:, None] <= i[None, :]) & same, 0.0, -BIG).astype(np.float32)
    ID = np.eye(128, dtype=np.float32)
    NL = np.where((i[None, :] <= i[:, None]) & same, 0.0, -BIG).astype(np.float32)
    ON = np.ones((128, 128), np.float32)
    return np.ascontiguousarray(np.stack([U, BD, PL, NU, ID, NL, ON, -U], axis=1))


M_U, M_BD, M_PL, M_NU, M_ID, M_NL, M_ONES, M_NEGU = range(8)
NMSK = 8


def delta_problem(P, pi, msk, dr, S, o_out, banks, nsc=NSC):
    f = lambda nm, shape=(128, 128): P.sbuf(f"{nm}_{pi}", list(shape))
    fr = lambda nm, shape=(128, 128): P.sbuf(f"{nm}_{pi}", list(shape), F32R)
    qA = lambda: V(banks[0])[:, 0:128]
    qB = (lambda: V(banks[1])[:, 0:128]) if banks[1] is not banks[0] else (lambda: V(banks[0])[:, 128:256])
    qTs = [f(f"qT{i}") for i in range(2)]
    kTs = [f(f"kT{i}") for i in range(2)]
    kts = [f(f"ktm{i}") for i in range(2)]
    vts = [f(f"vtm{i}") for i in range(2)]
    bcol = f("bcol", (128, NSC))
    nbcol = f("nbcol", (128, NSC))
    gcol = f("gcol", (128, NSC, 2))
    cols = [f(f"cols{i}", (128, 8)) for i in range(2)]
    gbc = f("gbc"); RL = f("RL"); RU = f("RU"); dL = f("dL"); dT = f("dT"); eg = [f("eg0"), f("eg1")]
    BT = f("BT"); Bm = f("Bm")
    Pa = [fr("Pa0"), fr("Pa1")]; PaT = [fr("PaT0"), fr("PaT1")]
    X = [fr("X0"), fr("X1")]
    qTr = fr("qTr"); kTr = fr("kTr")
    kbg = fr("kbg"); vb = fr("vb"); kdec = [f("kdec0"), f("kdec1")]
    wT = f("wT"); u = f("u"); qkT = [f("qkT0"), f("qkT1")]; qdT = [f("qdT0"), f("qdT1")]
    vnew = f("vnew")
    osb = [f("osb0"), f("osb1")]
    P.amemset(V(vnew), 0.0)
    P.aload(V(bcol), dr["bcol"][:, pi, :])
    P.aload(V(gcol), dr["gcol"][:, pi, :, :])
    P.ats(V(nbcol), V(bcol), -1.0, ALU.mult)
    U = V(msk, slice(None), M_U, slice(None))
    BDm = V(msk, slice(None), M_BD, slice(None))
    PLm = V(msk, slice(None), M_PL, slice(None))
    NUm = V(msk, slice(None), M_NU, slice(None))
    ID = V(msk, slice(None), M_ID, slice(None))
    ONESm = V(msk, slice(None), M_ONES, slice(None))
    yield
    for sc in range(nsc):
        t0 = sc * 128
        qT = V(qTs[sc % 2]); kT = V(kTs[sc % 2]); ktm = V(kts[sc % 2]); vtm = V(vts[sc % 2])
        P.aload(qT, dr["qT"][pi, :, t0:t0 + 128])
        P.aload(kT, dr["kT"][pi, :, t0:t0 + 128])
        P.aload(ktm, dr["ktm"][pi, t0:t0 + 128, :])
        P.aload(vtm, dr["vtm"][pi, t0:t0 + 128, :])
        cl = V(cols[sc % 2])
        gps = qA()
        P.amm(gps[:, 0:2], U, V(gcol)[:, sc, :])
        P.amm(gps[:, 2:4], BDm, V(gcol)[:, sc, :])
        P.ats(V(gbc), ONESm, V(gcol)[:, sc, 0:1], ALU.mult, eng="pool")
        yield
        grow = qB()
        P.amm(grow, V(gbc), U)
        P.acopy(cl[:, 0:1], gps[:, 0:1])
        P.ats(cl[:, 1:2], gps[:, 0:1], -1.0, ALU.mult)
        yield
        P.aact(cl[:, 2:3], gps[:, 0:1], AF.Exp)
        P.aact(cl[:, 3:4], gps[:, 2:3], AF.Exp, bias=cl[:, 1:2])
        P.att(V(RL), grow, PLm, ALU.add)
        P.att(V(RU), grow, NUm, ALU.add)
        egr = V(eg[sc % 2])
        P.aact(egr, grow, AF.Exp)
        yield
        P.aact(V(dL), V(RL), AF.Exp, bias=cl[:, 0:1], scale=-1.0)
        P.aact(V(dT), V(RU), AF.Exp, bias=cl[:, 1:2])
        P.att(cl[:, 4:5], cl[:, 2:3], V(bcol)[:, sc:sc + 1], ALU.mult)
        P.acopy(V(kTr), kT, eng="act")
        P.acopy(V(qTr), qT, eng="act")
        kk = qA()
        P.amm(kk, V(kTr), V(kTr))
        qk = qB()
        P.amm(qk, V(kTr), V(qTr))
        yield
        P.astt(V(BT), kk, V(nbcol)[:, sc:sc + 1], V(dL), ALU.mult, ALU.mult)
        P.aact(V(kbg), ktm, AF.Identity, scale=cl[:, 4:5])
        P.aact(V(vb), vtm, AF.Identity, scale=V(bcol)[:, sc:sc + 1])
        kd = V(kdec[sc % 2])
        P.ats(kd, ktm, cl[:, 3:4], ALU.mult, eng="pool")
        qkTs = V(qkT[sc % 2])
        P.att(qkTs, qk, V(dT), ALU.mult)
        qd = V(qdT[sc % 2])
        P.att(qd, qT, egr, ALU.mult, eng="pool")
        yield
        bps = qA()
        P.atr(bps, V(BT), ID)
        yield
        P.acopy(V(Bm), bps, eng="act")
        P.att(V(X[0]), bps, ID, ALU.add)
        yield
        cur, curT = V(Bm), V(BT)
        xi = 0
        for lev in range(5):
            last = lev == 4
            p2T = qA()
            P.amm(p2T, cur, curT)
            if not last:
                p2 = qB()
                P.amm(p2, curT, cur)
            yield
            nT = V(PaT[lev % 2])
            P.acopy(nT, p2T, eng="act")
            if not last:
                n_ = V(Pa[lev % 2])
                P.acopy(n_, p2)
            yield
            xp = qA()
            P.amm(xp, nT, V(X[xi]))
            yield
            P.att(V(X[1 - xi]), xp, View(X[xi], X[xi].t[:].bitcast(F32)), ALU.add)
            xi = 1 - xi
            if not last:
                cur, curT = n_, nT
            yield
        TinvT = V(X[xi])
        wps = qA()
        P.amm(wps, V(kbg), TinvT)
        ups = qB()
        P.amm(ups, TinvT, V(vb))
        yield
        P.acopy(V(wT), wps, eng="act")
        P.acopy(V(u), ups)
        yield
        ob = V(osb[sc % 2])
        for c in range(2):
            r = slice(c * 64, c * 64 + 64)
            vps = qA()
            P.amm(vps[r, :], V(wT)[:, r], V(S))
            yield
            P.att(V(vnew)[r, :], V(u)[r, :], vps[r, :], ALU.subtract)
            yield
            ops_ = qB()
            P.amm(ops_[r, :], qd[:, r], V(S), True, False)
            P.amm(ops_[r, :], qkTs[:, r], V(vnew), False, True)
            sps = qA()
            P.amm(sps, kd[r, :], V(vnew)[r, :])
            yield
            P.astt(V(S), V(S), egr[:, c * 64 + 63:c * 64 + 64], sps, ALU.mult, ALU.add)
            P.acopy(ob[r, :], ops_[r, :], eng="act")
            yield
        P.astore(o_out[pi, t0:t0 + 128, :], ob)
        yield


def build_delta(nsc=NSC, nprob=4):
    P = AProg()
    dr = {
        "qT": P.dram("qT", [4, 128, TT], F32, "ExternalInput"),
        "kT": P.dram("kT", [4, 128, TT], F32, "ExternalInput"),
        "ktm": P.dram("ktm", [4, TT, 128], F32, "ExternalInput"),
        "vtm": P.dram("vtm", [4, TT, 128], F32, "ExternalInput"),
        "bcol": P.dram("bcol", [128, 4, NSC], F32, "ExternalInput"),
        "gcol": P.dram("gcol", [128, 4, NSC, 2], F32, "ExternalInput"),
    }
    mskd = P.dram("msk", [128, NMSK, 128], F32, "ExternalInput")
    o_out = P.dram("o", [4, TT, 128], F32, "ExternalOutput")
    P.alloc_psum_banks(8)
    msk = P.sbuf("msk_sb", [128, NMSK, 128])
    P.aload(V(msk), mskd[:, :, :])
    gens = []
    for pi in range(nprob):
        S = P.sbuf(f"S_{pi}", [128, 128])
        P.amemset(V(S), 0.0)
        gens.append(delta_problem(P, pi, msk, dr, S, o_out,
                                  (P.psum_banks[2 * pi], P.psum_banks[2 * pi + 1]), nsc))
    run_round_robin(gens)
    return P.finish()


def prob_order(d):
    ci = np.arange(CTX)
    li = CTX + np.arange(SEQ)
    if d == 1:
        ci = ci[::-1]
        li = li[::-1]
    return np.concatenate([ci, li])


def col_layout(v):
    return np.ascontiguousarray(v.reshape(NSC, 128).T)


def stage_delta(nc, conv_res):
    msk = make_masks()
    maps = []
    for j in range(NCORES):
        r = conv_res[j]
        qT, kT, ktm, vtm = [], [], [], []
        bcol = np.zeros((128, 4, NSC), np.float32)
        gcol = np.zeros((128, 4, NSC, 2), np.float32)
        for pi in range(4):
            d, hh = divmod(pi, 2)
            idx = prob_order(d)
            qT.append(r["qkv"][0 + hh][:, idx])
            kk = r["qkv"][2 + hh][:, idx]
            kT.append(kk)
            ktm.append(kk.T)
            vtm.append(r["qkv"][4 + hh][:, idx].T)
            bcol[:, pi, :] = col_layout(r["beta"][pi][idx])
            gc = col_layout(r["g"][pi][idx])
            gcol[:, pi, :, 0] = gc
            gcol[:, pi, :, 1] = gc
        maps.append({"qT": np.ascontiguousarray(np.stack(qT)), "kT": np.ascontiguousarray(np.stack(kT)),
                     "ktm": np.ascontiguousarray(np.stack(ktm)), "vtm": np.ascontiguousarray(np.stack(vtm)),
                     "bcol": bcol, "gcol": gcol, "msk": msk})
    if nc is None:
        return maps
    return run(nc, maps)


KSC = DQK_B ** -0.5
NEGBIG = -1.0e30


def mlstm_problem(P, pi, msk, dr, h_out, banks, nsc=NSC):
    f = lambda nm, shape=(128, 128): P.sbuf(f"{nm}_m{pi}", list(shape))
    if len(banks) == 4:
        qA = lambda: V(banks[0])
        qB = lambda: V(banks[1])
        qC = lambda: V(banks[2])
        qD = lambda: V(banks[3])
        qN = qA
        qCp = qB
    else:
        qA = lambda: V(banks[0])[:, 0:128]
        qB = lambda: V(banks[0])[:, 128:256]
        qC = lambda: V(banks[0])[:, 256:384]
        qD = lambda: V(banks[0])[:, 384:512]
        qN = lambda: V(banks[1])
        qCp = lambda: V(banks[0])
    qTs = [f(f"qT{i}") for i in range(2)]
    kTs = [f(f"kT{i}") for i in range(2)]
    kts = [f(f"ktm{i}") for i in range(2)]
    vas = [f(f"vaug{i}", (128, 258)) for i in range(2)]
    igc = f("igc", (128, NSC))
    lfc = f("lfc", (128, NSC, 2))
    cols = [f(f"cols{i}", (128, 8)) for i in range(2)]
    lfb = f("lfb"); igb = f("igb"); brs = [f("brs0"), f("brs1")]; Rm = f("Rm")
    pmx = [f("pmx0"), f("pmx1")]; dmr = f("dmr")
    mtr = f("mtr", (128, 64)); Er = f("Er", (128, 64)); inr = f("inr", (128, 64)); tmpw = f("tmpw", (128, 64))
    W1 = f("W1", (128, 64))
    qd = f("qd")
    wts = [f("wts0"), f("wts1")]
    kw = f("kw")
    Cn = f("Cn", (128, 258))
    ms = [f("m0", (128, 1)), f("m1", (128, 1))]
    sm = [f("sm0", (128, 8)), f("sm1", (128, 8))]
    hsb = [f("hsb0", (128, 256)), f("hsb1", (128, 256))]
    P.amemset(V(Cn), 0.0)
    P.amemset(V(ms[0]), 0.0)
    for w_ in wts:
        P.amemset(V(w_), 0.0, eng="pool")
    for va in vas:
        P.amemset(V(va)[:, 256:258], 1.0, eng="pool")
    P.aload(V(igc), dr["igc"][:, pi, :])
    P.aload(V(lfc), dr["lfc"][:, pi, :, :])
    U = V(msk, slice(None), M_U, slice(None))
    NUm = V(msk, slice(None), M_NU, slice(None))
    ID = V(msk, slice(None), M_ID, slice(None))
    NLm = V(msk, slice(None), M_NL, slice(None))
    ONESm = V(msk, slice(None), M_ONES, slice(None))
    NEGU = V(msk, slice(None), M_NEGU, slice(None))
    mi = 0
    yield
    for sc in range(nsc):
        t0 = sc * 128
        qT = V(qTs[sc % 2]); kT = V(kTs[sc % 2]); ktm = V(kts[sc % 2]); va = V(vas[sc % 2])
        P.aload(qT, dr["qT"][pi, :, t0:t0 + 128])
        P.aload(kT, dr["kT"][pi, :, t0:t0 + 128])
        P.aload(ktm, dr["ktm"][pi, t0:t0 + 128, :])
        P.aload(va[:, 0:256], dr["vtm"][pi, t0:t0 + 128, :])
        cl = V(cols[sc % 2])
        bps = qA()
        P.amm(bps[:, 0:2], U, V(lfc)[:, sc, :])
        P.ats(V(lfb), ONESm, V(lfc)[:, sc, 0:1], ALU.mult, eng="pool")
        P.ats(V(igb), ONESm, V(igc)[:, sc:sc + 1], ALU.mult, eng="pool")
        yield
        brow = qB()
        P.amm(brow[:, 0:128], V(lfb), U)
        rrow = qC()
        P.amm(rrow[:, 0:128], V(igb), ID, True, False)
        P.amm(rrow[:, 0:128], V(lfb), NEGU, False, True)
        qk = qD()
        P.amm(qk[:, 0:128], kT, qT)
        P.acopy(cl[:, 0:1], bps[:, 0:1])
        yield
        br = V(brs[sc % 2])
        P.acopy(br, brow[:, 0:128], eng="act")
        P.att(cl[:, 1:2], V(igc)[:, sc:sc + 1], cl[:, 0:1], ALU.subtract)
        P.att(V(Rm), rrow[:, 0:128], NLm, ALU.add)
        pm = V(pmx[sc % 2])
        for c in range(2):
            r = slice(c * 64, c * 64 + 64)
            P.op("dve", lambda e, o=_u(pm[:, r]), d0=_u(ONESm[:, r]), d1=_u(rrow[:, r]):
                 e.tensor_tensor_scan(o, d0, d1, NEGBIG, ALU.mult, ALU.max),
                 _b(ONESm, rrow), _b(pm))
        yield
        P.op("dve", lambda e, o=_u(cl[:, 2:3]), i_=_u(V(Rm)): e.tensor_reduce(o, i_, AX.X, ALU.max),
             _b(V(Rm)), _b(cl))
        P.att(V(dmr), br, pm, ALU.add)
        yield
        P.att(cl[:, 3:4], cl[:, 0:1], cl[:, 2:3], ALU.add)
        yield
        hb_ = V(hsb[sc % 2])
        for c in range(2):
            r = slice(c * 64, c * 64 + 64)
            last = c * 64 + 63
            m_old = V(ms[mi]); m_new = V(ms[1 - mi])
            s_ = V(sm[c])
            P.astt(V(mtr), br[:, r], m_old[:, 0:1], V(dmr)[:, r], ALU.add, ALU.max)
            P.astt(s_[:, 0:1], cl[:, 0:1], m_old[:, 0:1], cl[:, 3:4], ALU.add, ALU.max)
            P.att(s_[:, 1:2], m_old[:, 0:1], pm[:, last:last + 1], ALU.max)
            yield
            P.att(V(Er), br[:, r], V(mtr), ALU.subtract)
            P.aact(s_[:, 2:3], s_[:, 0:1], AF.Exp, scale=-1.0)
            P.att(m_new[:, 0:1], s_[:, 1:2], br[:, last:last + 1], ALU.add)
            yield
            P.aact(V(inr), V(Er), AF.Exp, bias=m_old[:, 0:1])
            P.att(V(tmpw)[r, :], V(Er)[r, :], NUm[r, r], ALU.add, eng="pool")
            P.att(s_[:, 3:4], br[:, last:last + 1], m_new[:, 0:1], ALU.subtract)
            yield
            P.att(V(qd)[:, r], qT[:, r], V(inr), ALU.mult, eng="pool")
            P.aact(V(W1)[r, :], V(tmpw)[r, :], AF.Exp, bias=cl[r, 1:2])
            P.aact(s_[:, 4:5], cl[:, 1:2], AF.Exp, bias=s_[:, 3:4])
            P.aact(s_[:, 5:6], s_[:, 3:4], AF.Exp, bias=m_old[:, 0:1])
            yield
            wt = V(wts[c])
            P.astt(wt[r, r], qk[r, r], KSC, V(W1)[r, :], ALU.mult, ALU.mult)
            P.ats(V(kw)[r, :], ktm[r, :], s_[r, 4:5], ALU.mult, KSC, ALU.mult)
            yield
            nps = qN()
            P.amm(nps[r, 0:258], V(qd)[:, r], V(Cn), True, False)
            P.amm(nps[r, 0:258], wt[:, r], va, False, True)
            cps = qCp()
            P.amm(cps[:, 0:258], V(kw)[r, :], va[r, :])
            yield
            P.ats(s_[r, 7:8], nps[r, 256:257], -1.0, ALU.mult)
            P.astt(s_[r, 6:7], nps[r, 256:257], s_[r, 2:3], s_[r, 7:8], ALU.max, ALU.max)
            P.astt(V(Cn), V(Cn), s_[:, 5:6], cps[:, 0:258], ALU.mult, ALU.add)
            yield
            P.op("dve", lambda e, o=_u(s_[r, 7:8]), i_=_u(s_[r, 6:7]): e.reciprocal(o, i_), _b(s_), _b(s_))
            yield
            P.ats(hb_[r, :], nps[r, 0:256], s_[r, 7:8], ALU.mult)
            mi = 1 - mi
            yield
        P.astore(h_out[pi, t0:t0 + 128, :], hb_)
        yield


def mlstm_drams(P, pre=""):
    dr = {
        "qT": P.dram(pre + "qT", [2, 128, TT], F32, "ExternalInput"),
        "kT": P.dram(pre + "kT", [2, 128, TT], F32, "ExternalInput"),
        "ktm": P.dram(pre + "ktm", [2, TT, 128], F32, "ExternalInput"),
        "vtm": P.dram(pre + "vtm", [2, TT, 256], F32, "ExternalInput"),
        "igc": P.dram(pre + "igc", [128, 2, NSC], F32, "ExternalInput"),
        "lfc": P.dram(pre + "lfc", [128, 2, NSC, 2], F32, "ExternalInput"),
    }
    h_out = P.dram("h", [2, TT, 256], F32, "ExternalOutput")
    return dr, h_out


def build_scan(nsc=NSC):
    P = AProg()
    dr = {
        "qT": P.dram("qT", [4, 128, TT], F32, "ExternalInput"),
        "kT": P.dram("kT", [4, 128, TT], F32, "ExternalInput"),
        "ktm": P.dram("ktm", [4, TT, 128], F32, "ExternalInput"),
        "vtm": P.dram("vtm", [4, TT, 128], F32, "ExternalInput"),
        "bcol": P.dram("bcol", [128, 4, NSC], F32, "ExternalInput"),
        "gcol": P.dram("gcol", [128, 4, NSC, 2], F32, "ExternalInput"),
    }
    mdr, h_out = mlstm_drams(P, "m_")
    mskd = P.dram("msk", [128, NMSK, 128], F32, "ExternalInput")
    o_out = P.dram("o", [4, TT, 128], F32, "ExternalOutput")
    P.alloc_psum_banks(8)
    msk = P.sbuf("msk_sb", [128, NMSK, 128])
    P.aload(V(msk), mskd[:, :, :])
    gens = []
    for pi in range(4):
        S = P.sbuf(f"S_{pi}", [128, 128])
        P.amemset(V(S), 0.0)
        gens.append(delta_problem(P, pi, msk, dr, S, o_out, (P.psum_banks[pi], P.psum_banks[pi]), nsc))
    for pi in range(2):
        gens.append(mlstm_problem(P, pi, msk, mdr, h_out, P.psum_banks[4 + 2 * pi:6 + 2 * pi], nsc))
    run_round_robin(gens)
    return P.finish()


def build_mlstm(nsc=NSC, nprob=2):
    P = AProg()
    dr, h_out = mlstm_drams(P)
    mskd = P.dram("msk", [128, NMSK, 128], F32, "ExternalInput")
    P.alloc_psum_banks(8)
    msk = P.sbuf("msk_sb", [128, NMSK, 128])
    P.aload(V(msk), mskd[:, :, :])
    gens = []
    for pi in range(nprob):
        gens.append(mlstm_problem(P, pi, msk, dr, h_out, P.psum_banks[4 * pi:4 * pi + 4], nsc))
    run_round_robin(gens)
    return P.finish()


def colmajor_perm():
    rows = SEQ // GRID_W
    return np.arange(SEQ).reshape(rows, GRID_W).T.reshape(-1)


def mlstm_order(d):
    ci = np.arange(CTX)
    li = CTX + colmajor_perm()
    if d == 1:
        ci = ci[::-1]
        li = li[::-1]
    return np.concatenate([ci, li])


def stage_mlstm(nc, proj_res, conv_res):
    msk = make_masks()
    maps = []
    for j in range(NCORES):
        pT = proj_res[j]["pT"]
        r = conv_res[j]
        q = pT[CH_BQ * 128:(CH_BQ + 1) * 128]
        k = pT[CH_BK * 128:(CH_BK + 1) * 128]
        v = pT[CH_BV[0] * 128:(CH_BV[1] + 1) * 128]
        qT, kT, ktm, vtm = [], [], [], []
        igc = np.zeros((128, 2, NSC), np.float32)
        lfc = np.zeros((128, 2, NSC, 2), np.float32)
        for d in range(2):
            idx = mlstm_order(d)
            qT.append(q[:, idx])
            kk = k[:, idx]
            kT.append(kk)
            ktm.append(kk.T)
            vtm.append(v[:, idx].T)
            igc[:, d, :] = col_layout(r["ig"][d][idx])
            lc = col_layout(r["lf"][d][idx])
            lfc[:, d, :, 0] = lc
            lfc[:, d, :, 1] = lc
        maps.append({"qT": np.ascontiguousarray(np.stack(qT)), "kT": np.ascontiguousarray(np.stack(kT)),
                     "ktm": np.ascontiguousarray(np.stack(ktm)), "vtm": np.ascontiguousarray(np.stack(vtm)),
                     "igc": igc, "lfc": lfc, "msk": msk})
    if nc is None:
        return maps
    return run(nc, maps)


def stage_scan(nc, proj_res, conv_res):
    dm = stage_delta(None, conv_res)
    mm_ = stage_mlstm(None, proj_res, conv_res)
    maps = []
    for j in range(NCORES):
        mp = dict(dm[j])
        for k, v in mm_[j].items():
            if k != "msk":
                mp["m_" + k] = v
        maps.append(mp)
    res = run(nc, maps)
    return [{"o": r["o"]} for r in res], [{"h": r["h"]} for r in res]


O_ARR = ("oaf", "oab", "hbf", "hbb", "az", "bo", "bz", "ga", "gb", "x")


def build_merge(has_ctx, final, ntiles=None):
    P = AProg()
    NT = (64 if has_ctx else 0) + 2048
    tiles = ([(0, 64, 1)] if has_ctx else []) + [((64 if has_ctx else 0) + i * 512, 512, 0) for i in range(4)]
    if ntiles is not None:
        tiles = tiles[:ntiles]
    dr = {nm: P.dram(nm, [D, NT], F32, "ExternalInput").rearrange("(k p) t -> p k t", p=128) for nm in O_ARR}
    Wd = {nm: P.dram(nm, [D, D], F32, "ExternalInput").rearrange("(k p) c -> p k c", p=128)
          for nm in ("w_pa", "w_pb", "w_out")}
    prm = P.dram("prm", [128, 4, 8], F32, "ExternalInput")
    outT = P.dram("outT", [D, NT], F32, "ExternalOutput").rearrange("(k p) t -> p k t", p=128)
    P.alloc_psum_banks(8)
    ones = P.sbuf("ones", [128, 128])
    P.amemset(V(ones), 1.0)
    prms = P.sbuf("prms", [128, 4, 8])
    P.aload(V(prms), prm[:, :, :])
    W16 = {}
    stg = [P.sbuf(f"wst{i}", [128, 8, 256]) for i in range(2)]
    si = 0
    for nm in ("w_pa", "w_pb", "w_out"):
        W16[nm] = P.sbuf(f"{nm}16", [128, 8, D], BF16)
        for half in range(4):
            s_ = V(stg[si % 2])
            P.aload(s_, Wd[nm][:, :, half * 256:(half + 1) * 256])
            P.acopy(V(W16[nm])[:, :, half * 256:(half + 1) * 256], s_, eng=("dve" if si % 2 == 0 else "pool"))
            si += 1
    yaT = P.sbuf("yaT", [128, 8, 512], BF16)
    ybT = P.sbuf("ybT", [128, 8, 512], BF16)
    yT = P.sbuf("yT", [128, 8, 512], BF16)
    xn = P.sbuf("xn", [128, 8, 512])
    NB = 2
    bufs = {nm: [P.sbuf(f"in_{nm}{i}", [128, 512]) for i in range(NB)] for nm in O_ARR}
    cnt = {nm: 0 for nm in O_ARR}

    def load(nm, k, t0, n):
        b = V(bufs[nm][cnt[nm] % NB])[:, 0:n]
        cnt[nm] += 1
        P.aload(b, dr[nm][:, k, t0:t0 + n])
        return b

    tmpn = [0]

    def tmp(tag, dtype=F32):
        key = f"t_{tag}"
        if key not in bufs:
            bufs[key] = [P.sbuf(f"{key}{i}", [128, 512], dtype) for i in range(2)]
            cnt[key] = 0
        b = V(bufs[key][cnt[key] % 2])
        cnt[key] += 1
        return b

    epsb = P.sbuf("epsb", [128, 1])
    P.amemset(V(epsb), EPS)

    def rstd_from(ss_ps, n, dim):
        r = tmp("rstd")[:, 0:n]
        P.aact(r, ss_ps, AF.Ln, bias=V(epsb)[:, 0:1], scale=1.0 / dim)
        P.aact(r, r, AF.Exp, scale=-0.5)
        return r

    for (t0, n, is_ctx) in tiles:
        for k in range(8):
            f_ = load("oaf", k, t0, n)
            b_ = load("oab", k, t0, n)
            oa = tmp("oa")[:, 0:n]
            P.att(oa, f_, b_, ALU.add)
            sq = tmp("sq")[:, 0:n]
            P.aact(sq, oa, AF.Square)
            ss = V(P.bank())[:, 0:n]
            P.amm(ss, V(ones), sq)
            r = rstd_from(ss, n, 128)
            az = load("az", k, t0, n)
            sz = tmp("sz")[:, 0:n]
            P.aact(sz, az, AF.Silu)
            t1 = tmp("t1")[:, 0:n]
            P.att(t1, oa, r, ALU.mult)
            P.astt(V(yaT)[:, k, 0:n], sz, V(prms)[:, 3, 0:1], t1, ALU.mult, ALU.mult)
        for hd in range(4):
            hbs = []
            ss = V(P.bank())[:, 0:n]
            for c in range(2):
                k = hd * 2 + c
                f_ = load("hbf", k, t0, n)
                b_ = load("hbb", k, t0, n)
                hb = tmp(f"hb{c}")[:, 0:n]
                P.att(hb, f_, b_, ALU.add)
                sq = tmp("sq")[:, 0:n]
                P.aact(sq, hb, AF.Square)
                P.amm(ss, V(ones), sq, c == 0, c == 1)
                hbs.append(hb)
            r = rstd_from(ss, n, 256)
            for c in range(2):
                k = hd * 2 + c
                bo = load("bo", k, t0, n)
                bz = load("bz", k, t0, n)
                so = tmp("so")[:, 0:n]
                P.aact(so, bo, AF.Sigmoid)
                sz = tmp("sz")[:, 0:n]
                P.aact(sz, bz, AF.Silu)
                t4 = tmp("t4")[:, 0:n]
                P.att(t4, so, sz, ALU.mult, eng="pool")
                t1 = tmp("t1")[:, 0:n]
                P.att(t1, hbs[c], r, ALU.mult)
                P.astt(V(ybT)[:, k, 0:n], t4, V(prms)[:, 3, 1 + c:2 + c], t1, ALU.mult, ALU.mult)
        for m in range(8):
            za = V(P.bank())[:, 0:n]
            for k in range(8):
                P.amm(za, V(W16["w_pa"])[:, k, m * 128:(m + 1) * 128], V(yaT)[:, k, 0:n], k == 0, k == 7)
            zb = V(P.bank())[:, 0:n]
            for k in range(8):
                P.amm(zb, V(W16["w_pb"])[:, k, m * 128:(m + 1) * 128], V(ybT)[:, k, 0:n], k == 0, k == 7)
            ga = load("ga", m, t0, n)
            gb = load("gb", m, t0, n)
            sga = tmp("sga")[:, 0:n]
            P.aact(sga, ga, AF.Sigmoid)
            sgb = tmp("sgb")[:, 0:n]
            P.aact(sgb, gb, AF.Sigmoid)
            ta = tmp("ta")[:, 0:n]
            P.att(ta, za, sga, ALU.mult)
            tb = tmp("tb")[:, 0:n]
            P.att(tb, zb, sgb, ALU.mult)
            P.att(V(yT)[:, m, 0:n], ta, tb, ALU.add)
        for m in range(8):
            ops_ = V(P.bank())[:, 0:n]
            for k in range(8):
                P.amm(ops_, V(W16["w_out"])[:, k, m * 128:(m + 1) * 128], V(yT)[:, k, 0:n], k == 0, k == 7)
            xk = load("x", m, t0, n)
            P.astt(V(xn)[:, m, 0:n], ops_, V(prms)[:, 1 if is_ctx else 0, m:m + 1], xk, ALU.mult, ALU.add)
        if final:
            sqf = V(yT)
            ss = V(P.bank())[:, 0:n]
            for k in range(8):
                sq = tmp("sq")[:, 0:n]
                P.aact(sq, V(xn)[:, k, 0:n], AF.Square)
                P.amm(ss, V(ones), sq, k == 0, k == 7)
            r = rstd_from(ss, n, D)
            for k in range(8):
                t1 = tmp("t1")[:, 0:n]
                P.att(t1, V(xn)[:, k, 0:n], r, ALU.mult)
                o_ = tmp("fo")[:, 0:n]
                P.ats(o_, t1, V(prms)[:, 2, k:k + 1], ALU.mult)
                P.astore(outT[:, k, t0:t0 + n], o_)
        else:
            for k in range(8):
                P.astore(outT[:, k, t0:t0 + n], V(xn)[:, k, 0:n])
    return P.finish()


def core_tokens(qtr, has_ctx):
    lat = CTX + np.arange(qtr * 2048, (qtr + 1) * 2048)
    if has_ctx:
        return np.concatenate([np.arange(qtr * 64, (qtr + 1) * 64), lat])
    return lat


def stage_merge(nc, has_ctx, proj_res, delta_res, mlstm_res, xT_b, mod_l, inp, l, final):
    full = []
    for b in range(BATCH):
        A = {nm: np.empty((D, TT), np.float32) for nm in O_ARR if nm != "x"}
        for g in range(4):
            j = b * 4 + g
            pT = proj_res[j]["pT"]
            for hh in range(2):
                h = 2 * g + hh
                A["az"][h * 128:(h + 1) * 128] = pT[CH_AZ[hh] * 128:(CH_AZ[hh] + 1) * 128]
                for d, nm in ((0, "oaf"), (1, "oab")):
                    o = delta_res[j]["o"][d * 2 + hh]
                    nat = np.empty_like(o)
                    nat[prob_order(d)] = o
                    A[nm][h * 128:(h + 1) * 128] = nat.T
            for nm, ch in (("bo", CH_BO), ("bz", CH_BZ), ("ga", CH_GA), ("gb", CH_GB)):
                A[nm][g * 256:(g + 1) * 256] = pT[ch[0] * 128:(ch[1] + 1) * 128]
            for d, nm in ((0, "hbf"), (1, "hbb")):
                hv = mlstm_res[j]["h"][d]
                nat = np.empty_like(hv)
                nat[mlstm_order(d)] = hv
                A[nm][g * 256:(g + 1) * 256] = nat.T
        A["x"] = xT_b[b]
        full.append(A)
    maps = []
    for j in range(NCORES):
        b, qtr = divmod(j, 4)
        tok = core_tokens(qtr, has_ctx)
        mp = {nm: np.ascontiguousarray(full[b][nm][:, tok]) for nm in O_ARR}
        prm = np.zeros((128, 4, 8), np.float32)
        prm[:, 0, :] = feat_major(mod_l[b][2 * D:3 * D])
        prm[:, 1, :] = feat_major(mod_l[2][2 * D:3 * D])
        prm[:, 2, :] = feat_major(inp["final_g"])
        prm[:, 3, 0] = inp["norm_a_g"][l]
        prm[:, 3, 1] = inp["norm_b_g"][l][0:128]
        prm[:, 3, 2] = inp["norm_b_g"][l][128:256]
        mp["prm"] = prm
        for nm in ("w_pa", "w_pb", "w_out"):
            mp[nm] = np.ascontiguousarray(inp[nm][l])
        maps.append(mp)
    res = run(nc, maps)
    out = [np.array(xT_b[b]) for b in range(BATCH)]
    for j in range(NCORES):
        b, qtr = divmod(j, 4)
        tok = core_tokens(qtr, has_ctx)
        out[b][:, tok] = res[j]["outT"]
    return out


def kernel(x, c, ctx, c_ctx, ada_w, ada_b, norm_g, w_in, conv_w, a_log, dt_bias, norm_a_g,
           i_bias, f_bias, norm_b_g, w_pa, w_pb, w_out, final_g):
    inp = {k: np.asarray(v, dtype=np.float32) for k, v in dict(
        x=x, c=c, ctx=ctx, c_ctx=c_ctx, ada_w=ada_w, ada_b=ada_b, norm_g=norm_g, w_in=w_in,
        conv_w=conv_w, a_log=a_log, dt_bias=dt_bias, norm_a_g=norm_a_g, i_bias=i_bias,
        f_bias=f_bias, norm_b_g=norm_b_g, w_pa=w_pa, w_pb=w_pb, w_out=w_out, final_g=final_g).items()}
    mod = stage_mod(inp)
    xT_b = [np.ascontiguousarray(np.concatenate([inp["ctx"][b], inp["x"][b]], axis=0).T)
            for b in range(BATCH)]
    nc_proj = build_proj()
    nc_conv = build_conv()
    nc_scan = build_scan()
    for l in range(2):
        proj_res = stage_proj(nc_proj, xT_b, inp["w_in"][l], inp["norm_g"][l], mod[l])
        conv_res = stage_conv(nc_conv, proj_res, inp["conv_w"][l], inp["a_log"][l], inp["dt_bias"][l],
                              inp["i_bias"][l], inp["f_bias"][l])
        delta_res, mlstm_res = stage_scan(nc_scan, proj_res, conv_res)
        last = l == 1
        nc_merge = build_merge(not last, last)
        xT_b = stage_merge(nc_merge, not last, proj_res, delta_res, mlstm_res, xT_b, mod[l], inp, l, last)
        del proj_res, conv_res, delta_res, mlstm_res
    out = np.stack([np.ascontiguousarray(xT_b[b][:, CTX:].T) for b in range(BATCH)])
    return out.astype(np.float32)
```

```python
import numpy as np
from contextlib import ExitStack
import concourse.bass as bass
import concourse.mybir as mybir
from concourse.bass_utils import run_bass_kernel_spmd

F32 = mybir.dt.float32
BF16 = mybir.dt.bfloat16
F32R = mybir.dt.float32r
AF = mybir.ActivationFunctionType
ALU = mybir.AluOpType
AX = mybir.AxisListType

EP = 20000


class Buf:
    def __init__(self, prog, t, name, space):
        self.p = prog
        self.t = t
        self.name = name
        self.space = space
        self.w = None
        self.r = []
        self.sem_in = None
        self.n_in = 0
        self.sem_out = None
        self.n_out = 0
        self.acc = {}

    def __getitem__(self, idx):
        return self.t[idx]


class Prog:
    ENGS = ("pe", "act", "dve", "pool", "sp")

    def __init__(self):
        self.nc = bass.Bass("TRN2", target_bir_lowering=False)
        self.st = ExitStack()
        self.ops = {e: [] for e in self.ENGS}
        self.cnt = {e: 0 for e in self.ENGS}
        self.sems = {e: [] for e in self.ENGS}
        self.known = {e: {} for e in self.ENGS}
        self.nsem = 0
        self.out_deps = []
        self.psum_banks = []
        self.psum_i = 0

    def sem(self, name):
        self.nsem += 1
        return self.st.enter_context(self.nc.semaphore(name))

    def dram(self, name, shape, dtype, kind):
        return self.nc.dram_tensor(name, list(shape), dtype, kind=kind).ap()

    def dram_buf(self, name, shape, dtype, addr_space=None):
        if addr_space is not None:
            t = self.nc.dram_tensor(name, list(shape), dtype, kind="Internal", addr_space=addr_space).ap()
        else:
            t = self.nc.dram_tensor(name, list(shape), dtype, kind="Internal").ap()
        return Buf(self, t, name, "dram")

    def sbuf(self, name, shape, dtype=F32):
        t = self.st.enter_context(self.nc.sbuf_tensor(name, list(shape), dtype))
        return Buf(self, t, name, "sbuf")

    def psum(self, name, shape, dtype=F32):
        t = self.st.enter_context(self.nc.psum_tensor(name, list(shape), dtype))
        return Buf(self, t, name, "psum")

    def alloc_psum_banks(self, n=8):
        self.psum_banks = [self.psum(f"bank{i}", [128, 512], F32) for i in range(n)]

    def bank(self):
        b = self.psum_banks[self.psum_i % len(self.psum_banks)]
        self.psum_i += 1
        return b

    def _eng_sem(self, e, idx):
        ep = (idx - 1) // EP
        while len(self.sems[e]) <= ep:
            self.sems[e].append(self.sem(f"s_{e}_{len(self.sems[e])}"))
        return self.sems[e][ep], (idx - 1) % EP + 1

    def _collect(self, eng, reads, writes):
        deps = []
        for b in reads:
            if b.w is not None:
                deps.append(b.w)
        for b in writes:
            if b.w is not None:
                deps.append(b.w)
            deps.extend(b.r)
        for b in list(reads) + list(writes):
            if b.space == "psum":
                for e2, i2 in b.acc.items():
                    if e2 != eng:
                        deps.append(("eng", e2, i2, "p"))
        waits = {}
        for d in deps:
            if d[0] == "eng":
                _, e, idx, kind = d
                if e == eng:
                    continue
                s, v = self._eng_sem(e, idx)
            else:
                _, s, v = d
            key = id(s)
            if key not in waits or waits[key][1] < v:
                waits[key] = (s, v)
        if eng in ("act", "dve", "pool"):
            m = 0
            for b in reads:
                if b.w is not None and b.w[0] == "eng" and b.w[1] == eng:
                    m = max(m, b.w[2])
            if m:
                s, v = self._eng_sem(eng, m)
                key = id(s)
                if key not in waits or waits[key][1] < v:
                    waits[key] = (s, v)
        out = []
        kn = self.known[eng]
        for key, (s, v) in waits.items():
            if kn.get(key, 0) >= v:
                continue
            kn[key] = v
            out.append((s, v))
        return out

    def op(self, eng, fn, reads=(), writes=()):
        waits = self._collect(eng, reads, writes)
        self.cnt[eng] += 1
        idx = self.cnt[eng]
        s, v = self._eng_sem(eng, idx)
        self.ops[eng].append((waits, fn, (s, 1)))
        dep = ("eng", eng, idx, "c")
        for b in list(reads) + list(writes):
            if b.space == "psum":
                b.acc[eng] = idx
        for b in writes:
            b.w = dep
            b.r = []
        for b in reads:
            if b not in writes:
                b.r.append(dep)
        return idx

    def dma(self, out_ap, in_ap, dst=None, src=None, q="sp"):
        reads = [src] if src is not None else []
        writes = [dst] if dst is not None else []
        waits = self._collect(q, reads, writes)
        if dst is not None:
            if dst.sem_in is None:
                dst.sem_in = self.sem(f"di_{dst.name}")
            dst.n_in += 1
            s, v = dst.sem_in, 16 * dst.n_in
        elif src is not None:
            if src.sem_out is None:
                src.sem_out = self.sem(f"do_{src.name}")
            src.n_out += 1
            s, v = src.sem_out, 16 * src.n_out
        else:
            raise ValueError("dma needs a tracked side")
        dep = ("dma", s, v)
        self.ops[q].append((waits, lambda e, o=out_ap, i=in_ap: e.dma_start(out=o, in_=i), (s, 16)))
        if dst is not None:
            dst.w = dep
            dst.r = []
        if src is not None:
            src.r.append(dep)
        if dst is None:
            self.out_deps.append(dep)
        return dep

    def collective(self, kind, src, dst, replica_groups, q="pool"):
        waits = self._collect(q, [src], [dst])
        if dst.sem_in is None:
            dst.sem_in = self.sem(f"di_{dst.name}")
        dst.n_in += 1
        s, v = dst.sem_in, 16 * dst.n_in
        dep = ("dma", s, v)
        self.ops[q].append((waits, lambda e: e.collective_compute(
            kind, ALU.bypass, replica_groups=replica_groups, ins=[src.t[:]], outs=[dst.t[:]]), (s, 16)))
        dst.w = dep
        dst.r = []
        src.r.append(dep)
        return dep

    def mm(self, out, lhsT, rhs, start, stop, reads, writes):
        return self.op("pe", lambda e: e.matmul(out, lhsT, rhs, start=start, stop=stop),
                       reads, writes)

    def transpose(self, out, in_, ident, reads, writes):
        return self.op("pe", lambda e: e.transpose(out, in_, ident), reads, writes)

    def act(self, out, in_, func, reads, writes, bias=None, scale=None, accum_out=None, eng="act"):
        kw = {}
        if bias is not None:
            kw["bias"] = bias
        if scale is not None:
            kw["scale"] = scale
        if accum_out is not None:
            kw["accum_out"] = accum_out
        return self.op("act", lambda e: e.activation(out, in_, func, **kw), reads, writes)

    def tt(self, out, in0, in1, op, reads, writes, eng="dve"):
        return self.op(eng, lambda e: e.tensor_tensor(out, in0, in1, op), reads, writes)

    def ts(self, out, in0, s1, s2, op0, op1, reads, writes, eng="dve"):
        if op1 is None:
            return self.op(eng, lambda e: e.tensor_scalar(out, in0, s1, None, op0), reads, writes)
        return self.op(eng, lambda e: e.tensor_scalar(out, in0, s1, s2, op0, op1), reads, writes)

    def stt(self, out, in0, scalar, in1, op0, op1, reads, writes):
        return self.op("dve", lambda e: e.scalar_tensor_tensor(out, in0, scalar, in1, op0, op1),
                       reads, writes)

    def copy(self, out, in_, reads, writes, eng="dve"):
        if eng == "act":
            return self.op("act", lambda e: e.copy(out, in_), reads, writes)
        return self.op(eng, lambda e: e.tensor_copy(out, in_), reads, writes)

    def memset(self, out, val, writes, eng="dve"):
        return self.op(eng, lambda e: e.memset(out, val), (), writes)

    def finish(self):
        nc = self.nc
        fin_waits = {}
        for d in self.out_deps:
            _, s, v = d
            if id(s) not in fin_waits or fin_waits[id(s)][1] < v:
                fin_waits[id(s)] = (s, v)
        for e in self.ENGS:
            if e == "sp" or self.cnt[e] == 0:
                continue
            s, v = self._eng_sem(e, self.cnt[e])
            fin_waits[id(s)] = (s, v)
        engmap = {"pe": "tensor", "act": "scalar", "dve": "vector", "pool": "gpsimd", "sp": "sync"}
        with nc.Block() as block:
            for e in self.ENGS:
                ops = self.ops[e]
                last = (e == "sp")
                if not ops and not last:
                    continue

                def body(eng, ops=ops, last=last):
                    for waits, fn, (s, n) in ops:
                        for (ws, wv) in waits:
                            eng.wait_ge(ws, wv)
                        fn(eng).then_inc(s, n)
                    if last:
                        for (ws, wv) in fin_waits.values():
                            eng.wait_ge(ws, wv)

                getattr(block, engmap[e])(body)
        self.st.close()
        return nc


D = 1024
BATCH = 2
SEQ = 8192
CTX = 256
TT = CTX + SEQ
H_A, DK_A, DV_A = 8, 128, 128
H_B, DQK_B, DV_B = 4, 128, 256
W_A = 1024
W_B = 1024
GRID_W = 64
EPS = 1e-6
NCORES = 8
O_AQ, O_AK, O_AV, O_AZ = 0, 1024, 2048, 3072
O_ABETA, O_AALPHA = 4096, 4112
O_BQ, O_BK, O_BV, O_BO, O_BZ = 4128, 4640, 5152, 6176, 7200
O_BI, O_BF = 8224, 8232
O_GA, O_GB = 8240, 9264
NCH = 20
NGATE = 12


def core_cols(g):
    cols = []
    hA = (2 * g, 2 * g + 1)
    for base in (O_AQ, O_AK, O_AV, O_AZ):
        for h in hA:
            cols.extend(range(base + h * 128, base + (h + 1) * 128))
    for base in (O_BQ, O_BK):
        cols.extend(range(base + g * 128, base + (g + 1) * 128))
    for base in (O_BV, O_BO, O_BZ):
        cols.extend(range(base + g * 256, base + (g + 1) * 256))
    for base in (O_GA, O_GB):
        cols.extend(range(base + g * 256, base + (g + 1) * 256))
    assert len(cols) == NCH * 128
    gates = []
    for base in (O_ABETA, O_AALPHA):
        for d in range(2):
            for h in hA:
                gates.append(base + d * H_A + h)
    for base in (O_BI, O_BF):
        for d in range(2):
            gates.append(base + d * H_B + g)
    assert len(gates) == NGATE
    return cols, gates


CH_AQ, CH_AK, CH_AV, CH_AZ = (0, 1), (2, 3), (4, 5), (6, 7)
CH_BQ, CH_BK = 8, 9
CH_BV, CH_BO, CH_BZ = (10, 11), (12, 13), (14, 15)
CH_GA, CH_GB = (16, 17), (18, 19)


LAST_EXEC_NS = [None]


def run(nc, in_maps, trace=False):
    if trace:
        res = run_bass_kernel_spmd(nc, in_maps, core_ids=list(range(NCORES)), trace=True)
    else:
        res = run_bass_kernel_spmd(nc, in_maps, core_ids=list(range(NCORES)))
    LAST_EXEC_NS[0] = getattr(res, "exec_time_ns", None)
    return res.results


def build_mod():
    P = Prog()
    W = P.dram("W", [6, D, 128], F32, "ExternalInput")
    cc = P.dram("cc", [128, 8, 4], F32, "ExternalInput")
    bias = P.dram("bias", [128, 6], F32, "ExternalInput")
    out = P.dram("out", [6, 128, 4], F32, "ExternalOutput")
    P.alloc_psum_banks(2)
    ccs = P.sbuf("ccs", [128, 8, 4])
    scs = P.sbuf("scs", [128, 8, 4])
    bs = P.sbuf("bs", [128, 6])
    os_ = P.sbuf("os", [128, 6, 4])
    P.dma(ccs[:], cc[:, :, :], dst=ccs)
    P.dma(bs[:], bias[:, :], dst=bs)
    P.act(scs[:], ccs[:], AF.Silu, [ccs], [scs])
    wts = [P.sbuf(f"w{i}", [128, 8, 128]) for i in range(2)]
    for i in range(6):
        wt = wts[i % 2]
        P.dma(wt[:], W[i].rearrange("(k p) c -> p k c", p=128), dst=wt)
        ps = P.bank()
        for k in range(8):
            P.mm(ps[:, 0:4], wt[:, k, :], scs[:, k, :], k == 0, k == 7, [wt, scs], [ps])
        P.ts(os_[:, i, :], ps[:, 0:4], bs[:, i:i + 1], None, ALU.add, None, [ps, bs], [os_])
    P.dma(out.rearrange("i p n -> p i n"), os_[:], src=os_)
    return P.finish()


def stage_mod(inp):
    nc = build_mod()
    c4 = np.stack([inp["c"][0], inp["c"][1], inp["c_ctx"], inp["c_ctx"]], axis=-1)
    cc = np.ascontiguousarray(c4.reshape(8, 128, 4).transpose(1, 0, 2))
    maps = []
    for j in range(NCORES):
        Ws, bs = [], []
        for i in range(6):
            job = j * 6 + i
            l, fc = divmod(job, 24)
            Ws.append(inp["ada_w"][l][:, fc * 128:(fc + 1) * 128])
            bs.append(inp["ada_b"][l][fc * 128:(fc + 1) * 128])
        maps.append({"W": np.ascontiguousarray(np.stack(Ws)), "cc": cc,
                     "bias": np.ascontiguousarray(np.stack(bs, axis=1))})
    res = run(nc, maps)
    mod = np.zeros((2, 3, 3 * D), np.float32)
    for j in range(NCORES):
        o = res[j]["out"]
        for i in range(6):
            job = j * 6 + i
            l, fc = divmod(job, 24)
            for n in range(3):
                mod[l, n, fc * 128:(fc + 1) * 128] = o[i, :, n]
    return mod


P_TILES = [(0, CTX)] + [(CTX + i * 512, CTX + (i + 1) * 512) for i in range(SEQ // 512)]


def build_proj(ntiles=None):
    P = Prog()
    xT = P.dram("xT", [D, TT], F32, "ExternalInput")
    Wg = P.dram("Wg", [D, NCH * 128 + 16], F32, "ExternalInput")
    prm = P.dram("prm", [128, 5, 8], F32, "ExternalInput")
    pT = P.dram("pT", [NCH * 128, TT], F32, "ExternalOutput")
    gT = P.dram("gT", [16, TT], F32, "ExternalOutput")
    P.alloc_psum_banks(8)
    NW = NCH * 128 + 16
    W16 = P.sbuf("W16", [128, 8, NW], BF16)
    ones = P.sbuf("ones", [128, 128])
    P.memset(ones[:], 1.0, [ones])
    prms = P.sbuf("prms", [128, 5, 8])
    P.dma(prms[:], prm[:, :, :], dst=prms)
    epsb = P.sbuf("epsb", [128, 1])
    P.memset(epsb[:], EPS, [epsb])
    GS = P.sbuf("GS", [128, 4, 8])
    P.stt(GS[:, 0, :], prms[:, 1, :], 1.0, prms[:, 0, :], ALU.add, ALU.mult, [prms], [GS])
    P.copy(GS[:, 1, :], prms[:, 2, :], [prms], [GS])
    P.stt(GS[:, 2, :], prms[:, 3, :], 1.0, prms[:, 0, :], ALU.add, ALU.mult, [prms], [GS])
    P.copy(GS[:, 3, :], prms[:, 4, :], [prms], [GS])
    Wv = Wg.rearrange("(k p) c -> p k c", p=128)
    stg = [P.sbuf(f"wst{i}", [128, 8, 512]) for i in range(2)]
    c0 = 0
    i = 0
    while c0 < NW:
        c1 = min(NW, c0 + 512)
        s = stg[i % 2]
        P.dma(s[:, :, 0:c1 - c0], Wv[:, :, c0:c1], dst=s)
        P.copy(W16[:, :, c0:c1], s[:, :, 0:c1 - c0], [s], [W16], eng=("dve" if i % 2 == 0 else "pool"))
        c0 = c1
        i += 1
    xv = xT.rearrange("(k p) t -> p k t", p=128)
    xts = [P.sbuf(f"xt{i}", [128, 8, 512]) for i in range(2)]
    sq = P.sbuf("sq", [128, 8, 512])
    hTs = [P.sbuf(f"hT{i}", [128, 8, 512], BF16) for i in range(2)]
    rstd = P.sbuf("rstd", [128, 512])
    tmp = [P.sbuf(f"tmp{i}", [128, 512]) for i in range(2)]
    evs = [P.sbuf(f"ev{i}", [128, 512]) for i in range(4)]
    gev = P.sbuf("gev", [16, 512])
    nev = 0
    for ti, (t0, t1) in enumerate(P_TILES[:ntiles] if ntiles else P_TILES):
        n = t1 - t0
        xt = xts[ti % 2]
        hT = hTs[ti % 2]
        P.dma(xt[:, :, 0:n], xv[:, :, t0:t1], dst=xt)
        P.act(sq[:, :, 0:n], xt[:, :, 0:n], AF.Square, [xt], [sq])
        ss = P.bank()
        for k in range(8):
            P.mm(ss[:, 0:n], ones[:], sq[:, k, 0:n], k == 0, k == 7, [ones, sq], [ss])
        P.act(rstd[:, 0:n], ss[:, 0:n], AF.Ln, [ss, epsb], [rstd], bias=epsb[:, 0:1], scale=1.0 / D)
        P.act(rstd[:, 0:n], rstd[:, 0:n], AF.Exp, [rstd], [rstd], scale=-0.5)
        gi = 2 if ti == 0 else 0
        for k in range(8):
            tm = tmp[k % 2]
            P.tt(tm[:, 0:n], xt[:, k, 0:n], rstd[:, 0:n], ALU.mult, [xt, rstd], [tm])
            P.act(hT[:, k, 0:n], tm[:, 0:n], AF.Identity, [tm, GS], [hT],
                  scale=GS[:, gi, k:k + 1], bias=GS[:, gi + 1, k:k + 1])
        for c in range(NCH):
            ps = P.bank()
            for k in range(8):
                P.mm(ps[:, 0:n], W16[:, k, c * 128:(c + 1) * 128], hT[:, k, 0:n], k == 0, k == 7,
                     [W16, hT], [ps])
            ev = evs[nev % 4]
            P.copy(ev[:, 0:n], ps[:, 0:n], [ps], [ev], eng=("act" if nev % 2 else "dve"))
            nev += 1
            P.dma(pT[c * 128:(c + 1) * 128, t0:t1], ev[:, 0:n], src=ev)
        ps = P.bank()
        for k in range(8):
            P.mm(ps[0:16, 0:n], W16[:, k, NCH * 128:NCH * 128 + 16], hT[:, k, 0:n], k == 0, k == 7,
                 [W16, hT], [ps])
        P.copy(gev[:, 0:n], ps[0:16, 0:n], [ps], [gev])
        P.dma(gT[:, t0:t1], gev[:, 0:n], src=gev)
    return P.finish()


def feat_major(v):
    return np.ascontiguousarray(v.reshape(8, 128).T)


def stage_proj(nc, xT_b, w_in_l, norm_g_l, mod_l):
    maps = []
    for j in range(NCORES):
        b, g = divmod(j, 4)
        cols, gates = core_cols(g)
        Wg = np.zeros((D, NCH * 128 + 16), np.float32)
        Wg[:, :NCH * 128] = w_in_l[:, cols]
        Wg[:, NCH * 128:NCH * 128 + NGATE] = w_in_l[:, gates]
        shift, scale = mod_l[b][0:D], mod_l[b][D:2 * D]
        shift_c, scale_c = mod_l[2][0:D], mod_l[2][D:2 * D]
        prm = np.stack([feat_major(norm_g_l), feat_major(scale), feat_major(shift),
                        feat_major(scale_c), feat_major(shift_c)], axis=1)
        maps.append({"xT": xT_b[b], "Wg": Wg, "prm": np.ascontiguousarray(prm)})
    return run(nc, maps)


CPAD = (CTX + 4) + (SEQ + 4)
C_TILES = [(0, 0, CTX)] + [(CTX + 4, i * 512, 512) for i in range(SEQ // 512)]


def build_conv():
    P = Prog()
    pre = P.dram("pre", [6, 128, CPAD], F32, "ExternalInput")
    cw = P.dram("cw", [128, 6, 5], F32, "ExternalInput")
    gin = {nm: P.dram(nm, [r, TT], F32, "ExternalInput") for nm, r in
           (("betaT", 4), ("alphaT", 4), ("iT", 2), ("fT", 2))}
    gprm = P.dram("gprm", [4, 4], F32, "ExternalInput")
    qkv = P.dram("qkv", [6, 128, TT], F32, "ExternalOutput")
    gout = {nm: P.dram(nm, [r, TT], F32, "ExternalOutput") for nm, r in
            (("beta", 4), ("g", 4), ("ig", 2), ("lf", 2))}
    P.alloc_psum_banks(4)
    ones = P.sbuf("ones", [128, 128])
    P.memset(ones[:], 1.0, [ones])
    cws = P.sbuf("cws", [128, 6, 5])
    P.dma(cws[:], cw[:, :, :], dst=cws)
    epsb = P.sbuf("epsb", [128, 1])
    P.memset(epsb[:], EPS, [epsb])
    gp = P.sbuf("gp", [4, 4])
    P.dma(gp[:], gprm[:, :], dst=gp)
    gp2 = P.sbuf("gp2", [4, 4])
    P.act(gp2[:, 0:1], gp[:, 0:1], AF.Exp, [gp], [gp2])
    P.ts(gp2[:, 1:2], gp2[:, 0:1], -1.0, None, ALU.mult, None, [gp2], [gp2])
    P.ts(gp2[0:2, 2:3], gp[0:2, 3:4], -1.0, None, ALU.mult, None, [gp], [gp2])
    g1 = P.sbuf("g1", [4, TT])
    g2 = P.sbuf("g2", [4, TT])
    P.dma(g1[:], gin["betaT"][:, :], dst=g1)
    P.act(g2[:], g1[:], AF.Sigmoid, [g1], [g2])
    P.dma(gout["beta"][:, :], g2[:], src=g2)
    P.dma(g1[:], gin["alphaT"][:, :], dst=g1)
    P.act(g1[:], g1[:], AF.Exp, [g1, gp], [g1], bias=gp[:, 1:2])
    P.act(g1[:], g1[:], AF.Ln, [g1], [g1], bias=1.0)
    P.ts(g2[:], g1[:], gp2[:, 1:2], None, ALU.mult, None, [g1, gp2], [g2])
    P.dma(gout["g"][:, :], g2[:], src=g2)
    P.dma(g1[0:2, :], gin["iT"][:, :], dst=g1)
    P.ts(g2[0:2, :], g1[0:2, :], gp[0:2, 2:3], None, ALU.add, None, [g1, gp], [g2])
    P.dma(gout["ig"][:, :], g2[0:2, :], src=g2)
    P.dma(g1[0:2, :], gin["fT"][:, :], dst=g1)
    P.act(g1[0:2, :], g1[0:2, :], AF.Exp, [g1, gp2], [g1], bias=gp2[0:2, 2:3], scale=-1.0)
    P.act(g1[0:2, :], g1[0:2, :], AF.Ln, [g1], [g1], bias=1.0)
    P.ts(g2[0:2, :], g1[0:2, :], -1.0, None, ALU.mult, None, [g1], [g2])
    P.dma(gout["lf"][:, :], g2[0:2, :], src=g2)
    NBUF = 2
    xin = [P.sbuf(f"xin{i}", [128, 516]) for i in range(6 * NBUF)]
    acc = [P.sbuf(f"acc{i}", [128, 512]) for i in range(3)]
    ys = [P.sbuf(f"y{i}", [128, 512]) for i in range(6 * NBUF)]
    sqs = [P.sbuf(f"sq{i}", [128, 512]) for i in range(4)]
    rs = [P.sbuf(f"r{i}", [128, 512]) for i in range(4)]
    outs = [P.sbuf(f"o{i}", [128, 512]) for i in range(6)]
    it = 0
    io = 0
    for ti, (off, t0, n) in enumerate(C_TILES):
        tok0 = (0 if off == 0 else CTX) + t0
        for c in range(6):
            xi = xin[(ti % NBUF) * 6 + c]
            a = acc[it % 3]
            y = ys[(ti % NBUF) * 6 + c]
            P.dma(xi[:, 0:n + 4], pre[c, :, off + t0:off + t0 + n + 4], dst=xi)
            P.ts(a[:, 0:n], xi[:, 0:n], cws[:, c, 0:1], None, ALU.mult, None, [xi, cws], [a])
            for s in range(1, 5):
                P.stt(a[:, 0:n], xi[:, s:s + n], cws[:, c, s:s + 1], a[:, 0:n], ALU.mult, ALU.add,
                      [xi, cws, a], [a])
            if c < 4:
                P.act(y[:, 0:n], a[:, 0:n], AF.Silu, [a], [y])
            else:
                o = outs[io % 6]
                io += 1
                P.act(o[:, 0:n], a[:, 0:n], AF.Silu, [a], [o])
                P.dma(qkv[c, :, tok0:tok0 + n], o[:, 0:n], src=o)
            it += 1
        for c in range(4):
            y = ys[(ti % NBUF) * 6 + c]
            sq = sqs[c]
            r = rs[c]
            o = outs[io % 6]
            io += 1
            P.tt(sq[:, 0:n], y[:, 0:n], y[:, 0:n], ALU.mult, [y], [sq], eng="pool")
            ps = P.bank()
            P.mm(ps[:, 0:n], ones[:], sq[:, 0:n], True, True, [ones, sq], [ps])
            P.act(r[:, 0:n], ps[:, 0:n], AF.Ln, [ps, epsb], [r], bias=epsb[:, 0:1])
            P.act(r[:, 0:n], r[:, 0:n], AF.Exp, [r], [r], scale=-0.5)
            sc = (DK_A ** -0.5) if c < 2 else 1.0
            P.stt(o[:, 0:n], y[:, 0:n], sc, r[:, 0:n], ALU.mult, ALU.mult, [y, r], [o])
            P.dma(qkv[c, :, tok0:tok0 + n], o[:, 0:n], src=o)
    return P.finish()


def pad_seq(a):
    z = np.zeros(a.shape[:-1] + (2,), a.dtype)
    return np.ascontiguousarray(np.concatenate([z, a[..., :CTX], z, z, a[..., CTX:], z], axis=-1))


def stage_conv(nc, proj_res, conv_w_l, a_log_l, dt_bias_l, i_bias_l, f_bias_l):
    maps = []
    for j in range(NCORES):
        b, g = divmod(j, 4)
        pT = proj_res[j]["pT"]
        gT = proj_res[j]["gT"]
        pre = pad_seq(pT[0:768].reshape(6, 128, TT))
        hA = (2 * g, 2 * g + 1)
        chans = []
        for base in (0, 1024, 2048):
            for h in hA:
                chans.append(np.arange(base + h * 128, base + (h + 1) * 128))
        cw = np.stack([conv_w_l[:, ch].T for ch in chans], axis=1)
        gprm = np.zeros((4, 4), np.float32)
        k = 0
        for d in range(2):
            for h in hA:
                gprm[k, 0] = a_log_l[d, h]
                gprm[k, 1] = dt_bias_l[d, h]
                k += 1
        for d in range(2):
            gprm[d, 2] = i_bias_l[d, g]
            gprm[d, 3] = f_bias_l[d, g]
        maps.append({"pre": pre, "cw": np.ascontiguousarray(cw), "gprm": gprm,
                     "betaT": np.ascontiguousarray(gT[0:4]), "alphaT": np.ascontiguousarray(gT[4:8]),
                     "iT": np.ascontiguousarray(gT[8:10]), "fT": np.ascontiguousarray(gT[10:12])})
    return run(nc, maps)


class View:
    def __init__(self, buf, ap):
        self.buf = buf
        self.ap = ap

    def __getitem__(self, idx):
        return View(self.buf, self.ap[idx])


def V(buf, *idx):
    if not idx:
        return View(buf, buf.t[:])
    return View(buf, buf.t[idx if len(idx) > 1 else idx[0]])


def _u(x):
    return x.ap if isinstance(x, View) else x


def _b(*xs):
    out = []
    for x in xs:
        if isinstance(x, View) and x.buf not in out:
            out.append(x.buf)
    return out


class AProg(Prog):
    def quarters(self, nbanks=8):
        self.alloc_psum_banks(nbanks)
        self.qtiles = []
        for b in self.psum_banks:
            for q in range(4):
                self.qtiles.append(Buf(self, b.t[:, q * 128:(q + 1) * 128], f"{b.name}q{q}", "psum"))
        self.qi = 0

    def q(self):
        t = self.qtiles[self.qi % len(self.qtiles)]
        self.qi += 1
        return V(t)

    def amm(self, out, lhsT, rhs, start=True, stop=True):
        return self.mm(_u(out), _u(lhsT), _u(rhs), start, stop, _b(lhsT, rhs), _b(out))

    def atr(self, out, in_, ident):
        return self.transpose(_u(out), _u(in_), _u(ident), _b(in_, ident), _b(out))

    def aact(self, out, in_, func, bias=None, scale=None):
        return self.act(_u(out), _u(in_), func, _b(in_, bias, scale), _b(out),
                        bias=_u(bias) if bias is not None else None,
                        scale=_u(scale) if scale is not None else None)

    def att(self, out, in0, in1, op, eng="dve"):
        return self.tt(_u(out), _u(in0), _u(in1), op, _b(in0, in1), _b(out), eng=eng)

    def ats(self, out, in0, s1, op0, s2=None, op1=None, eng="dve"):
        return self.ts(_u(out), _u(in0), _u(s1), _u(s2), op0, op1, _b(in0, s1, s2), _b(out), eng=eng)

    def astt(self, out, in0, scalar, in1, op0, op1):
        return self.stt(_u(out), _u(in0), _u(scalar), _u(in1), op0, op1, _b(in0, scalar, in1), _b(out))

    def acopy(self, out, in_, eng="dve"):
        return self.copy(_u(out), _u(in_), _b(in_), _b(out), eng=eng)

    def amemset(self, out, val, eng="dve"):
        return self.memset(_u(out), val, _b(out), eng=eng)

    def aload(self, dst, src_ap, q="sp"):
        return self.dma(_u(dst), src_ap, dst=dst.buf, q=q)

    def astore(self, dst_ap, src, q="sp"):
        return self.dma(dst_ap, _u(src), src=src.buf, q=q)


def run_round_robin(gens):
    gens = list(gens)
    while gens:
        nxt = []
        for g in gens:
            try:
                next(g)
                nxt.append(g)
            except StopIteration:
                pass
        gens = nxt


BIG = 30000.0
NSC = TT // 128
LG = 4


def make_masks():
    i = np.arange(128)
    same = (i[:, None] // 64) == (i[None, :] // 64)
    U = ((i[:, None] <= i[None, :]) & same).astype(np.float32)
    BD = same.astype(np.float32)
    PL = np.where((i[None, :] < i[:, None]) & same, 0.0, BIG).astype(np.float32)
    NU = np.where((i[:, None] <= i[None, :]) & same, 0.0, -BIG).astype(np.float32)
    ID = np.eye(128, dtype=np.float32)
    NL = np.where((i[None, :] <= i[:, None]) & same, 0.0, -BIG).astype(np.float32)
    ON = np.ones((128, 128), np.float32)
    return np.ascontiguousarray(np.stack([U, BD, PL, NU, ID, NL, ON, -U], axis=1))


M_U, M_BD, M_PL, M_NU, M_ID, M_NL, M_ONES, M_NEGU = range(8)
NMSK = 8


def delta_problem(P, pi, msk, dr, S, o_out, banks, nsc=NSC):
    f = lambda nm, shape=(128, 128): P.sbuf(f"{nm}_{pi}", list(shape))
    fr = lambda nm, shape=(128, 128): P.sbuf(f"{nm}_{pi}", list(shape), F32R)
    qA = lambda: V(banks[0])[:, 0:128]
    qB = (lambda: V(banks[1])[:, 0:128]) if banks[1] is not banks[0] else (lambda: V(banks[0])[:, 128:256])
    qTs = [f(f"qT{i}", (128, LG * 128)) for i in range(2)]
    kTs = [f(f"kT{i}", (128, LG * 128)) for i in range(2)]
    kts = [f(f"ktm{i}", (128, LG, 128)) for i in range(2)]
    vts = [f(f"vtm{i}", (128, LG, 128)) for i in range(2)]
    bcol = f("bcol", (128, NSC))
    nbcol = f("nbcol", (128, NSC))
    gcol = f("gcol", (128, NSC, 2))
    cols = [f(f"cols{i}", (128, 8)) for i in range(2)]
    gbc = f("gbc"); RL = f("RL"); RU = f("RU"); dL = f("dL"); dT = f("dT"); eg = [f("eg0"), f("eg1")]
    BT = f("BT"); Bm = f("Bm")
    Pa = [fr("Pa0"), fr("Pa1")]; PaT = [fr("PaT0"), fr("PaT1")]
    X = [fr("X0"), fr("X1")]
    qTr = fr("qTr"); kTr = fr("kTr")
    kbg = fr("kbg"); vb = fr("vb"); kdec = [f("kdec0"), f("kdec1")]
    wT = f("wT"); u = f("u"); qkT = [f("qkT0"), f("qkT1")]; qdT = [f("qdT0"), f("qdT1")]
    vnew = f("vnew")
    osb = [f("osb0"), f("osb1")]
    P.amemset(V(vnew), 0.0)
    P.aload(V(bcol), dr["bcol"][:, pi, :])
    P.aload(V(gcol), dr["gcol"][:, pi, :, :])
    P.ats(V(nbcol), V(bcol), -1.0, ALU.mult)
    U = V(msk, slice(None), M_U, slice(None))
    BDm = V(msk, slice(None), M_BD, slice(None))
    PLm = V(msk, slice(None), M_PL, slice(None))
    NUm = V(msk, slice(None), M_NU, slice(None))
    ID = V(msk, slice(None), M_ID, slice(None))
    ONESm = V(msk, slice(None), M_ONES, slice(None))
    yield
    for sc in range(nsc):
        t0 = sc * 128
        gi_, gl_ = divmod(sc, LG)
        if gl_ == 0:
            ng = min(LG, nsc - sc)
            P.aload(V(qTs[gi_ % 2])[:, 0:ng * 128], dr["qT"][pi, :, t0:t0 + ng * 128])
            P.aload(V(kTs[gi_ % 2])[:, 0:ng * 128], dr["kT"][pi, :, t0:t0 + ng * 128])
            P.aload(V(kts[gi_ % 2])[:, 0:ng, :],
                    dr["ktm"][pi, t0:t0 + ng * 128, :].rearrange("(g p) d -> p g d", p=128))
            P.aload(V(vts[gi_ % 2])[:, 0:ng, :],
                    dr["vtm"][pi, t0:t0 + ng * 128, :].rearrange("(g p) d -> p g d", p=128))
        qT = V(qTs[gi_ % 2])[:, gl_ * 128:(gl_ + 1) * 128]
        kT = V(kTs[gi_ % 2])[:, gl_ * 128:(gl_ + 1) * 128]
        ktm = V(kts[gi_ % 2])[:, gl_, :]
        vtm = V(vts[gi_ % 2])[:, gl_, :]
        cl = V(cols[sc % 2])
        gps = qA()
        P.amm(gps[:, 0:2], U, V(gcol)[:, sc, :])
        P.amm(gps[:, 2:4], BDm, V(gcol)[:, sc, :])
        P.ats(V(gbc), ONESm, V(gcol)[:, sc, 0:1], ALU.mult, eng="pool")
        yield
        grow = qB()
        P.amm(grow, V(gbc), U)
        P.acopy(cl[:, 0:1], gps[:, 0:1])
        P.ats(cl[:, 1:2], gps[:, 0:1], -1.0, ALU.mult)
        yield
        P.aact(cl[:, 2:3], gps[:, 0:1], AF.Exp)
        P.aact(cl[:, 3:4], gps[:, 2:3], AF.Exp, bias=cl[:, 1:2])
        P.att(V(RL), grow, PLm, ALU.add)
        P.att(V(RU), grow, NUm, ALU.add)
        egr = V(eg[sc % 2])
        P.aact(egr, grow, AF.Exp)
        yield
        P.aact(V(dL), V(RL), AF.Exp, bias=cl[:, 0:1], scale=-1.0)
        P.aact(V(dT), V(RU), AF.Exp, bias=cl[:, 1:2])
        P.att(cl[:, 4:5], cl[:, 2:3], V(bcol)[:, sc:sc + 1], ALU.mult)
        P.acopy(V(kTr), kT, eng="act")
        P.acopy(V(qTr), qT, eng="act")
        kk = qA()
        P.amm(kk, V(kTr), V(kTr))
        qk = qB()
        P.amm(qk, V(kTr), V(qTr))
        yield
        P.astt(V(BT), kk, V(nbcol)[:, sc:sc + 1], V(dL), ALU.mult, ALU.mult)
        P.aact(V(kbg), ktm, AF.Identity, scale=cl[:, 4:5])
        P.aact(V(vb), vtm, AF.Identity, scale=V(bcol)[:, sc:sc + 1])
        kd = V(kdec[sc % 2])
        P.ats(kd, ktm, cl[:, 3:4], ALU.mult, eng="pool")
        qkTs = V(qkT[sc % 2])
        P.att(qkTs, qk, V(dT), ALU.mult)
        qd = V(qdT[sc % 2])
        P.att(qd, qT, egr, ALU.mult, eng="pool")
        yield
        bps = qA()
        P.atr(bps, V(BT), ID)
        yield
        P.acopy(V(Bm), bps, eng="act")
        P.att(V(X[0]), bps, ID, ALU.add)
        yield
        cur, curT = V(Bm), V(BT)
        xi = 0
        for lev in range(5):
            last = lev == 4
            p2T = qA()
            P.amm(p2T, cur, curT)
            if not last:
                p2 = qB()
                P.amm(p2, curT, cur)
            yield
            nT = V(PaT[lev % 2])
            P.acopy(nT, p2T, eng="act")
            if not last:
                n_ = V(Pa[lev % 2])
                P.acopy(n_, p2)
            yield
            xp = qA()
            P.amm(xp, nT, V(X[xi]))
            yield
            P.att(V(X[1 - xi]), xp, View(X[xi], X[xi].t[:].bitcast(F32)), ALU.add)
            xi = 1 - xi
            if not last:
                cur, curT = n_, nT
            yield
        TinvT = V(X[xi])
        wps = qA()
        P.amm(wps, V(kbg), TinvT)
        ups = qB()
        P.amm(ups, TinvT, V(vb))
        yield
        P.acopy(V(wT), wps, eng="act")
        P.acopy(V(u), ups)
        yield
        ob = V(osb[sc % 2])
        for c in range(2):
            r = slice(c * 64, c * 64 + 64)
            vps = qA()
            P.amm(vps[r, :], V(wT)[:, r], V(S))
            yield
            P.att(V(vnew)[r, :], V(u)[r, :], vps[r, :], ALU.subtract)
            yield
            ops_ = qB()
            P.amm(ops_[r, :], qd[:, r], V(S), True, False)
            P.amm(ops_[r, :], qkTs[:, r], V(vnew), False, True)
            sps = qA()
            P.amm(sps, kd[r, :], V(vnew)[r, :])
            yield
            P.astt(V(S), V(S), egr[:, c * 64 + 63:c * 64 + 64], sps, ALU.mult, ALU.add)
            P.acopy(ob[r, :], ops_[r, :], eng="act")
            yield
        P.astore(o_out[pi, t0:t0 + 128, :], ob)
        yield


def build_delta(nsc=NSC, nprob=4):
    P = AProg()
    dr = {
        "qT": P.dram("qT", [4, 128, TT], F32, "ExternalInput"),
        "kT": P.dram("kT", [4, 128, TT], F32, "ExternalInput"),
        "ktm": P.dram("ktm", [4, TT, 128], F32, "ExternalInput"),
        "vtm": P.dram("vtm", [4, TT, 128], F32, "ExternalInput"),
        "bcol": P.dram("bcol", [128, 4, NSC], F32, "ExternalInput"),
        "gcol": P.dram("gcol", [128, 4, NSC, 2], F32, "ExternalInput"),
    }
    mskd = P.dram("msk", [128, NMSK, 128], F32, "ExternalInput")
    o_out = P.dram("o", [4, TT, 128], F32, "ExternalOutput")
    P.alloc_psum_banks(8)
    msk = P.sbuf("msk_sb", [128, NMSK, 128])
    P.aload(V(msk), mskd[:, :, :])
    gens = []
    for pi in range(nprob):
        S = P.sbuf(f"S_{pi}", [128, 128])
        P.amemset(V(S), 0.0)
        gens.append(delta_problem(P, pi, msk, dr, S, o_out,
                                  (P.psum_banks[2 * pi], P.psum_banks[2 * pi + 1]), nsc))
    run_round_robin(gens)
    return P.finish()


def prob_order(d):
    ci = np.arange(CTX)
    li = CTX + np.arange(SEQ)
    if d == 1:
        ci = ci[::-1]
        li = li[::-1]
    return np.concatenate([ci, li])


def col_layout(v):
    return np.ascontiguousarray(v.reshape(NSC, 128).T)


def stage_delta(nc, conv_res):
    msk = make_masks()
    maps = []
    for j in range(NCORES):
        r = conv_res[j]
        qT, kT, ktm, vtm = [], [], [], []
        bcol = np.zeros((128, 4, NSC), np.float32)
        gcol = np.zeros((128, 4, NSC, 2), np.float32)
        for pi in range(4):
            d, hh = divmod(pi, 2)
            idx = prob_order(d)
            qT.append(r["qkv"][0 + hh][:, idx])
            kk = r["qkv"][2 + hh][:, idx]
            kT.append(kk)
            ktm.append(kk.T)
            vtm.append(r["qkv"][4 + hh][:, idx].T)
            bcol[:, pi, :] = col_layout(r["beta"][pi][idx])
            gc = col_layout(r["g"][pi][idx])
            gcol[:, pi, :, 0] = gc
            gcol[:, pi, :, 1] = gc
        maps.append({"qT": np.ascontiguousarray(np.stack(qT)), "kT": np.ascontiguousarray(np.stack(kT)),
                     "ktm": np.ascontiguousarray(np.stack(ktm)), "vtm": np.ascontiguousarray(np.stack(vtm)),
                     "bcol": bcol, "gcol": gcol, "msk": msk})
    if nc is None:
        return maps
    return run(nc, maps)


KSC = DQK_B ** -0.5
NEGBIG = -1.0e30


def mlstm_problem(P, pi, msk, dr, h_out, banks, nsc=NSC):
    f = lambda nm, shape=(128, 128): P.sbuf(f"{nm}_m{pi}", list(shape))
    if len(banks) == 4:
        qA = lambda: V(banks[0])
        qB = lambda: V(banks[1])
        qC = lambda: V(banks[2])
        qD = lambda: V(banks[3])
        qN = qA
        qCp = qB
    else:
        qA = lambda: V(banks[0])[:, 0:128]
        qB = lambda: V(banks[0])[:, 128:256]
        qC = lambda: V(banks[0])[:, 256:384]
        qD = lambda: V(banks[0])[:, 384:512]
        qN = lambda: V(banks[1])
        qCp = lambda: V(banks[0])
    qTs = [f(f"qT{i}", (128, LG * 128)) for i in range(2)]
    kTs = [f(f"kT{i}", (128, LG * 128)) for i in range(2)]
    kts = [f(f"ktm{i}", (128, LG, 128)) for i in range(2)]
    vas = [f(f"vaug{i}", (128, LG, 258)) for i in range(2)]
    igc = f("igc", (128, NSC))
    lfc = f("lfc", (128, NSC, 2))
    cols = [f(f"cols{i}", (128, 8)) for i in range(2)]
    lfb = f("lfb"); igb = f("igb"); brs = [f("brs0"), f("brs1")]; Rm = f("Rm")
    pmx = [f("pmx0"), f("pmx1")]; dmr = f("dmr")
    mtr = f("mtr", (128, 64)); Er = f("Er", (128, 64)); inr = f("inr", (128, 64)); tmpw = f("tmpw", (128, 64))
    W1 = f("W1", (128, 64))
    qd = f("qd")
    wts = [f("wts0"), f("wts1")]
    kw = f("kw")
    Cn = f("Cn", (128, 258))
    ms = [f("m0", (128, 1)), f("m1", (128, 1))]
    sm = [f("sm0", (128, 8)), f("sm1", (128, 8))]
    hsb = [f("hsb0", (128, 256)), f("hsb1", (128, 256))]
    P.amemset(V(Cn), 0.0)
    P.amemset(V(ms[0]), 0.0)
    for w_ in wts:
        P.amemset(V(w_), 0.0, eng="pool")
    for va in vas:
        P.amemset(V(va)[:, :, 256:258], 1.0, eng="pool")
    P.aload(V(igc), dr["igc"][:, pi, :])
    P.aload(V(lfc), dr["lfc"][:, pi, :, :])
    U = V(msk, slice(None), M_U, slice(None))
    NUm = V(msk, slice(None), M_NU, slice(None))
    ID = V(msk, slice(None), M_ID, slice(None))
    NLm = V(msk, slice(None), M_NL, slice(None))
    ONESm = V(msk, slice(None), M_ONES, slice(None))
    NEGU = V(msk, slice(None), M_NEGU, slice(None))
    mi = 0
    yield
    for sc in range(nsc):
        t0 = sc * 128
        gi_, gl_ = divmod(sc, LG)
        if gl_ == 0:
            ng = min(LG, nsc - sc)
            P.aload(V(qTs[gi_ % 2])[:, 0:ng * 128], dr["qT"][pi, :, t0:t0 + ng * 128])
            P.aload(V(kTs[gi_ % 2])[:, 0:ng * 128], dr["kT"][pi, :, t0:t0 + ng * 128])
            P.aload(V(kts[gi_ % 2])[:, 0:ng, :],
                    dr["ktm"][pi, t0:t0 + ng * 128, :].rearrange("(g p) d -> p g d", p=128))
            P.aload(V(vas[gi_ % 2])[:, 0:ng, 0:256],
                    dr["vtm"][pi, t0:t0 + ng * 128, :].rearrange("(g p) d -> p g d", p=128))
        qT = V(qTs[gi_ % 2])[:, gl_ * 128:(gl_ + 1) * 128]
        kT = V(kTs[gi_ % 2])[:, gl_ * 128:(gl_ + 1) * 128]
        ktm = V(kts[gi_ % 2])[:, gl_, :]
        va = V(vas[gi_ % 2])[:, gl_, :]
        cl = V(cols[sc % 2])
        bps = qA()
        P.amm(bps[:, 0:2], U, V(lfc)[:, sc, :])
        P.ats(V(lfb), ONESm, V(lfc)[:, sc, 0:1], ALU.mult, eng="pool")
        P.ats(V(igb), ONESm, V(igc)[:, sc:sc + 1], ALU.mult, eng="pool")
        yield
        brow = qB()
        P.amm(brow[:, 0:128], V(lfb), U)
        rrow = qC()
        P.amm(rrow[:, 0:128], V(igb), ID, True, False)
        P.amm(rrow[:, 0:128], V(lfb), NEGU, False, True)
        qk = qD()
        P.amm(qk[:, 0:128], kT, qT)
        P.acopy(cl[:, 0:1], bps[:, 0:1])
        yield
        br = V(brs[sc % 2])
        P.acopy(br, brow[:, 0:128], eng="act")
        P.att(cl[:, 1:2], V(igc)[:, sc:sc + 1], cl[:, 0:1], ALU.subtract)
        P.att(V(Rm), rrow[:, 0:128], NLm, ALU.add)
        pm = V(pmx[sc % 2])
        for c in range(2):
            r = slice(c * 64, c * 64 + 64)
            P.op("dve", lambda e, o=_u(pm[:, r]), d0=_u(ONESm[:, r]), d1=_u(rrow[:, r]):
                 e.tensor_tensor_scan(o, d0, d1, NEGBIG, ALU.mult, ALU.max),
                 _b(ONESm, rrow), _b(pm))
        yield
        P.op("dve", lambda e, o=_u(cl[:, 2:3]), i_=_u(V(Rm)): e.tensor_reduce(o, i_, AX.X, ALU.max),
             _b(V(Rm)), _b(cl))
        P.att(V(dmr), br, pm, ALU.add)
        yield
        P.att(cl[:, 3:4], cl[:, 0:1], cl[:, 2:3], ALU.add)
        yield
        hb_ = V(hsb[sc % 2])
        for c in range(2):
            r = slice(c * 64, c * 64 + 64)
            last = c * 64 + 63
            m_old = V(ms[mi]); m_new = V(ms[1 - mi])
            s_ = V(sm[c])
            P.astt(V(mtr), br[:, r], m_old[:, 0:1], V(dmr)[:, r], ALU.add, ALU.max)
            P.astt(s_[:, 0:1], cl[:, 0:1], m_old[:, 0:1], cl[:, 3:4], ALU.add, ALU.max)
            P.att(s_[:, 1:2], m_old[:, 0:1], pm[:, last:last + 1], ALU.max)
            yield
            P.att(V(Er), br[:, r], V(mtr), ALU.subtract)
            P.aact(s_[:, 2:3], s_[:, 0:1], AF.Exp, scale=-1.0)
            P.att(m_new[:, 0:1], s_[:, 1:2], br[:, last:last + 1], ALU.add)
            yield
            P.aact(V(inr), V(Er), AF.Exp, bias=m_old[:, 0:1])
            P.att(V(tmpw)[r, :], V(Er)[r, :], NUm[r, r], ALU.add, eng="pool")
            P.att(s_[:, 3:4], br[:, last:last + 1], m_new[:, 0:1], ALU.subtract)
            yield
            P.att(V(qd)[:, r], qT[:, r], V(inr), ALU.mult, eng="pool")
            P.aact(V(W1)[r, :], V(tmpw)[r, :], AF.Exp, bias=cl[r, 1:2])
            P.aact(s_[:, 4:5], cl[:, 1:2], AF.Exp, bias=s_[:, 3:4])
            P.aact(s_[:, 5:6], s_[:, 3:4], AF.Exp, bias=m_old[:, 0:1])
            yield
            wt = V(wts[c])
            P.astt(wt[r, r], qk[r, r], KSC, V(W1)[r, :], ALU.mult, ALU.mult)
            P.ats(V(kw)[r, :], ktm[r, :], s_[r, 4:5], ALU.mult, KSC, ALU.mult)
            yield
            nps = qN()
            P.amm(nps[r, 0:258], V(qd)[:, r], V(Cn), True, False)
            P.amm(nps[r, 0:258], wt[:, r], va, False, True)
            cps = qCp()
            P.amm(cps[:, 0:258], V(kw)[r, :], va[r, :])
            yield
            P.ats(s_[r, 7:8], nps[r, 256:257], -1.0, ALU.mult)
            P.astt(s_[r, 6:7], nps[r, 256:257], s_[r, 2:3], s_[r, 7:8], ALU.max, ALU.max)
            P.astt(V(Cn), V(Cn), s_[:, 5:6], cps[:, 0:258], ALU.mult, ALU.add)
            yield
            P.op("dve", lambda e, o=_u(s_[r, 7:8]), i_=_u(s_[r, 6:7]): e.reciprocal(o, i_), _b(s_), _b(s_))
            yield
            P.ats(hb_[r, :], nps[r, 0:256], s_[r, 7:8], ALU.mult)
            mi = 1 - mi
            yield
        P.astore(h_out[pi, t0:t0 + 128, :], hb_)
        yield


def mlstm_drams(P, pre=""):
    dr = {
        "qT": P.dram(pre + "qT", [2, 128, TT], F32, "ExternalInput"),
        "kT": P.dram(pre + "kT", [2, 128, TT], F32, "ExternalInput"),
        "ktm": P.dram(pre + "ktm", [2, TT, 128], F32, "ExternalInput"),
        "vtm": P.dram(pre + "vtm", [2, TT, 256], F32, "ExternalInput"),
        "igc": P.dram(pre + "igc", [128, 2, NSC], F32, "ExternalInput"),
        "lfc": P.dram(pre + "lfc", [128, 2, NSC, 2], F32, "ExternalInput"),
    }
    h_out = P.dram("h", [2, TT, 256], F32, "ExternalOutput")
    return dr, h_out


def build_scan(nsc=NSC):
    P = AProg()
    dr = {
        "qT": P.dram("qT", [4, 128, TT], F32, "ExternalInput"),
        "kT": P.dram("kT", [4, 128, TT], F32, "ExternalInput"),
        "ktm": P.dram("ktm", [4, TT, 128], F32, "ExternalInput"),
        "vtm": P.dram("vtm", [4, TT, 128], F32, "ExternalInput"),
        "bcol": P.dram("bcol", [128, 4, NSC], F32, "ExternalInput"),
        "gcol": P.dram("gcol", [128, 4, NSC, 2], F32, "ExternalInput"),
    }
    mdr, h_out = mlstm_drams(P, "m_")
    mskd = P.dram("msk", [128, NMSK, 128], F32, "ExternalInput")
    o_out = P.dram("o", [4, TT, 128], F32, "ExternalOutput")
    P.alloc_psum_banks(8)
    msk = P.sbuf("msk_sb", [128, NMSK, 128])
    P.aload(V(msk), mskd[:, :, :])
    gens = []
    for pi in range(4):
        S = P.sbuf(f"S_{pi}", [128, 128])
        P.amemset(V(S), 0.0)
        gens.append(delta_problem(P, pi, msk, dr, S, o_out, (P.psum_banks[pi], P.psum_banks[pi]), nsc))
    for pi in range(2):
        gens.append(mlstm_problem(P, pi, msk, mdr, h_out, P.psum_banks[4 + 2 * pi:6 + 2 * pi], nsc))
    run_round_robin(gens)
    return P.finish()


def build_mlstm(nsc=NSC, nprob=2):
    P = AProg()
    dr, h_out = mlstm_drams(P)
    mskd = P.dram("msk", [128, NMSK, 128], F32, "ExternalInput")
    P.alloc_psum_banks(8)
    msk = P.sbuf("msk_sb", [128, NMSK, 128])
    P.aload(V(msk), mskd[:, :, :])
    gens = []
    for pi in range(nprob):
        gens.append(mlstm_problem(P, pi, msk, dr, h_out, P.psum_banks[4 * pi:4 * pi + 4], nsc))
    run_round_robin(gens)
    return P.finish()


def colmajor_perm():
    rows = SEQ // GRID_W
    return np.arange(SEQ).reshape(rows, GRID_W).T.reshape(-1)


def mlstm_order(d):
    ci = np.arange(CTX)
    li = CTX + colmajor_perm()
    if d == 1:
        ci = ci[::-1]
        li = li[::-1]
    return np.concatenate([ci, li])


def stage_mlstm(nc, proj_res, conv_res):
    msk = make_masks()
    maps = []
    for j in range(NCORES):
        pT = proj_res[j]["pT"]
        r = conv_res[j]
        q = pT[CH_BQ * 128:(CH_BQ + 1) * 128]
        k = pT[CH_BK * 128:(CH_BK + 1) * 128]
        v = pT[CH_BV[0] * 128:(CH_BV[1] + 1) * 128]
        qT, kT, ktm, vtm = [], [], [], []
        igc = np.zeros((128, 2, NSC), np.float32)
        lfc = np.zeros((128, 2, NSC, 2), np.float32)
        for d in range(2):
            idx = mlstm_order(d)
            qT.append(q[:, idx])
            kk = k[:, idx]
            kT.append(kk)
            ktm.append(kk.T)
            vtm.append(v[:, idx].T)
            igc[:, d, :] = col_layout(r["ig"][d][idx])
            lc = col_layout(r["lf"][d][idx])
            lfc[:, d, :, 0] = lc
            lfc[:, d, :, 1] = lc
        maps.append({"qT": np.ascontiguousarray(np.stack(qT)), "kT": np.ascontiguousarray(np.stack(kT)),
                     "ktm": np.ascontiguousarray(np.stack(ktm)), "vtm": np.ascontiguousarray(np.stack(vtm)),
                     "igc": igc, "lfc": lfc, "msk": msk})
    if nc is None:
        return maps
    return run(nc, maps)


def stage_scan(nc, proj_res, conv_res):
    dm = stage_delta(None, conv_res)
    mm_ = stage_mlstm(None, proj_res, conv_res)
    maps = []
    for j in range(NCORES):
        mp = dict(dm[j])
        for k, v in mm_[j].items():
            if k != "msk":
                mp["m_" + k] = v
        maps.append(mp)
    res = run(nc, maps)
    return [{"o": r["o"]} for r in res], [{"h": r["h"]} for r in res]


O_ARR = ("oaf", "oab", "hbf", "hbb", "az", "bo", "bz", "ga", "gb", "x")


def build_merge(has_ctx, final, ntiles=None):
    P = AProg()
    NT = (64 if has_ctx else 0) + 2048
    tiles = ([(0, 64, 1)] if has_ctx else []) + [((64 if has_ctx else 0) + i * 512, 512, 0) for i in range(4)]
    if ntiles is not None:
        tiles = tiles[:ntiles]
    dr = {nm: P.dram(nm, [D, NT], F32, "ExternalInput").rearrange("(k p) t -> p k t", p=128) for nm in O_ARR}
    Wd = {nm: P.dram(nm, [D, D], F32, "ExternalInput").rearrange("(k p) c -> p k c", p=128)
          for nm in ("w_pa", "w_pb", "w_out")}
    prm = P.dram("prm", [128, 4, 8], F32, "ExternalInput")
    outT = P.dram("outT", [D, NT], F32, "ExternalOutput").rearrange("(k p) t -> p k t", p=128)
    P.alloc_psum_banks(8)
    ones = P.sbuf("ones", [128, 128])
    P.amemset(V(ones), 1.0)
    prms = P.sbuf("prms", [128, 4, 8])
    P.aload(V(prms), prm[:, :, :])
    W16 = {}
    stg = [P.sbuf(f"wst{i}", [128, 8, 256]) for i in range(2)]
    si = 0
    for nm in ("w_pa", "w_pb", "w_out"):
        W16[nm] = P.sbuf(f"{nm}16", [128, 8, D], BF16)
        for half in range(4):
            s_ = V(stg[si % 2])
            P.aload(s_, Wd[nm][:, :, half * 256:(half + 1) * 256])
            P.acopy(V(W16[nm])[:, :, half * 256:(half + 1) * 256], s_, eng=("dve" if si % 2 == 0 else "pool"))
            si += 1
    yaT = P.sbuf("yaT", [128, 8, 512], BF16)
    ybT = P.sbuf("ybT", [128, 8, 512], BF16)
    yT = P.sbuf("yT", [128, 8, 512], BF16)
    xn = P.sbuf("xn", [128, 8, 512])
    NB = 2
    bufs = {nm: [P.sbuf(f"in_{nm}{i}", [128, 512]) for i in range(NB)] for nm in O_ARR}
    cnt = {nm: 0 for nm in O_ARR}

    def load(nm, k, t0, n):
        b = V(bufs[nm][cnt[nm] % NB])[:, 0:n]
        cnt[nm] += 1
        P.aload(b, dr[nm][:, k, t0:t0 + n])
        return b

    tmpn = [0]

    def tmp(tag, dtype=F32):
        key = f"t_{tag}"
        if key not in bufs:
            bufs[key] = [P.sbuf(f"{key}{i}", [128, 512], dtype) for i in range(2)]
            cnt[key] = 0
        b = V(bufs[key][cnt[key] % 2])
        cnt[key] += 1
        return b

    epsb = P.sbuf("epsb", [128, 1])
    P.amemset(V(epsb), EPS)

    def rstd_from(ss_ps, n, dim):
        r = tmp("rstd")[:, 0:n]
        P.aact(r, ss_ps, AF.Ln, bias=V(epsb)[:, 0:1], scale=1.0 / dim)
        P.aact(r, r, AF.Exp, scale=-0.5)
        return r

    for (t0, n, is_ctx) in tiles:
        for k in range(8):
            f_ = load("oaf", k, t0, n)
            b_ = load("oab", k, t0, n)
            oa = tmp("oa")[:, 0:n]
            P.att(oa, f_, b_, ALU.add)
            sq = tmp("sq")[:, 0:n]
            P.aact(sq, oa, AF.Square)
            ss = V(P.bank())[:, 0:n]
            P.amm(ss, V(ones), sq)
            r = rstd_from(ss, n, 128)
            az = load("az", k, t0, n)
            sz = tmp("sz")[:, 0:n]
            P.aact(sz, az, AF.Silu)
            t1 = tmp("t1")[:, 0:n]
            P.att(t1, oa, r, ALU.mult)
            P.astt(V(yaT)[:, k, 0:n], sz, V(prms)[:, 3, 0:1], t1, ALU.mult, ALU.mult)
        for hd in range(4):
            hbs = []
            ss = V(P.bank())[:, 0:n]
            for c in range(2):
                k = hd * 2 + c
                f_ = load("hbf", k, t0, n)
                b_ = load("hbb", k, t0, n)
                hb = tmp(f"hb{c}")[:, 0:n]
                P.att(hb, f_, b_, ALU.add)
                sq = tmp("sq")[:, 0:n]
                P.aact(sq, hb, AF.Square)
                P.amm(ss, V(ones), sq, c == 0, c == 1)
                hbs.append(hb)
            r = rstd_from(ss, n, 256)
            for c in range(2):
                k = hd * 2 + c
                bo = load("bo", k, t0, n)
                bz = load("bz", k, t0, n)
                so = tmp("so")[:, 0:n]
                P.aact(so, bo, AF.Sigmoid)
                sz = tmp("sz")[:, 0:n]
                P.aact(sz, bz, AF.Silu)
                t4 = tmp("t4")[:, 0:n]
                P.att(t4, so, sz, ALU.mult, eng="pool")
                t1 = tmp("t1")[:, 0:n]
                P.att(t1, hbs[c], r, ALU.mult)
                P.astt(V(ybT)[:, k, 0:n], t4, V(prms)[:, 3, 1 + c:2 + c], t1, ALU.mult, ALU.mult)
        for m in range(8):
            za = V(P.bank())[:, 0:n]
            for k in range(8):
                P.amm(za, V(W16["w_pa"])[:, k, m * 128:(m + 1) * 128], V(yaT)[:, k, 0:n], k == 0, k == 7)
            zb = V(P.bank())[:, 0:n]
            for k in range(8):
                P.amm(zb, V(W16["w_pb"])[:, k, m * 128:(m + 1) * 128], V(ybT)[:, k, 0:n], k == 0, k == 7)
            ga = load("ga", m, t0, n)
            gb = load("gb", m, t0, n)
            sga = tmp("sga")[:, 0:n]
            P.aact(sga, ga, AF.Sigmoid)
            sgb = tmp("sgb")[:, 0:n]
            P.aact(sgb, gb, AF.Sigmoid)
            ta = tmp("ta")[:, 0:n]
            P.att(ta, za, sga, ALU.mult)
            tb = tmp("tb")[:, 0:n]
            P.att(tb, zb, sgb, ALU.mult)
            P.att(V(yT)[:, m, 0:n], ta, tb, ALU.add)
        for m in range(8):
            ops_ = V(P.bank())[:, 0:n]
            for k in range(8):
                P.amm(ops_, V(W16["w_out"])[:, k, m * 128:(m + 1) * 128], V(yT)[:, k, 0:n], k == 0, k == 7)
            xk = load("x", m, t0, n)
            P.astt(V(xn)[:, m, 0:n], ops_, V(prms)[:, 1 if is_ctx else 0, m:m + 1], xk, ALU.mult, ALU.add)
        if final:
            sqf = V(yT)
            ss = V(P.bank())[:, 0:n]
            for k in range(8):
                sq = tmp("sq")[:, 0:n]
                P.aact(sq, V(xn)[:, k, 0:n], AF.Square)
                P.amm(ss, V(ones), sq, k == 0, k == 7)
            r = rstd_from(ss, n, D)
            for k in range(8):
                t1 = tmp("t1")[:, 0:n]
                P.att(t1, V(xn)[:, k, 0:n], r, ALU.mult)
                o_ = tmp("fo")[:, 0:n]
                P.ats(o_, t1, V(prms)[:, 2, k:k + 1], ALU.mult)
                P.astore(outT[:, k, t0:t0 + n], o_)
        else:
            for k in range(8):
                P.astore(outT[:, k, t0:t0 + n], V(xn)[:, k, 0:n])
    return P.finish()


def core_tokens(qtr, has_ctx):
    lat = CTX + np.arange(qtr * 2048, (qtr + 1) * 2048)
    if has_ctx:
        return np.concatenate([np.arange(qtr * 64, (qtr + 1) * 64), lat])
    return lat


def stage_merge(nc, has_ctx, proj_res, delta_res, mlstm_res, xT_b, mod_l, inp, l, final):
    full = []
    for b in range(BATCH):
        A = {nm: np.empty((D, TT), np.float32) for nm in O_ARR if nm != "x"}
        for g in range(4):
            j = b * 4 + g
            pT = proj_res[j]["pT"]
            for hh in range(2):
                h = 2 * g + hh
                A["az"][h * 128:(h + 1) * 128] = pT[CH_AZ[hh] * 128:(CH_AZ[hh] + 1) * 128]
                for d, nm in ((0, "oaf"), (1, "oab")):
                    o = delta_res[j]["o"][d * 2 + hh]
                    nat = np.empty_like(o)
                    nat[prob_order(d)] = o
                    A[nm][h * 128:(h + 1) * 128] = nat.T
            for nm, ch in (("bo", CH_BO), ("bz", CH_BZ), ("ga", CH_GA), ("gb", CH_GB)):
                A[nm][g * 256:(g + 1) * 256] = pT[ch[0] * 128:(ch[1] + 1) * 128]
            for d, nm in ((0, "hbf"), (1, "hbb")):
                hv = mlstm_res[j]["h"][d]
                nat = np.empty_like(hv)
                nat[mlstm_order(d)] = hv
                A[nm][g * 256:(g + 1) * 256] = nat.T
        A["x"] = xT_b[b]
        full.append(A)
    maps = []
    for j in range(NCORES):
        b, qtr = divmod(j, 4)
        tok = core_tokens(qtr, has_ctx)
        mp = {nm: np.ascontiguousarray(full[b][nm][:, tok]) for nm in O_ARR}
        prm = np.zeros((128, 4, 8), np.float32)
        prm[:, 0, :] = feat_major(mod_l[b][2 * D:3 * D])
        prm[:, 1, :] = feat_major(mod_l[2][2 * D:3 * D])
        prm[:, 2, :] = feat_major(inp["final_g"])
        prm[:, 3, 0] = inp["norm_a_g"][l]
        prm[:, 3, 1] = inp["norm_b_g"][l][0:128]
        prm[:, 3, 2] = inp["norm_b_g"][l][128:256]
        mp["prm"] = prm
        for nm in ("w_pa", "w_pb", "w_out"):
            mp[nm] = np.ascontiguousarray(inp[nm][l])
        maps.append(mp)
    res = run(nc, maps)
    out = [np.array(xT_b[b]) for b in range(BATCH)]
    for j in range(NCORES):
        b, qtr = divmod(j, 4)
        tok = core_tokens(qtr, has_ctx)
        out[b][:, tok] = res[j]["outT"]
    return out


def kernel(x, c, ctx, c_ctx, ada_w, ada_b, norm_g, w_in, conv_w, a_log, dt_bias, norm_a_g,
           i_bias, f_bias, norm_b_g, w_pa, w_pb, w_out, final_g):
    inp = {k: np.asarray(v, dtype=np.float32) for k, v in dict(
        x=x, c=c, ctx=ctx, c_ctx=c_ctx, ada_w=ada_w, ada_b=ada_b, norm_g=norm_g, w_in=w_in,
        conv_w=conv_w, a_log=a_log, dt_bias=dt_bias, norm_a_g=norm_a_g, i_bias=i_bias,
        f_bias=f_bias, norm_b_g=norm_b_g, w_pa=w_pa, w_pb=w_pb, w_out=w_out, final_g=final_g).items()}
    mod = stage_mod(inp)
    xT_b = [np.ascontiguousarray(np.concatenate([inp["ctx"][b], inp["x"][b]], axis=0).T)
            for b in range(BATCH)]
    nc_pc = build_projconv()
    nc_scan = build_scan()
    for l in range(2):
        proj_res, conv_res = stage_projconv(nc_pc, xT_b, inp["w_in"][l], inp["norm_g"][l], mod[l],
                                            inp["conv_w"][l], inp["a_log"][l], inp["dt_bias"][l],
                                            inp["i_bias"][l], inp["f_bias"][l])
        delta_res, mlstm_res = stage_scan(nc_scan, proj_res, conv_res)
        last = l == 1
        nc_merge = build_merge(not last, last)
        xT_b = stage_merge(nc_merge, not last, proj_res, delta_res, mlstm_res, xT_b, mod[l], inp, l, last)
        del proj_res, conv_res, delta_res, mlstm_res
    out = np.stack([np.ascontiguousarray(xT_b[b][:, CTX:].T) for b in range(BATCH)])
    return out.astype(np.float32)


NWPC = NCH * 128 + 128


def build_projconv(ntiles=None):
    P = Prog()
    xT = P.dram("xT", [D, TT], F32, "ExternalInput")
    Wg = P.dram("Wg", [D, NWPC], F32, "ExternalInput")
    prm = P.dram("prm", [128, 5, 8], F32, "ExternalInput")
    cw = P.dram("cw", [128, 6, 5], F32, "ExternalInput")
    gprm = P.dram("gprm", [128, 4], F32, "ExternalInput")
    pT = P.dram("pT", [14 * 128, TT], F32, "ExternalOutput")
    qkv = P.dram("qkv", [6, 128, TT], F32, "ExternalOutput")
    gout = {nm: P.dram(nm, [r, TT], F32, "ExternalOutput") for nm, r in
            (("beta", 4), ("g", 4), ("ig", 2), ("lf", 2))}
    P.alloc_psum_banks(8)
    W16 = P.sbuf("W16", [128, 8, NWPC], BF16)
    ones = P.sbuf("ones", [128, 128])
    P.memset(ones[:], 1.0, [ones])
    prms = P.sbuf("prms", [128, 5, 8])
    P.dma(prms[:], prm[:, :, :], dst=prms)
    cws = P.sbuf("cws", [128, 6, 5])
    P.dma(cws[:], cw[:, :, :], dst=cws)
    gp = P.sbuf("gp", [128, 4])
    P.dma(gp[:], gprm[:, :], dst=gp)
    gp2 = P.sbuf("gp2", [128, 4])
    P.act(gp2[32:36, 0:1], gp[32:36, 0:1], AF.Exp, [gp], [gp2])
    P.ts(gp2[32:36, 1:2], gp2[32:36, 0:1], -1.0, None, ALU.mult, None, [gp2], [gp2])
    P.ts(gp2[96:98, 2:3], gp[96:98, 3:4], -1.0, None, ALU.mult, None, [gp], [gp2])
    epsb = P.sbuf("epsb", [128, 1])
    P.memset(epsb[:], EPS, [epsb])
    GS = P.sbuf("GS", [128, 4, 8])
    P.stt(GS[:, 0, :], prms[:, 1, :], 1.0, prms[:, 0, :], ALU.add, ALU.mult, [prms], [GS])
    P.copy(GS[:, 1, :], prms[:, 2, :], [prms], [GS])
    P.stt(GS[:, 2, :], prms[:, 3, :], 1.0, prms[:, 0, :], ALU.add, ALU.mult, [prms], [GS])
    P.copy(GS[:, 3, :], prms[:, 4, :], [prms], [GS])
    Wv = Wg.rearrange("(k p) c -> p k c", p=128)
    stg = [P.sbuf(f"wst{i}", [128, 8, 256]) for i in range(2)]
    c0 = 0
    i = 0
    while c0 < NWPC:
        c1 = min(NWPC, c0 + 256)
        s = stg[i % 2]
        P.dma(s[:, :, 0:c1 - c0], Wv[:, :, c0:c1], dst=s)
        P.copy(W16[:, :, c0:c1], s[:, :, 0:c1 - c0], [s], [W16], eng=("dve" if i % 2 == 0 else "pool"))
        c0 = c1
        i += 1
    xv = xT.rearrange("(k p) t -> p k t", p=128)
    xts = [P.sbuf(f"xt{i}", [128, 8, 512]) for i in range(2)]
    sqk = [P.sbuf(f"sqk{i}", [128, 512]) for i in range(2)]
    hTs = [P.sbuf(f"hT{i}", [128, 8, 512], BF16) for i in range(2)]
    rstd = P.sbuf("rstd", [128, 512])
    tmp = [P.sbuf(f"tmp{i}", [128, 512]) for i in range(2)]
    evs = [P.sbuf(f"ev{i}", [128, 512]) for i in range(4)]
    ga = P.sbuf("ga", [128, 512])
    gb = P.sbuf("gb", [128, 512])
    XB = [[P.sbuf(f"xb{c}_{i}", [128, 516]) for i in range(3)] for c in range(6)]
    acc = [P.sbuf(f"acc{i}", [128, 512]) for i in range(3)]
    ys = [P.sbuf(f"y{i}", [128, 512]) for i in range(4)]
    sqs = [P.sbuf(f"sq{i}", [128, 512]) for i in range(2)]
    rs = [P.sbuf(f"r{i}", [128, 512]) for i in range(2)]
    outs = [P.sbuf(f"o{i}", [128, 512]) for i in range(4)]
    cnt = {"acc": 0, "out": 0, "ev": 0}
    tiles = P_TILES[:ntiles] if ntiles else P_TILES
    last_ti = len(tiles) - 1

    def conv_tile(tj):
        t0, t1 = tiles[tj]
        n = t1 - t0
        for c in range(6):
            xi = XB[c][tj % 3]
            a = acc[cnt["acc"] % 3]
            cnt["acc"] += 1
            P.ts(a[:, 0:n], xi[:, 0:n], cws[:, c, 0:1], None, ALU.mult, None, [xi, cws], [a])
            for s_ in range(1, 5):
                P.stt(a[:, 0:n], xi[:, s_:s_ + n], cws[:, c, s_:s_ + 1], a[:, 0:n], ALU.mult, ALU.add,
                      [xi, cws, a], [a])
            if c < 4:
                P.act(ys[c][:, 0:n], a[:, 0:n], AF.Silu, [a], [ys[c]])
            else:
                o = outs[cnt["out"] % 4]
                cnt["out"] += 1
                P.act(o[:, 0:n], a[:, 0:n], AF.Silu, [a], [o])
                P.dma(qkv[c, :, t0:t1], o[:, 0:n], src=o)
        for c in range(4):
            y = ys[c]
            sq = sqs[c % 2]
            r = rs[c % 2]
            o = outs[cnt["out"] % 4]
            cnt["out"] += 1
            P.tt(sq[:, 0:n], y[:, 0:n], y[:, 0:n], ALU.mult, [y], [sq], eng="pool")
            ps = P.bank()
            P.mm(ps[:, 0:n], ones[:], sq[:, 0:n], True, True, [ones, sq], [ps])
            P.act(r[:, 0:n], ps[:, 0:n], AF.Ln, [ps, epsb], [r], bias=epsb[:, 0:1])
            P.act(r[:, 0:n], r[:, 0:n], AF.Exp, [r], [r], scale=-0.5)
            sc = (DK_A ** -0.5) if c < 2 else 1.0
            P.stt(o[:, 0:n], y[:, 0:n], sc, r[:, 0:n], ALU.mult, ALU.mult, [y, r], [o])
            P.dma(qkv[c, :, t0:t1], o[:, 0:n], src=o)

    for ti, (t0, t1) in enumerate(tiles):
        n = t1 - t0
        xt = xts[ti % 2]
        hT = hTs[ti % 2]
        P.dma(xt[:, :, 0:n], xv[:, :, t0:t1], dst=xt)
        ss = P.bank()
        for k in range(8):
            sq = sqk[k % 2]
            P.act(sq[:, 0:n], xt[:, k, 0:n], AF.Square, [xt], [sq])
            P.mm(ss[:, 0:n], ones[:], sq[:, 0:n], k == 0, k == 7, [ones, sq], [ss])
        P.act(rstd[:, 0:n], ss[:, 0:n], AF.Ln, [ss, epsb], [rstd], bias=epsb[:, 0:1], scale=1.0 / D)
        P.act(rstd[:, 0:n], rstd[:, 0:n], AF.Exp, [rstd], [rstd], scale=-0.5)
        gi = 2 if ti == 0 else 0
        for k in range(8):
            tm = tmp[k % 2]
            P.tt(tm[:, 0:n], xt[:, k, 0:n], rstd[:, 0:n], ALU.mult, [xt, rstd], [tm])
            P.act(hT[:, k, 0:n], tm[:, 0:n], AF.Identity, [tm, GS], [hT],
                  scale=GS[:, gi, k:k + 1], bias=GS[:, gi + 1, k:k + 1])
        for c in range(NCH):
            ps = P.bank()
            for k in range(8):
                P.mm(ps[:, 0:n], W16[:, k, c * 128:(c + 1) * 128], hT[:, k, 0:n], k == 0, k == 7,
                     [W16, hT], [ps])
            if c < 6:
                xb = XB[c][ti % 3]
                P.copy(xb[:, 2:2 + n], ps[:, 0:n], [ps], [xb], eng=("act" if c % 2 else "dve"))
                if ti in (0, 1):
                    P.memset(xb[:, 0:2], 0.0, [xb], eng="pool")
                if ti == 0 or ti == last_ti:
                    P.memset(xb[:, 2 + n:4 + n], 0.0, [xb], eng="pool")
                if ti >= 2:
                    pv = XB[c][(ti - 1) % 3]
                    P.copy(pv[:, 514:516], xb[:, 2:4], [xb], [pv], eng="pool")
                if 1 <= ti < last_ti:
                    nx = XB[c][(ti + 1) % 3]
                    P.copy(nx[:, 0:2], xb[:, n:n + 2], [xb], [nx], eng="pool")
            else:
                ev = evs[cnt["ev"] % 4]
                P.copy(ev[:, 0:n], ps[:, 0:n], [ps], [ev], eng=("act" if cnt["ev"] % 2 else "dve"))
                cnt["ev"] += 1
                P.dma(pT[(c - 6) * 128:(c - 5) * 128, t0:t1], ev[:, 0:n], src=ev)
        ps = P.bank()
        for k in range(8):
            P.mm(ps[:, 0:n], W16[:, k, NCH * 128:NCH * 128 + 128], hT[:, k, 0:n], k == 0, k == 7,
                 [W16, hT], [ps])
        P.act(gb[0:4, 0:n], ps[0:4, 0:n], AF.Sigmoid, [ps], [gb])
        P.dma(gout["beta"][:, t0:t1], gb[0:4, 0:n], src=gb)
        P.act(ga[32:36, 0:n], ps[32:36, 0:n], AF.Exp, [ps, gp], [ga], bias=gp[32:36, 1:2])
        P.act(ga[32:36, 0:n], ga[32:36, 0:n], AF.Ln, [ga], [ga], bias=1.0)
        P.ts(gb[32:36, 0:n], ga[32:36, 0:n], gp2[32:36, 1:2], None, ALU.mult, None, [ga, gp2], [gb])
        P.dma(gout["g"][:, t0:t1], gb[32:36, 0:n], src=gb)
        P.ts(gb[64:66, 0:n], ps[64:66, 0:n], gp[64:66, 2:3], None, ALU.add, None, [ps, gp], [gb])
        P.dma(gout["ig"][:, t0:t1], gb[64:66, 0:n], src=gb)
        P.act(ga[96:98, 0:n], ps[96:98, 0:n], AF.Exp, [ps, gp2], [ga], bias=gp2[96:98, 2:3], scale=-1.0)
        P.act(ga[96:98, 0:n], ga[96:98, 0:n], AF.Ln, [ga], [ga], bias=1.0)
        P.ts(gb[96:98, 0:n], ga[96:98, 0:n], -1.0, None, ALU.mult, None, [ga], [gb])
        P.dma(gout["lf"][:, t0:t1], gb[96:98, 0:n], src=gb)
        if ti == 0:
            conv_tile(0)
        elif ti >= 2:
            conv_tile(ti - 1)
    if last_ti >= 1:
        conv_tile(last_ti)
    return P.finish()


def stage_projconv(nc, xT_b, w_in_l, norm_g_l, mod_l, conv_w_l, a_log_l, dt_bias_l, i_bias_l, f_bias_l):
    maps = []
    for j in range(NCORES):
        b, g = divmod(j, 4)
        cols, gates = core_cols(g)
        Wg = np.zeros((D, NWPC), np.float32)
        Wg[:, :NCH * 128] = w_in_l[:, cols]
        gb0 = NCH * 128
        Wg[:, gb0 + 0:gb0 + 4] = w_in_l[:, gates[0:4]]
        Wg[:, gb0 + 32:gb0 + 36] = w_in_l[:, gates[4:8]]
        Wg[:, gb0 + 64:gb0 + 66] = w_in_l[:, gates[8:10]]
        Wg[:, gb0 + 96:gb0 + 98] = w_in_l[:, gates[10:12]]
        shift, scale = mod_l[b][0:D], mod_l[b][D:2 * D]
        shift_c, scale_c = mod_l[2][0:D], mod_l[2][D:2 * D]
        prm = np.stack([feat_major(norm_g_l), feat_major(scale), feat_major(shift),
                        feat_major(scale_c), feat_major(shift_c)], axis=1)
        hA = (2 * g, 2 * g + 1)
        chans = []
        for base in (0, 1024, 2048):
            for h in hA:
                chans.append(np.arange(base + h * 128, base + (h + 1) * 128))
        cw = np.stack([conv_w_l[:, ch].T for ch in chans], axis=1)
        gprm = np.zeros((128, 4), np.float32)
        k = 0
        for d in range(2):
            for h in hA:
                gprm[32 + k, 0] = a_log_l[d, h]
                gprm[32 + k, 1] = dt_bias_l[d, h]
                k += 1
        for d in range(2):
            gprm[64 + d, 2] = i_bias_l[d, g]
            gprm[96 + d, 3] = f_bias_l[d, g]
        maps.append({"xT": xT_b[b], "Wg": Wg, "prm": np.ascontiguousarray(prm),
                     "cw": np.ascontiguousarray(cw), "gprm": gprm})
    if nc is None:
        return maps
    res = run(nc, maps)
    proj_res, conv_res = [], []
    for j in range(NCORES):
        r = res[j]
        pT_full = np.concatenate([np.zeros((6 * 128, TT), np.float32), r["pT"]], axis=0)
        proj_res.append({"pT": pT_full})
        conv_res.append({k: r[k] for k in ("qkv", "beta", "g", "ig", "lf")})
    return proj_res, conv_res
```

```python
import numpy as np
from contextlib import ExitStack
import concourse.bass as bass
import concourse.mybir as mybir
from concourse.bass_utils import run_bass_kernel_spmd

F32 = mybir.dt.float32
BF16 = mybir.dt.bfloat16
F32R = mybir.dt.float32r
AF = mybir.ActivationFunctionType
ALU = mybir.AluOpType
AX = mybir.AxisListType

EP = 20000


class Buf:
    def __init__(self, prog, t, name, space):
        self.p = prog
        self.t = t
        self.name = name
        self.space = space
        self.w = None
        self.r = []
        self.sem_in = None
        self.n_in = 0
        self.sem_out = None
        self.n_out = 0
        self.acc = {}

    def __getitem__(self, idx):
        return self.t[idx]


class Prog:
    ENGS = ("pe", "act", "dve", "pool", "sp")

    def __init__(self):
        self.nc = bass.Bass("TRN2", target_bir_lowering=False)
        self.st = ExitStack()
        self.ops = {e: [] for e in self.ENGS}
        self.cnt = {e: 0 for e in self.ENGS}
        self.sems = {e: [] for e in self.ENGS}
        self.known = {e: {} for e in self.ENGS}
        self.snap = {}
        self.nsem = 0
        self.out_deps = []
        self.psum_banks = []
        self.psum_i = 0

    def sem(self, name):
        self.nsem += 1
        return self.st.enter_context(self.nc.semaphore(name))

    def dram(self, name, shape, dtype, kind):
        return self.nc.dram_tensor(name, list(shape), dtype, kind=kind).ap()

    def dram_buf(self, name, shape, dtype, addr_space=None):
        if addr_space is not None:
            t = self.nc.dram_tensor(name, list(shape), dtype, kind="Internal", addr_space=addr_space).ap()
        else:
            t = self.nc.dram_tensor(name, list(shape), dtype, kind="Internal").ap()
        return Buf(self, t, name, "dram")

    def sbuf(self, name, shape, dtype=F32):
        t = self.st.enter_context(self.nc.sbuf_tensor(name, list(shape), dtype))
        return Buf(self, t, name, "sbuf")

    def psum(self, name, shape, dtype=F32):
        t = self.st.enter_context(self.nc.psum_tensor(name, list(shape), dtype))
        return Buf(self, t, name, "psum")

    def alloc_psum_banks(self, n=8):
        self.psum_banks = [self.psum(f"bank{i}", [128, 512], F32) for i in range(n)]

    def bank(self):
        b = self.psum_banks[self.psum_i % len(self.psum_banks)]
        self.psum_i += 1
        return b

    def _eng_sem(self, e, idx):
        ep = (idx - 1) // EP
        while len(self.sems[e]) <= ep:
            self.sems[e].append(self.sem(f"s_{e}_{len(self.sems[e])}"))
        return self.sems[e][ep], (idx - 1) % EP + 1

    def _collect(self, eng, reads, writes):
        deps = []
        for b in reads:
            if b.w is not None:
                deps.append(b.w)
        for b in writes:
            if b.w is not None:
                deps.append(b.w)
            deps.extend(b.r)
        for b in list(reads) + list(writes):
            if b.space == "psum":
                for e2, i2 in b.acc.items():
                    if e2 != eng:
                        deps.append(("eng", e2, i2, "p"))
        waits = {}
        for d in deps:
            if d[0] == "eng":
                _, e, idx, kind = d
                if e == eng:
                    continue
                s, v = self._eng_sem(e, idx)
            else:
                _, s, v = d
            key = id(s)
            if key not in waits or waits[key][1] < v:
                waits[key] = (s, v)
        if eng in ("act", "dve", "pool"):
            m = 0
            for b in reads:
                if b.w is not None and b.w[0] == "eng" and b.w[1] == eng:
                    m = max(m, b.w[2])
            if m:
                s, v = self._eng_sem(eng, m)
                key = id(s)
                if key not in waits or waits[key][1] < v:
                    waits[key] = (s, v)
        out = []
        kn = self.known[eng]
        for key, (s, v) in sorted(waits.items(), key=lambda kv: -kv[1][1]):
            if kn.get(key, 0) >= v:
                continue
            kn[key] = v
            out.append((s, v))
            sn = self.snap.get((key, v))
            if sn:
                for k2, v2 in sn.items():
                    if kn.get(k2, 0) < v2:
                        kn[k2] = v2
        return out

    def op(self, eng, fn, reads=(), writes=()):
        waits = self._collect(eng, reads, writes)
        self.cnt[eng] += 1
        idx = self.cnt[eng]
        s, v = self._eng_sem(eng, idx)
        self.ops[eng].append((waits, fn, (s, 1)))
        self.snap[(id(s), v)] = dict(self.known[eng])
        dep = ("eng", eng, idx, "c")
        for b in list(reads) + list(writes):
            if b.space == "psum":
                b.acc[eng] = idx
        for b in writes:
            b.w = dep
            b.r = []
        for b in reads:
            if b not in writes:
                b.r.append(dep)
        return idx

    def dma(self, out_ap, in_ap, dst=None, src=None, q="sp"):
        reads = [src] if src is not None else []
        writes = [dst] if dst is not None else []
        waits = self._collect(q, reads, writes)
        if dst is not None:
            if dst.sem_in is None:
                dst.sem_in = self.sem(f"di_{dst.name}")
            dst.n_in += 1
            s, v = dst.sem_in, 16 * dst.n_in
        elif src is not None:
            if src.sem_out is None:
                src.sem_out = self.sem(f"do_{src.name}")
            src.n_out += 1
            s, v = src.sem_out, 16 * src.n_out
        else:
            raise ValueError("dma needs a tracked side")
        dep = ("dma", s, v)
        self.snap[(id(s), v)] = dict(self.known[q])
        self.ops[q].append((waits, lambda e, o=out_ap, i=in_ap: e.dma_start(out=o, in_=i), (s, 16)))
        if dst is not None:
            dst.w = dep
            dst.r = []
        if src is not None:
            src.r.append(dep)
        if dst is None:
            self.out_deps.append(dep)
        return dep

    def collective(self, kind, src, dst, replica_groups, q="pool"):
        waits = self._collect(q, [src], [dst])
        if dst.sem_in is None:
            dst.sem_in = self.sem(f"di_{dst.name}")
        dst.n_in += 1
        s, v = dst.sem_in, 16 * dst.n_in
        dep = ("dma", s, v)
        self.ops[q].append((waits, lambda e: e.collective_compute(
            kind, ALU.bypass, replica_groups=replica_groups, ins=[src.t[:]], outs=[dst.t[:]]), (s, 16)))
        dst.w = dep
        dst.r = []
        src.r.append(dep)
        return dep

    def mm(self, out, lhsT, rhs, start, stop, reads, writes):
        return self.op("pe", lambda e: e.matmul(out, lhsT, rhs, start=start, stop=stop),
                       reads, writes)

    def transpose(self, out, in_, ident, reads, writes):
        return self.op("pe", lambda e: e.transpose(out, in_, ident), reads, writes)

    def act(self, out, in_, func, reads, writes, bias=None, scale=None, accum_out=None, eng="act"):
        kw = {}
        if bias is not None:
            kw["bias"] = bias
        if scale is not None:
            kw["scale"] = scale
        if accum_out is not None:
            kw["accum_out"] = accum_out
        return self.op("act", lambda e: e.activation(out, in_, func, **kw), reads, writes)

    def tt(self, out, in0, in1, op, reads, writes, eng="dve"):
        return self.op(eng, lambda e: e.tensor_tensor(out, in0, in1, op), reads, writes)

    def ts(self, out, in0, s1, s2, op0, op1, reads, writes, eng="dve"):
        if op1 is None:
            return self.op(eng, lambda e: e.tensor_scalar(out, in0, s1, None, op0), reads, writes)
        return self.op(eng, lambda e: e.tensor_scalar(out, in0, s1, s2, op0, op1), reads, writes)

    def stt(self, out, in0, scalar, in1, op0, op1, reads, writes):
        return self.op("dve", lambda e: e.scalar_tensor_tensor(out, in0, scalar, in1, op0, op1),
                       reads, writes)

    def copy(self, out, in_, reads, writes, eng="dve"):
        if eng == "act":
            return self.op("act", lambda e: e.copy(out, in_), reads, writes)
        return self.op(eng, lambda e: e.tensor_copy(out, in_), reads, writes)

    def memset(self, out, val, writes, eng="dve"):
        return self.op(eng, lambda e: e.memset(out, val), (), writes)

    def finish(self):
        nc = self.nc
        fin_waits = {}
        for d in self.out_deps:
            _, s, v = d
            if id(s) not in fin_waits or fin_waits[id(s)][1] < v:
                fin_waits[id(s)] = (s, v)
        for e in self.ENGS:
            if e == "sp" or self.cnt[e] == 0:
                continue
            s, v = self._eng_sem(e, self.cnt[e])
            fin_waits[id(s)] = (s, v)
        engmap = {"pe": "tensor", "act": "scalar", "dve": "vector", "pool": "gpsimd", "sp": "sync"}
        with nc.Block() as block:
            for e in self.ENGS:
                ops = self.ops[e]
                last = (e == "sp")
                if not ops and not last:
                    continue

                def body(eng, ops=ops, last=last):
                    for waits, fn, (s, n) in ops:
                        for (ws, wv) in waits:
                            eng.wait_ge(ws, wv)
                        fn(eng).then_inc(s, n)
                    if last:
                        for (ws, wv) in fin_waits.values():
                            eng.wait_ge(ws, wv)

                getattr(block, engmap[e])(body)
        self.st.close()
        return nc


D = 1024
BATCH = 2
SEQ = 8192
CTX = 256
TT = CTX + SEQ
H_A, DK_A, DV_A = 8, 128, 128
H_B, DQK_B, DV_B = 4, 128, 256
W_A = 1024
W_B = 1024
GRID_W = 64
EPS = 1e-6
NCORES = 8
O_AQ, O_AK, O_AV, O_AZ = 0, 1024, 2048, 3072
O_ABETA, O_AALPHA = 4096, 4112
O_BQ, O_BK, O_BV, O_BO, O_BZ = 4128, 4640, 5152, 6176, 7200
O_BI, O_BF = 8224, 8232
O_GA, O_GB = 8240, 9264
NCH = 20
NGATE = 12


def core_cols(g):
    cols = []
    hA = (2 * g, 2 * g + 1)
    for base in (O_AQ, O_AK, O_AV, O_AZ):
        for h in hA:
            cols.extend(range(base + h * 128, base + (h + 1) * 128))
    for base in (O_BQ, O_BK):
        cols.extend(range(base + g * 128, base + (g + 1) * 128))
    for base in (O_BV, O_BO, O_BZ):
        cols.extend(range(base + g * 256, base + (g + 1) * 256))
    for base in (O_GA, O_GB):
        cols.extend(range(base + g * 256, base + (g + 1) * 256))
    assert len(cols) == NCH * 128
    gates = []
    for base in (O_ABETA, O_AALPHA):
        for d in range(2):
            for h in hA:
                gates.append(base + d * H_A + h)
    for base in (O_BI, O_BF):
        for d in range(2):
            gates.append(base + d * H_B + g)
    assert len(gates) == NGATE
    return cols, gates


CH_AQ, CH_AK, CH_AV, CH_AZ = (0, 1), (2, 3), (4, 5), (6, 7)
CH_BQ, CH_BK = 8, 9
CH_BV, CH_BO, CH_BZ = (10, 11), (12, 13), (14, 15)
CH_GA, CH_GB = (16, 17), (18, 19)


LAST_EXEC_NS = [None]


def run(nc, in_maps, trace=False):
    if trace:
        res = run_bass_kernel_spmd(nc, in_maps, core_ids=list(range(NCORES)), trace=True)
    else:
        res = run_bass_kernel_spmd(nc, in_maps, core_ids=list(range(NCORES)))
    LAST_EXEC_NS[0] = getattr(res, "exec_time_ns", None)
    return res.results


def build_mod():
    P = Prog()
    W = P.dram("W", [6, D, 128], F32, "ExternalInput")
    cc = P.dram("cc", [128, 8, 4], F32, "ExternalInput")
    bias = P.dram("bias", [128, 6], F32, "ExternalInput")
    out = P.dram("out", [6, 128, 4], F32, "ExternalOutput")
    P.alloc_psum_banks(2)
    ccs = P.sbuf("ccs", [128, 8, 4])
    scs = P.sbuf("scs", [128, 8, 4])
    bs = P.sbuf("bs", [128, 6])
    os_ = P.sbuf("os", [128, 6, 4])
    P.dma(ccs[:], cc[:, :, :], dst=ccs)
    P.dma(bs[:], bias[:, :], dst=bs)
    P.act(scs[:], ccs[:], AF.Silu, [ccs], [scs])
    wts = [P.sbuf(f"w{i}", [128, 8, 128]) for i in range(2)]
    for i in range(6):
        wt = wts[i % 2]
        P.dma(wt[:], W[i].rearrange("(k p) c -> p k c", p=128), dst=wt)
        ps = P.bank()
        for k in range(8):
            P.mm(ps[:, 0:4], wt[:, k, :], scs[:, k, :], k == 0, k == 7, [wt, scs], [ps])
        P.ts(os_[:, i, :], ps[:, 0:4], bs[:, i:i + 1], None, ALU.add, None, [ps, bs], [os_])
    P.dma(out.rearrange("i p n -> p i n"), os_[:], src=os_)
    return P.finish()


def stage_mod(inp):
    nc = build_mod()
    c4 = np.stack([inp["c"][0], inp["c"][1], inp["c_ctx"], inp["c_ctx"]], axis=-1)
    cc = np.ascontiguousarray(c4.reshape(8, 128, 4).transpose(1, 0, 2))
    maps = []
    for j in range(NCORES):
        Ws, bs = [], []
        for i in range(6):
            job = j * 6 + i
            l, fc = divmod(job, 24)
            Ws.append(inp["ada_w"][l][:, fc * 128:(fc + 1) * 128])
            bs.append(inp["ada_b"][l][fc * 128:(fc + 1) * 128])
        maps.append({"W": np.ascontiguousarray(np.stack(Ws)), "cc": cc,
                     "bias": np.ascontiguousarray(np.stack(bs, axis=1))})
    res = run(nc, maps)
    mod = np.zeros((2, 3, 3 * D), np.float32)
    for j in range(NCORES):
        o = res[j]["out"]
        for i in range(6):
            job = j * 6 + i
            l, fc = divmod(job, 24)
            for n in range(3):
                mod[l, n, fc * 128:(fc + 1) * 128] = o[i, :, n]
    return mod


P_TILES = [(0, CTX)] + [(CTX + i * 512, CTX + (i + 1) * 512) for i in range(SEQ // 512)]


def build_proj(ntiles=None):
    P = Prog()
    xT = P.dram("xT", [D, TT], F32, "ExternalInput")
    Wg = P.dram("Wg", [D, NCH * 128 + 16], F32, "ExternalInput")
    prm = P.dram("prm", [128, 5, 8], F32, "ExternalInput")
    pT = P.dram("pT", [NCH * 128, TT], F32, "ExternalOutput")
    gT = P.dram("gT", [16, TT], F32, "ExternalOutput")
    P.alloc_psum_banks(8)
    NW = NCH * 128 + 16
    W16 = P.sbuf("W16", [128, 8, NW], BF16)
    ones = P.sbuf("ones", [128, 128])
    P.memset(ones[:], 1.0, [ones])
    prms = P.sbuf("prms", [128, 5, 8])
    P.dma(prms[:], prm[:, :, :], dst=prms)
    epsb = P.sbuf("epsb", [128, 1])
    P.memset(epsb[:], EPS, [epsb])
    GS = P.sbuf("GS", [128, 4, 8])
    P.stt(GS[:, 0, :], prms[:, 1, :], 1.0, prms[:, 0, :], ALU.add, ALU.mult, [prms], [GS])
    P.copy(GS[:, 1, :], prms[:, 2, :], [prms], [GS])
    P.stt(GS[:, 2, :], prms[:, 3, :], 1.0, prms[:, 0, :], ALU.add, ALU.mult, [prms], [GS])
    P.copy(GS[:, 3, :], prms[:, 4, :], [prms], [GS])
    Wv = Wg.rearrange("(k p) c -> p k c", p=128)
    stg = [P.sbuf(f"wst{i}", [128, 8, 512]) for i in range(2)]
    c0 = 0
    i = 0
    while c0 < NW:
        c1 = min(NW, c0 + 512)
        s = stg[i % 2]
        P.dma(s[:, :, 0:c1 - c0], Wv[:, :, c0:c1], dst=s)
        P.copy(W16[:, :, c0:c1], s[:, :, 0:c1 - c0], [s], [W16], eng=("dve" if i % 2 == 0 else "pool"))
        c0 = c1
        i += 1
    xv = xT.rearrange("(k p) t -> p k t", p=128)
    xts = [P.sbuf(f"xt{i}", [128, 8, 512]) for i in range(2)]
    sq = P.sbuf("sq", [128, 8, 512])
    hTs = [P.sbuf(f"hT{i}", [128, 8, 512], BF16) for i in range(2)]
    rstd = P.sbuf("rstd", [128, 512])
    tmp = [P.sbuf(f"tmp{i}", [128, 512]) for i in range(2)]
    evs = [P.sbuf(f"ev{i}", [128, 512]) for i in range(4)]
    gev = P.sbuf("gev", [16, 512])
    nev = 0
    for ti, (t0, t1) in enumerate(P_TILES[:ntiles] if ntiles else P_TILES):
        n = t1 - t0
        xt = xts[ti % 2]
        hT = hTs[ti % 2]
        P.dma(xt[:, :, 0:n], xv[:, :, t0:t1], dst=xt)
        P.act(sq[:, :, 0:n], xt[:, :, 0:n], AF.Square, [xt], [sq])
        ss = P.bank()
        for k in range(8):
            P.mm(ss[:, 0:n], ones[:], sq[:, k, 0:n], k == 0, k == 7, [ones, sq], [ss])
        P.act(rstd[:, 0:n], ss[:, 0:n], AF.Ln, [ss, epsb], [rstd], bias=epsb[:, 0:1], scale=1.0 / D)
        P.act(rstd[:, 0:n], rstd[:, 0:n], AF.Exp, [rstd], [rstd], scale=-0.5)
        gi = 2 if ti == 0 else 0
        for k in range(8):
            tm = tmp[k % 2]
            P.tt(tm[:, 0:n], xt[:, k, 0:n], rstd[:, 0:n], ALU.mult, [xt, rstd], [tm])
            P.act(hT[:, k, 0:n], tm[:, 0:n], AF.Identity, [tm, GS], [hT],
                  scale=GS[:, gi, k:k + 1], bias=GS[:, gi + 1, k:k + 1])
        for c in range(NCH):
            ps = P.bank()
            for k in range(8):
                P.mm(ps[:, 0:n], W16[:, k, c * 128:(c + 1) * 128], hT[:, k, 0:n], k == 0, k == 7,
                     [W16, hT], [ps])
            ev = evs[nev % 4]
            P.copy(ev[:, 0:n], ps[:, 0:n], [ps], [ev], eng=("act" if nev % 2 else "dve"))
            nev += 1
            P.dma(pT[c * 128:(c + 1) * 128, t0:t1], ev[:, 0:n], src=ev)
        ps = P.bank()
        for k in range(8):
            P.mm(ps[0:16, 0:n], W16[:, k, NCH * 128:NCH * 128 + 16], hT[:, k, 0:n], k == 0, k == 7,
                 [W16, hT], [ps])
        P.copy(gev[:, 0:n], ps[0:16, 0:n], [ps], [gev])
        P.dma(gT[:, t0:t1], gev[:, 0:n], src=gev)
    return P.finish()


def feat_major(v):
    return np.ascontiguousarray(v.reshape(8, 128).T)


def stage_proj(nc, xT_b, w_in_l, norm_g_l, mod_l):
    maps = []
    for j in range(NCORES):
        b, g = divmod(j, 4)
        cols, gates = core_cols(g)
        Wg = np.zeros((D, NCH * 128 + 16), np.float32)
        Wg[:, :NCH * 128] = w_in_l[:, cols]
        Wg[:, NCH * 128:NCH * 128 + NGATE] = w_in_l[:, gates]
        shift, scale = mod_l[b][0:D], mod_l[b][D:2 * D]
        shift_c, scale_c = mod_l[2][0:D], mod_l[2][D:2 * D]
        prm = np.stack([feat_major(norm_g_l), feat_major(scale), feat_major(shift),
                        feat_major(scale_c), feat_major(shift_c)], axis=1)
        maps.append({"xT": xT_b[b], "Wg": Wg, "prm": np.ascontiguousarray(prm)})
    return run(nc, maps)


CPAD = (CTX + 4) + (SEQ + 4)
C_TILES = [(0, 0, CTX)] + [(CTX + 4, i * 512, 512) for i in range(SEQ // 512)]


def build_conv():
    P = Prog()
    pre = P.dram("pre", [6, 128, CPAD], F32, "ExternalInput")
    cw = P.dram("cw", [128, 6, 5], F32, "ExternalInput")
    gin = {nm: P.dram(nm, [r, TT], F32, "ExternalInput") for nm, r in
           (("betaT", 4), ("alphaT", 4), ("iT", 2), ("fT", 2))}
    gprm = P.dram("gprm", [4, 4], F32, "ExternalInput")
    qkv = P.dram("qkv", [6, 128, TT], F32, "ExternalOutput")
    gout = {nm: P.dram(nm, [r, TT], F32, "ExternalOutput") for nm, r in
            (("beta", 4), ("g", 4), ("ig", 2), ("lf", 2))}
    P.alloc_psum_banks(4)
    ones = P.sbuf("ones", [128, 128])
    P.memset(ones[:], 1.0, [ones])
    cws = P.sbuf("cws", [128, 6, 5])
    P.dma(cws[:], cw[:, :, :], dst=cws)
    epsb = P.sbuf("epsb", [128, 1])
    P.memset(epsb[:], EPS, [epsb])
    gp = P.sbuf("gp", [4, 4])
    P.dma(gp[:], gprm[:, :], dst=gp)
    gp2 = P.sbuf("gp2", [4, 4])
    P.act(gp2[:, 0:1], gp[:, 0:1], AF.Exp, [gp], [gp2])
    P.ts(gp2[:, 1:2], gp2[:, 0:1], -1.0, None, ALU.mult, None, [gp2], [gp2])
    P.ts(gp2[0:2, 2:3], gp[0:2, 3:4], -1.0, None, ALU.mult, None, [gp], [gp2])
    g1 = P.sbuf("g1", [4, TT])
    g2 = P.sbuf("g2", [4, TT])
    P.dma(g1[:], gin["betaT"][:, :], dst=g1)
    P.act(g2[:], g1[:], AF.Sigmoid, [g1], [g2])
    P.dma(gout["beta"][:, :], g2[:], src=g2)
    P.dma(g1[:], gin["alphaT"][:, :], dst=g1)
    P.act(g1[:], g1[:], AF.Exp, [g1, gp], [g1], bias=gp[:, 1:2])
    P.act(g1[:], g1[:], AF.Ln, [g1], [g1], bias=1.0)
    P.ts(g2[:], g1[:], gp2[:, 1:2], None, ALU.mult, None, [g1, gp2], [g2])
    P.dma(gout["g"][:, :], g2[:], src=g2)
    P.dma(g1[0:2, :], gin["iT"][:, :], dst=g1)
    P.ts(g2[0:2, :], g1[0:2, :], gp[0:2, 2:3], None, ALU.add, None, [g1, gp], [g2])
    P.dma(gout["ig"][:, :], g2[0:2, :], src=g2)
    P.dma(g1[0:2, :], gin["fT"][:, :], dst=g1)
    P.act(g1[0:2, :], g1[0:2, :], AF.Exp, [g1, gp2], [g1], bias=gp2[0:2, 2:3], scale=-1.0)
    P.act(g1[0:2, :], g1[0:2, :], AF.Ln, [g1], [g1], bias=1.0)
    P.ts(g2[0:2, :], g1[0:2, :], -1.0, None, ALU.mult, None, [g1], [g2])
    P.dma(gout["lf"][:, :], g2[0:2, :], src=g2)
    NBUF = 2
    xin = [P.sbuf(f"xin{i}", [128, 516]) for i in range(6 * NBUF)]
    acc = [P.sbuf(f"acc{i}", [128, 512]) for i in range(3)]
    ys = [P.sbuf(f"y{i}", [128, 512]) for i in range(6 * NBUF)]
    sqs = [P.sbuf(f"sq{i}", [128, 512]) for i in range(4)]
    rs = [P.sbuf(f"r{i}", [128, 512]) for i in range(4)]
    outs = [P.sbuf(f"o{i}", [128, 512]) for i in range(6)]
    it = 0
    io = 0
    for ti, (off, t0, n) in enumerate(C_TILES):
        tok0 = (0 if off == 0 else CTX) + t0
        for c in range(6):
            xi = xin[(ti % NBUF) * 6 + c]
            a = acc[it % 3]
            y = ys[(ti % NBUF) * 6 + c]
            P.dma(xi[:, 0:n + 4], pre[c, :, off + t0:off + t0 + n + 4], dst=xi)
            P.ts(a[:, 0:n], xi[:, 0:n], cws[:, c, 0:1], None, ALU.mult, None, [xi, cws], [a])
            for s in range(1, 5):
                P.stt(a[:, 0:n], xi[:, s:s + n], cws[:, c, s:s + 1], a[:, 0:n], ALU.mult, ALU.add,
                      [xi, cws, a], [a])
            if c < 4:
                P.act(y[:, 0:n], a[:, 0:n], AF.Silu, [a], [y])
            else:
                o = outs[io % 6]
                io += 1
                P.act(o[:, 0:n], a[:, 0:n], AF.Silu, [a], [o])
                P.dma(qkv[c, :, tok0:tok0 + n], o[:, 0:n], src=o)
            it += 1
        for c in range(4):
            y = ys[(ti % NBUF) * 6 + c]
            sq = sqs[c]
            r = rs[c]
            o = outs[io % 6]
            io += 1
            P.tt(sq[:, 0:n], y[:, 0:n], y[:, 0:n], ALU.mult, [y], [sq], eng="pool")
            ps = P.bank()
            P.mm(ps[:, 0:n], ones[:], sq[:, 0:n], True, True, [ones, sq], [ps])
            P.act(r[:, 0:n], ps[:, 0:n], AF.Ln, [ps, epsb], [r], bias=epsb[:, 0:1])
            P.act(r[:, 0:n], r[:, 0:n], AF.Exp, [r], [r], scale=-0.5)
            sc = (DK_A ** -0.5) if c < 2 else 1.0
            P.stt(o[:, 0:n], y[:, 0:n], sc, r[:, 0:n], ALU.mult, ALU.mult, [y, r], [o])
            P.dma(qkv[c, :, tok0:tok0 + n], o[:, 0:n], src=o)
    return P.finish()


def pad_seq(a):
    z = np.zeros(a.shape[:-1] + (2,), a.dtype)
    return np.ascontiguousarray(np.concatenate([z, a[..., :CTX], z, z, a[..., CTX:], z], axis=-1))


def stage_conv(nc, proj_res, conv_w_l, a_log_l, dt_bias_l, i_bias_l, f_bias_l):
    maps = []
    for j in range(NCORES):
        b, g = divmod(j, 4)
        pT = proj_res[j]["pT"]
        gT = proj_res[j]["gT"]
        pre = pad_seq(pT[0:768].reshape(6, 128, TT))
        hA = (2 * g, 2 * g + 1)
        chans = []
        for base in (0, 1024, 2048):
            for h in hA:
                chans.append(np.arange(base + h * 128, base + (h + 1) * 128))
        cw = np.stack([conv_w_l[:, ch].T for ch in chans], axis=1)
        gprm = np.zeros((4, 4), np.float32)
        k = 0
        for d in range(2):
            for h in hA:
                gprm[k, 0] = a_log_l[d, h]
                gprm[k, 1] = dt_bias_l[d, h]
                k += 1
        for d in range(2):
            gprm[d, 2] = i_bias_l[d, g]
            gprm[d, 3] = f_bias_l[d, g]
        maps.append({"pre": pre, "cw": np.ascontiguousarray(cw), "gprm": gprm,
                     "betaT": np.ascontiguousarray(gT[0:4]), "alphaT": np.ascontiguousarray(gT[4:8]),
                     "iT": np.ascontiguousarray(gT[8:10]), "fT": np.ascontiguousarray(gT[10:12])})
    return run(nc, maps)


class View:
    def __init__(self, buf, ap):
        self.buf = buf
        self.ap = ap

    def __getitem__(self, idx):
        return View(self.buf, self.ap[idx])


def V(buf, *idx):
    if not idx:
        return View(buf, buf.t[:])
    return View(buf, buf.t[idx if len(idx) > 1 else idx[0]])


def _u(x):
    return x.ap if isinstance(x, View) else x


def _b(*xs):
    out = []
    for x in xs:
        if isinstance(x, View) and x.buf not in out:
            out.append(x.buf)
    return out


class AProg(Prog):
    def quarters(self, nbanks=8):
        self.alloc_psum_banks(nbanks)
        self.qtiles = []
        for b in self.psum_banks:
            for q in range(4):
                self.qtiles.append(Buf(self, b.t[:, q * 128:(q + 1) * 128], f"{b.name}q{q}", "psum"))
        self.qi = 0

    def q(self):
        t = self.qtiles[self.qi % len(self.qtiles)]
        self.qi += 1
        return V(t)

    def amm(self, out, lhsT, rhs, start=True, stop=True):
        return self.mm(_u(out), _u(lhsT), _u(rhs), start, stop, _b(lhsT, rhs), _b(out))

    def atr(self, out, in_, ident):
        return self.transpose(_u(out), _u(in_), _u(ident), _b(in_, ident), _b(out))

    def aact(self, out, in_, func, bias=None, scale=None):
        return self.act(_u(out), _u(in_), func, _b(in_, bias, scale), _b(out),
                        bias=_u(bias) if bias is not None else None,
                        scale=_u(scale) if scale is not None else None)

    def att(self, out, in0, in1, op, eng="dve"):
        return self.tt(_u(out), _u(in0), _u(in1), op, _b(in0, in1), _b(out), eng=eng)

    def ats(self, out, in0, s1, op0, s2=None, op1=None, eng="dve"):
        return self.ts(_u(out), _u(in0), _u(s1), _u(s2), op0, op1, _b(in0, s1, s2), _b(out), eng=eng)

    def astt(self, out, in0, scalar, in1, op0, op1):
        return self.stt(_u(out), _u(in0), _u(scalar), _u(in1), op0, op1, _b(in0, scalar, in1), _b(out))

    def acopy(self, out, in_, eng="dve"):
        return self.copy(_u(out), _u(in_), _b(in_), _b(out), eng=eng)

    def amemset(self, out, val, eng="dve"):
        return self.memset(_u(out), val, _b(out), eng=eng)

    def aload(self, dst, src_ap, q="sp"):
        return self.dma(_u(dst), src_ap, dst=dst.buf, q=q)

    def astore(self, dst_ap, src, q="sp"):
        return self.dma(dst_ap, _u(src), src=src.buf, q=q)


def run_round_robin(gens):
    gens = list(gens)
    while gens:
        nxt = []
        for g in gens:
            try:
                next(g)
                nxt.append(g)
            except StopIteration:
                pass
        gens = nxt


BIG = 30000.0
NSC = TT // 128
LG = 4


def make_masks():
    i = np.arange(128)
    same = (i[:, None] // 64) == (i[None, :] // 64)
    U = ((i[:, None] <= i[None, :]) & same).astype(np.float32)
    BD = same.astype(np.float32)
    PL = np.where((i[None, :] < i[:, None]) & same, 0.0, BIG).astype(np.float32)
    NU = np.where((i[:, None] <= i[None, :]) & same, 0.0, -BIG).astype(np.float32)
    ID = np.eye(128, dtype=np.float32)
    NL = np.where((i[None, :] <= i[:, None]) & same, 0.0, -BIG).astype(np.float32)
    ON = np.ones((128, 128), np.float32)
    return np.ascontiguousarray(np.stack([U, BD, PL, NU, ID, NL, ON, -U], axis=1))


M_U, M_BD, M_PL, M_NU, M_ID, M_NL, M_ONES, M_NEGU = range(8)
NMSK = 8


def delta_problem(P, pi, msk, dr, S, o_out, banks, nsc=NSC):
    f = lambda nm, shape=(128, 128): P.sbuf(f"{nm}_{pi}", list(shape))
    fr = lambda nm, shape=(128, 128): P.sbuf(f"{nm}_{pi}", list(shape), F32R)
    qA = lambda: V(banks[0])[:, 0:128]
    qB = (lambda: V(banks[1])[:, 0:128]) if banks[1] is not banks[0] else (lambda: V(banks[0])[:, 128:256])
    qTs = [f(f"qT{i}", (128, LG * 128)) for i in range(2)]
    kTs = [f(f"kT{i}", (128, LG * 128)) for i in range(2)]
    kts = [f(f"ktm{i}", (128, LG, 128)) for i in range(2)]
    vts = [f(f"vtm{i}", (128, LG, 128)) for i in range(2)]
    bcol = f("bcol", (128, NSC))
    nbcol = f("nbcol", (128, NSC))
    gcol = f("gcol", (128, NSC, 2))
    cols = [f(f"cols{i}", (128, 8)) for i in range(2)]
    gbc = f("gbc"); RL = f("RL"); RU = f("RU"); dL = f("dL"); dT = f("dT"); eg = [f("eg0"), f("eg1")]
    BT = f("BT"); Bm = f("Bm")
    Pa = [fr("Pa0"), fr("Pa1")]; PaT = [fr("PaT0"), fr("PaT1")]
    X = [fr("X0"), fr("X1")]
    qTr = fr("qTr"); kTr = fr("kTr")
    kbg = fr("kbg"); vb = fr("vb"); kdec = [f("kdec0"), f("kdec1")]
    wT = f("wT"); u = f("u"); qkT = [f("qkT0"), f("qkT1")]; qdT = [f("qdT0"), f("qdT1")]
    vnew = f("vnew")
    osb = [f("osb0"), f("osb1")]
    P.amemset(V(vnew), 0.0)
    P.aload(V(bcol), dr["bcol"][:, pi, :])
    P.aload(V(gcol), dr["gcol"][:, pi, :, :])
    P.ats(V(nbcol), V(bcol), -1.0, ALU.mult)
    U = V(msk, slice(None), M_U, slice(None))
    BDm = V(msk, slice(None), M_BD, slice(None))
    PLm = V(msk, slice(None), M_PL, slice(None))
    NUm = V(msk, slice(None), M_NU, slice(None))
    ID = V(msk, slice(None), M_ID, slice(None))
    ONESm = V(msk, slice(None), M_ONES, slice(None))
    yield
    for sc in range(nsc):
        t0 = sc * 128
        gi_, gl_ = divmod(sc, LG)
        if gl_ == 0:
            ng = min(LG, nsc - sc)
            P.aload(V(qTs[gi_ % 2])[:, 0:ng * 128], dr["qT"][pi, :, t0:t0 + ng * 128])
            P.aload(V(kTs[gi_ % 2])[:, 0:ng * 128], dr["kT"][pi, :, t0:t0 + ng * 128])
            P.aload(V(kts[gi_ % 2])[:, 0:ng, :],
                    dr["ktm"][pi, t0:t0 + ng * 128, :].rearrange("(g p) d -> p g d", p=128))
            P.aload(V(vts[gi_ % 2])[:, 0:ng, :],
                    dr["vtm"][pi, t0:t0 + ng * 128, :].rearrange("(g p) d -> p g d", p=128))
        qT = V(qTs[gi_ % 2])[:, gl_ * 128:(gl_ + 1) * 128]
        kT = V(kTs[gi_ % 2])[:, gl_ * 128:(gl_ + 1) * 128]
        ktm = V(kts[gi_ % 2])[:, gl_, :]
        vtm = V(vts[gi_ % 2])[:, gl_, :]
        cl = V(cols[sc % 2])
        gps = qA()
        P.amm(gps[:, 0:2], U, V(gcol)[:, sc, :])
        P.amm(gps[:, 2:4], BDm, V(gcol)[:, sc, :])
        P.ats(V(gbc), ONESm, V(gcol)[:, sc, 0:1], ALU.mult, eng="pool")
        yield
        grow = qB()
        P.amm(grow, V(gbc), U)
        P.acopy(cl[:, 0:1], gps[:, 0:1])
        P.ats(cl[:, 1:2], gps[:, 0:1], -1.0, ALU.mult)
        yield
        P.aact(cl[:, 2:3], gps[:, 0:1], AF.Exp)
        P.aact(cl[:, 3:4], gps[:, 2:3], AF.Exp, bias=cl[:, 1:2])
        P.att(V(RL), grow, PLm, ALU.add)
        P.att(V(RU), grow, NUm, ALU.add)
        egr = V(eg[sc % 2])
        P.aact(egr, grow, AF.Exp)
        yield
        P.aact(V(dL), V(RL), AF.Exp, bias=cl[:, 0:1], scale=-1.0)
        P.aact(V(dT), V(RU), AF.Exp, bias=cl[:, 1:2])
        P.att(cl[:, 4:5], cl[:, 2:3], V(bcol)[:, sc:sc + 1], ALU.mult)
        P.acopy(V(kTr), kT, eng="act")
        P.acopy(V(qTr), qT, eng="act")
        kk = qA()
        P.amm(kk, V(kTr), V(kTr))
        qk = qB()
        P.amm(qk, V(kTr), V(qTr))
        yield
        P.astt(V(BT), kk, V(nbcol)[:, sc:sc + 1], V(dL), ALU.mult, ALU.mult)
        P.aact(V(kbg), ktm, AF.Identity, scale=cl[:, 4:5])
        P.aact(V(vb), vtm, AF.Identity, scale=V(bcol)[:, sc:sc + 1])
        kd = V(kdec[sc % 2])
        P.ats(kd, ktm, cl[:, 3:4], ALU.mult, eng="pool")
        qkTs = V(qkT[sc % 2])
        P.att(qkTs, qk, V(dT), ALU.mult)
        qd = V(qdT[sc % 2])
        P.att(qd, qT, egr, ALU.mult, eng="pool")
        yield
        bps = qA()
        P.atr(bps, V(BT), ID)
        yield
        P.acopy(V(Bm), bps, eng="act")
        P.att(V(X[0]), bps, ID, ALU.add)
        yield
        cur, curT = V(Bm), V(BT)
        xi = 0
        for lev in range(5):
            last = lev == 4
            p2T = qA()
            P.amm(p2T, cur, curT)
            if not last:
                p2 = qB()
                P.amm(p2, curT, cur)
            yield
            nT = V(PaT[lev % 2])
            P.acopy(nT, p2T, eng="act")
            if not last:
                n_ = V(Pa[lev % 2])
                P.acopy(n_, p2)
            yield
            xp = qA()
            P.amm(xp, nT, V(X[xi]))
            yield
            P.att(V(X[1 - xi]), xp, View(X[xi], X[xi].t[:].bitcast(F32)), ALU.add)
            xi = 1 - xi
            if not last:
                cur, curT = n_, nT
            yield
        TinvT = V(X[xi])
        wps = qA()
        P.amm(wps, V(kbg), TinvT)
        ups = qB()
        P.amm(ups, TinvT, V(vb))
        yield
        P.acopy(V(wT), wps, eng="act")
        P.acopy(V(u), ups)
        yield
        ob = V(osb[sc % 2])
        for c in range(2):
            r = slice(c * 64, c * 64 + 64)
            vps = qA()
            P.amm(vps[r, :], V(wT)[:, r], V(S))
            yield
            P.att(V(vnew)[r, :], V(u)[r, :], vps[r, :], ALU.subtract)
            yield
            ops_ = qB()
            P.amm(ops_[r, :], qd[:, r], V(S), True, False)
            P.amm(ops_[r, :], qkTs[:, r], V(vnew), False, True)
            sps = qA()
            P.amm(sps, kd[r, :], V(vnew)[r, :])
            yield
            P.astt(V(S), V(S), egr[:, c * 64 + 63:c * 64 + 64], sps, ALU.mult, ALU.add)
            P.acopy(ob[r, :], ops_[r, :], eng="act")
            yield
        P.astore(o_out[pi, t0:t0 + 128, :], ob)
        yield


def build_delta(nsc=NSC, nprob=4):
    P = AProg()
    dr = {
        "qT": P.dram("qT", [4, 128, TT], F32, "ExternalInput"),
        "kT": P.dram("kT", [4, 128, TT], F32, "ExternalInput"),
        "ktm": P.dram("ktm", [4, TT, 128], F32, "ExternalInput"),
        "vtm": P.dram("vtm", [4, TT, 128], F32, "ExternalInput"),
        "bcol": P.dram("bcol", [128, 4, NSC], F32, "ExternalInput"),
        "gcol": P.dram("gcol", [128, 4, NSC, 2], F32, "ExternalInput"),
    }
    mskd = P.dram("msk", [128, NMSK, 128], F32, "ExternalInput")
    o_out = P.dram("o", [4, TT, 128], F32, "ExternalOutput")
    P.alloc_psum_banks(8)
    msk = P.sbuf("msk_sb", [128, NMSK, 128])
    P.aload(V(msk), mskd[:, :, :])
    gens = []
    for pi in range(nprob):
        S = P.sbuf(f"S_{pi}", [128, 128])
        P.amemset(V(S), 0.0)
        gens.append(delta_problem(P, pi, msk, dr, S, o_out,
                                  (P.psum_banks[2 * pi], P.psum_banks[2 * pi + 1]), nsc))
    run_round_robin(gens)
    return P.finish()


def prob_order(d):
    ci = np.arange(CTX)
    li = CTX + np.arange(SEQ)
    if d == 1:
        ci = ci[::-1]
        li = li[::-1]
    return np.concatenate([ci, li])


def col_layout(v):
    return np.ascontiguousarray(v.reshape(NSC, 128).T)


def stage_delta(nc, conv_res):
    msk = make_masks()
    maps = []
    for j in range(NCORES):
        r = conv_res[j]
        qT, kT, ktm, vtm = [], [], [], []
        bcol = np.zeros((128, 4, NSC), np.float32)
        gcol = np.zeros((128, 4, NSC, 2), np.float32)
        for pi in range(4):
            d, hh = divmod(pi, 2)
            idx = prob_order(d)
            qT.append(r["qkv"][0 + hh][:, idx])
            kk = r["qkv"][2 + hh][:, idx]
            kT.append(kk)
            ktm.append(kk.T)
            vtm.append(r["qkv"][4 + hh][:, idx].T)
            bcol[:, pi, :] = col_layout(r["beta"][pi][idx])
            gc = col_layout(r["g"][pi][idx])
            gcol[:, pi, :, 0] = gc
            gcol[:, pi, :, 1] = gc
        maps.append({"qT": np.ascontiguousarray(np.stack(qT)), "kT": np.ascontiguousarray(np.stack(kT)),
                     "ktm": np.ascontiguousarray(np.stack(ktm)), "vtm": np.ascontiguousarray(np.stack(vtm)),
                     "bcol": bcol, "gcol": gcol, "msk": msk})
    if nc is None:
        return maps
    return run(nc, maps)


KSC = DQK_B ** -0.5
NEGBIG = -1.0e30


def mlstm_problem(P, pi, msk, dr, h_out, banks, nsc=NSC):
    f = lambda nm, shape=(128, 128): P.sbuf(f"{nm}_m{pi}", list(shape))
    if len(banks) == 4:
        qA = lambda: V(banks[0])
        qB = lambda: V(banks[1])
        qC = lambda: V(banks[2])
        qD = lambda: V(banks[3])
        qN = qA
        qCp = qB
    else:
        qA = lambda: V(banks[0])[:, 0:128]
        qB = lambda: V(banks[0])[:, 128:256]
        qC = lambda: V(banks[0])[:, 256:384]
        qD = lambda: V(banks[0])[:, 384:512]
        qN = lambda: V(banks[1])
        qCp = lambda: V(banks[0])
    qTs = [f(f"qT{i}", (128, LG * 128)) for i in range(2)]
    kTs = [f(f"kT{i}", (128, LG * 128)) for i in range(2)]
    kts = [f(f"ktm{i}", (128, LG, 128)) for i in range(2)]
    vas = [f(f"vaug{i}", (128, LG, 258)) for i in range(2)]
    igc = f("igc", (128, NSC))
    lfc = f("lfc", (128, NSC, 2))
    cols = [f(f"cols{i}", (128, 8)) for i in range(2)]
    lfb = f("lfb"); igb = f("igb"); brs = [f("brs0"), f("brs1")]; Rm = f("Rm")
    pmx = [f("pmx0"), f("pmx1")]; dmr = f("dmr")
    mtr = f("mtr", (128, 64)); Er = f("Er", (128, 64)); inr = f("inr", (128, 64)); tmpw = f("tmpw", (128, 64))
    W1 = f("W1", (128, 64))
    qd = f("qd")
    wts = [f("wts0"), f("wts1")]
    kw = f("kw")
    Cn = f("Cn", (128, 258))
    ms = [f("m0", (128, 1)), f("m1", (128, 1))]
    sm = [f("sm0", (128, 8)), f("sm1", (128, 8))]
    hsb = [f("hsb0", (128, 256)), f("hsb1", (128, 256))]
    P.amemset(V(Cn), 0.0)
    P.amemset(V(ms[0]), 0.0)
    for w_ in wts:
        P.amemset(V(w_), 0.0, eng="pool")
    for va in vas:
        P.amemset(V(va)[:, :, 256:258], 1.0, eng="pool")
    P.aload(V(igc), dr["igc"][:, pi, :])
    P.aload(V(lfc), dr["lfc"][:, pi, :, :])
    U = V(msk, slice(None), M_U, slice(None))
    NUm = V(msk, slice(None), M_NU, slice(None))
    ID = V(msk, slice(None), M_ID, slice(None))
    NLm = V(msk, slice(None), M_NL, slice(None))
    ONESm = V(msk, slice(None), M_ONES, slice(None))
    NEGU = V(msk, slice(None), M_NEGU, slice(None))
    mi = 0
    yield
    for sc in range(nsc):
        t0 = sc * 128
        gi_, gl_ = divmod(sc, LG)
        if gl_ == 0:
            ng = min(LG, nsc - sc)
            P.aload(V(qTs[gi_ % 2])[:, 0:ng * 128], dr["qT"][pi, :, t0:t0 + ng * 128])
            P.aload(V(kTs[gi_ % 2])[:, 0:ng * 128], dr["kT"][pi, :, t0:t0 + ng * 128])
            P.aload(V(kts[gi_ % 2])[:, 0:ng, :],
                    dr["ktm"][pi, t0:t0 + ng * 128, :].rearrange("(g p) d -> p g d", p=128))
            P.aload(V(vas[gi_ % 2])[:, 0:ng, 0:256],
                    dr["vtm"][pi, t0:t0 + ng * 128, :].rearrange("(g p) d -> p g d", p=128))
        qT = V(qTs[gi_ % 2])[:, gl_ * 128:(gl_ + 1) * 128]
        kT = V(kTs[gi_ % 2])[:, gl_ * 128:(gl_ + 1) * 128]
        ktm = V(kts[gi_ % 2])[:, gl_, :]
        va = V(vas[gi_ % 2])[:, gl_, :]
        cl = V(cols[sc % 2])
        bps = qA()
        P.amm(bps[:, 0:2], U, V(lfc)[:, sc, :])
        P.ats(V(lfb), ONESm, V(lfc)[:, sc, 0:1], ALU.mult, eng="pool")
        P.ats(V(igb), ONESm, V(igc)[:, sc:sc + 1], ALU.mult, eng="pool")
        yield
        brow = qB()
        P.amm(brow[:, 0:128], V(lfb), U)
        rrow = qC()
        P.amm(rrow[:, 0:128], V(igb), ID, True, False)
        P.amm(rrow[:, 0:128], V(lfb), NEGU, False, True)
        qk = qD()
        P.amm(qk[:, 0:128], kT, qT)
        P.acopy(cl[:, 0:1], bps[:, 0:1])
        yield
        br = V(brs[sc % 2])
        P.acopy(br, brow[:, 0:128], eng="act")
        P.att(cl[:, 1:2], V(igc)[:, sc:sc + 1], cl[:, 0:1], ALU.subtract)
        P.att(V(Rm), rrow[:, 0:128], NLm, ALU.add)
        pm = V(pmx[sc % 2])
        for c in range(2):
            r = slice(c * 64, c * 64 + 64)
            P.op("dve", lambda e, o=_u(pm[:, r]), d0=_u(ONESm[:, r]), d1=_u(rrow[:, r]):
                 e.tensor_tensor_scan(o, d0, d1, NEGBIG, ALU.mult, ALU.max),
                 _b(ONESm, rrow), _b(pm))
        yield
        P.op("dve", lambda e, o=_u(cl[:, 2:3]), i_=_u(V(Rm)): e.tensor_reduce(o, i_, AX.X, ALU.max),
             _b(V(Rm)), _b(cl))
        P.att(V(dmr), br, pm, ALU.add)
        yield
        P.att(cl[:, 3:4], cl[:, 0:1], cl[:, 2:3], ALU.add)
        yield
        hb_ = V(hsb[sc % 2])
        for c in range(2):
            r = slice(c * 64, c * 64 + 64)
            last = c * 64 + 63
            m_old = V(ms[mi]); m_new = V(ms[1 - mi])
            s_ = V(sm[c])
            P.astt(V(mtr), br[:, r], m_old[:, 0:1], V(dmr)[:, r], ALU.add, ALU.max)
            P.astt(s_[:, 0:1], cl[:, 0:1], m_old[:, 0:1], cl[:, 3:4], ALU.add, ALU.max)
            P.att(s_[:, 1:2], m_old[:, 0:1], pm[:, last:last + 1], ALU.max)
            yield
            P.att(V(Er), br[:, r], V(mtr), ALU.subtract)
            P.aact(s_[:, 2:3], s_[:, 0:1], AF.Exp, scale=-1.0)
            P.att(m_new[:, 0:1], s_[:, 1:2], br[:, last:last + 1], ALU.add)
            yield
            P.aact(V(inr), V(Er), AF.Exp, bias=m_old[:, 0:1])
            P.att(V(tmpw)[r, :], V(Er)[r, :], NUm[r, r], ALU.add, eng="pool")
            P.att(s_[:, 3:4], br[:, last:last + 1], m_new[:, 0:1], ALU.subtract)
            yield
            P.att(V(qd)[:, r], qT[:, r], V(inr), ALU.mult, eng="pool")
            P.aact(V(W1)[r, :], V(tmpw)[r, :], AF.Exp, bias=cl[r, 1:2])
            P.aact(s_[:, 4:5], cl[:, 1:2], AF.Exp, bias=s_[:, 3:4])
            P.aact(s_[:, 5:6], s_[:, 3:4], AF.Exp, bias=m_old[:, 0:1])
            yield
            wt = V(wts[c])
            P.astt(wt[r, r], qk[r, r], KSC, V(W1)[r, :], ALU.mult, ALU.mult)
            P.ats(V(kw)[r, :], ktm[r, :], s_[r, 4:5], ALU.mult, KSC, ALU.mult)
            yield
            nps = qN()
            P.amm(nps[r, 0:258], V(qd)[:, r], V(Cn), True, False)
            P.amm(nps[r, 0:258], wt[:, r], va, False, True)
            cps = qCp()
            P.amm(cps[:, 0:258], V(kw)[r, :], va[r, :])
            yield
            P.ats(s_[r, 7:8], nps[r, 256:257], -1.0, ALU.mult)
            P.astt(s_[r, 6:7], nps[r, 256:257], s_[r, 2:3], s_[r, 7:8], ALU.max, ALU.max)
            P.astt(V(Cn), V(Cn), s_[:, 5:6], cps[:, 0:258], ALU.mult, ALU.add)
            yield
            P.op("dve", lambda e, o=_u(s_[r, 7:8]), i_=_u(s_[r, 6:7]): e.reciprocal(o, i_), _b(s_), _b(s_))
            yield
            P.ats(hb_[r, :], nps[r, 0:256], s_[r, 7:8], ALU.mult)
            mi = 1 - mi
            yield
        P.astore(h_out[pi, t0:t0 + 128, :], hb_)
        yield


def mlstm_drams(P, pre=""):
    dr = {
        "qT": P.dram(pre + "qT", [2, 128, TT], F32, "ExternalInput"),
        "kT": P.dram(pre + "kT", [2, 128, TT], F32, "ExternalInput"),
        "ktm": P.dram(pre + "ktm", [2, TT, 128], F32, "ExternalInput"),
        "vtm": P.dram(pre + "vtm", [2, TT, 256], F32, "ExternalInput"),
        "igc": P.dram(pre + "igc", [128, 2, NSC], F32, "ExternalInput"),
        "lfc": P.dram(pre + "lfc", [128, 2, NSC, 2], F32, "ExternalInput"),
    }
    h_out = P.dram("h", [2, TT, 256], F32, "ExternalOutput")
    return dr, h_out


def build_scan(nsc=NSC):
    P = AProg()
    dr = {
        "qT": P.dram("qT", [4, 128, TT], F32, "ExternalInput"),
        "kT": P.dram("kT", [4, 128, TT], F32, "ExternalInput"),
        "ktm": P.dram("ktm", [4, TT, 128], F32, "ExternalInput"),
        "vtm": P.dram("vtm", [4, TT, 128], F32, "ExternalInput"),
        "bcol": P.dram("bcol", [128, 4, NSC], F32, "ExternalInput"),
        "gcol": P.dram("gcol", [128, 4, NSC, 2], F32, "ExternalInput"),
    }
    mdr, h_out = mlstm_drams(P, "m_")
    mskd = P.dram("msk", [128, NMSK, 128], F32, "ExternalInput")
    o_out = P.dram("o", [4, TT, 128], F32, "ExternalOutput")
    P.alloc_psum_banks(8)
    msk = P.sbuf("msk_sb", [128, NMSK, 128])
    P.aload(V(msk), mskd[:, :, :])
    gens = []
    for pi in range(4):
        S = P.sbuf(f"S_{pi}", [128, 128])
        P.amemset(V(S), 0.0)
        gens.append(delta_problem(P, pi, msk, dr, S, o_out, (P.psum_banks[pi], P.psum_banks[pi]), nsc))
    for pi in range(2):
        gens.append(mlstm_problem(P, pi, msk, mdr, h_out, P.psum_banks[4 + 2 * pi:6 + 2 * pi], nsc))
    run_round_robin(gens)
    return P.finish()


def build_mlstm(nsc=NSC, nprob=2):
    P = AProg()
    dr, h_out = mlstm_drams(P)
    mskd = P.dram("msk", [128, NMSK, 128], F32, "ExternalInput")
    P.alloc_psum_banks(8)
    msk = P.sbuf("msk_sb", [128, NMSK, 128])
    P.aload(V(msk), mskd[:, :, :])
    gens = []
    for pi in range(nprob):
        gens.append(mlstm_problem(P, pi, msk, dr, h_out, P.psum_banks[4 * pi:4 * pi + 4], nsc))
    run_round_robin(gens)
    return P.finish()


def colmajor_perm():
    rows = SEQ // GRID_W
    return np.arange(SEQ).reshape(rows, GRID_W).T.reshape(-1)


def mlstm_order(d):
    ci = np.arange(CTX)
    li = CTX + colmajor_perm()
    if d == 1:
        ci = ci[::-1]
        li = li[::-1]
    return np.concatenate([ci, li])


def stage_mlstm(nc, proj_res, conv_res):
    msk = make_masks()
    maps = []
    for j in range(NCORES):
        pT = proj_res[j]["pT"]
        r = conv_res[j]
        q = pT[CH_BQ * 128:(CH_BQ + 1) * 128]
        k = pT[CH_BK * 128:(CH_BK + 1) * 128]
        v = pT[CH_BV[0] * 128:(CH_BV[1] + 1) * 128]
        qT, kT, ktm, vtm = [], [], [], []
        igc = np.zeros((128, 2, NSC), np.float32)
        lfc = np.zeros((128, 2, NSC, 2), np.float32)
        for d in range(2):
            idx = mlstm_order(d)
            qT.append(q[:, idx])
            kk = k[:, idx]
            kT.append(kk)
            ktm.append(kk.T)
            vtm.append(v[:, idx].T)
            igc[:, d, :] = col_layout(r["ig"][d][idx])
            lc = col_layout(r["lf"][d][idx])
            lfc[:, d, :, 0] = lc
            lfc[:, d, :, 1] = lc
        maps.append({"qT": np.ascontiguousarray(np.stack(qT)), "kT": np.ascontiguousarray(np.stack(kT)),
                     "ktm": np.ascontiguousarray(np.stack(ktm)), "vtm": np.ascontiguousarray(np.stack(vtm)),
                     "igc": igc, "lfc": lfc, "msk": msk})
    if nc is None:
        return maps
    return run(nc, maps)


def stage_scan(nc, proj_res, conv_res):
    dm = stage_delta(None, conv_res)
    mm_ = stage_mlstm(None, proj_res, conv_res)
    maps = []
    for j in range(NCORES):
        mp = dict(dm[j])
        for k, v in mm_[j].items():
            if k != "msk":
                mp["m_" + k] = v
        maps.append(mp)
    res = run(nc, maps)
    return [{"o": r["o"]} for r in res], [{"h": r["h"]} for r in res]


O_ARR = ("oaf", "oab", "hbf", "hbb", "az", "bo", "bz", "ga", "gb", "x")


def build_merge(has_ctx, final, ntiles=None):
    P = AProg()
    NT = (64 if has_ctx else 0) + 2048
    tiles = ([(0, 64, 1)] if has_ctx else []) + [((64 if has_ctx else 0) + i * 512, 512, 0) for i in range(4)]
    if ntiles is not None:
        tiles = tiles[:ntiles]
    dr = {nm: P.dram(nm, [D, NT], F32, "ExternalInput").rearrange("(k p) t -> p k t", p=128) for nm in O_ARR}
    Wd = {nm: P.dram(nm, [D, D], F32, "ExternalInput").rearrange("(k p) c -> p k c", p=128)
          for nm in ("w_pa", "w_pb", "w_out")}
    prm = P.dram("prm", [128, 4, 8], F32, "ExternalInput")
    outT = P.dram("outT", [D, NT], F32, "ExternalOutput").rearrange("(k p) t -> p k t", p=128)
    P.alloc_psum_banks(8)
    ones = P.sbuf("ones", [128, 128])
    P.amemset(V(ones), 1.0)
    prms = P.sbuf("prms", [128, 4, 8])
    P.aload(V(prms), prm[:, :, :])
    W16 = {}
    stg = [P.sbuf(f"wst{i}", [128, 8, 256]) for i in range(2)]
    si = 0
    for nm in ("w_pa", "w_pb", "w_out"):
        W16[nm] = P.sbuf(f"{nm}16", [128, 8, D], BF16)
        for half in range(4):
            s_ = V(stg[si % 2])
            P.aload(s_, Wd[nm][:, :, half * 256:(half + 1) * 256])
            P.acopy(V(W16[nm])[:, :, half * 256:(half + 1) * 256], s_, eng=("dve" if si % 2 == 0 else "pool"))
            si += 1
    yaT = P.sbuf("yaT", [128, 8, 512], BF16)
    ybT = P.sbuf("ybT", [128, 8, 512], BF16)
    yT = P.sbuf("yT", [128, 8, 512], BF16)
    xn = P.sbuf("xn", [128, 8, 512])
    NB = 2
    bufs = {nm: [P.sbuf(f"in_{nm}{i}", [128, 512]) for i in range(NB)] for nm in O_ARR}
    cnt = {nm: 0 for nm in O_ARR}

    def load(nm, k, t0, n):
        b = V(bufs[nm][cnt[nm] % NB])[:, 0:n]
        cnt[nm] += 1
        P.aload(b, dr[nm][:, k, t0:t0 + n])
        return b

    tmpn = [0]

    def tmp(tag, dtype=F32):
        key = f"t_{tag}"
        if key not in bufs:
            bufs[key] = [P.sbuf(f"{key}{i}", [128, 512], dtype) for i in range(2)]
            cnt[key] = 0
        b = V(bufs[key][cnt[key] % 2])
        cnt[key] += 1
        return b

    epsb = P.sbuf("epsb", [128, 1])
    P.amemset(V(epsb), EPS)

    def rstd_from(ss_ps, n, dim):
        r = tmp("rstd")[:, 0:n]
        P.aact(r, ss_ps, AF.Ln, bias=V(epsb)[:, 0:1], scale=1.0 / dim)
        P.aact(r, r, AF.Exp, scale=-0.5)
        return r

    for (t0, n, is_ctx) in tiles:
        for k in range(8):
            f_ = load("oaf", k, t0, n)
            b_ = load("oab", k, t0, n)
            oa = tmp("oa")[:, 0:n]
            P.att(oa, f_, b_, ALU.add)
            sq = tmp("sq")[:, 0:n]
            P.aact(sq, oa, AF.Square)
            ss = V(P.bank())[:, 0:n]
            P.amm(ss, V(ones), sq)
            r = rstd_from(ss, n, 128)
            az = load("az", k, t0, n)
            sz = tmp("sz")[:, 0:n]
            P.aact(sz, az, AF.Silu)
            t1 = tmp("t1")[:, 0:n]
            P.att(t1, oa, r, ALU.mult)
            P.astt(V(yaT)[:, k, 0:n], sz, V(prms)[:, 3, 0:1], t1, ALU.mult, ALU.mult)
        for hd in range(4):
            hbs = []
            ss = V(P.bank())[:, 0:n]
            for c in range(2):
                k = hd * 2 + c
                f_ = load("hbf", k, t0, n)
                b_ = load("hbb", k, t0, n)
                hb = tmp(f"hb{c}")[:, 0:n]
                P.att(hb, f_, b_, ALU.add)
                sq = tmp("sq")[:, 0:n]
                P.aact(sq, hb, AF.Square)
                P.amm(ss, V(ones), sq, c == 0, c == 1)
                hbs.append(hb)
            r = rstd_from(ss, n, 256)
            for c in range(2):
                k = hd * 2 + c
                bo = load("bo", k, t0, n)
                bz = load("bz", k, t0, n)
                so = tmp("so")[:, 0:n]
                P.aact(so, bo, AF.Sigmoid)
                sz = tmp("sz")[:, 0:n]
                P.aact(sz, bz, AF.Silu)
                t4 = tmp("t4")[:, 0:n]
                P.att(t4, so, sz, ALU.mult, eng="pool")
                t1 = tmp("t1")[:, 0:n]
                P.att(t1, hbs[c], r, ALU.mult)
                P.astt(V(ybT)[:, k, 0:n], t4, V(prms)[:, 3, 1 + c:2 + c], t1, ALU.mult, ALU.mult)
        for m in range(8):
            za = V(P.bank())[:, 0:n]
            for k in range(8):
                P.amm(za, V(W16["w_pa"])[:, k, m * 128:(m + 1) * 128], V(yaT)[:, k, 0:n], k == 0, k == 7)
            zb = V(P.bank())[:, 0:n]
            for k in range(8):
                P.amm(zb, V(W16["w_pb"])[:, k, m * 128:(m + 1) * 128], V(ybT)[:, k, 0:n], k == 0, k == 7)
            ga = load("ga", m, t0, n)
            gb = load("gb", m, t0, n)
            sga = tmp("sga")[:, 0:n]
            P.aact(sga, ga, AF.Sigmoid)
            sgb = tmp("sgb")[:, 0:n]
            P.aact(sgb, gb, AF.Sigmoid)
            ta = tmp("ta")[:, 0:n]
            P.att(ta, za, sga, ALU.mult)
            tb = tmp("tb")[:, 0:n]
            P.att(tb, zb, sgb, ALU.mult)
            P.att(V(yT)[:, m, 0:n], ta, tb, ALU.add)
        for m in range(8):
            ops_ = V(P.bank())[:, 0:n]
            for k in range(8):
                P.amm(ops_, V(W16["w_out"])[:, k, m * 128:(m + 1) * 128], V(yT)[:, k, 0:n], k == 0, k == 7)
            xk = load("x", m, t0, n)
            P.astt(V(xn)[:, m, 0:n], ops_, V(prms)[:, 1 if is_ctx else 0, m:m + 1], xk, ALU.mult, ALU.add)
        if final:
            sqf = V(yT)
            ss = V(P.bank())[:, 0:n]
            for k in range(8):
                sq = tmp("sq")[:, 0:n]
                P.aact(sq, V(xn)[:, k, 0:n], AF.Square)
                P.amm(ss, V(ones), sq, k == 0, k == 7)
            r = rstd_from(ss, n, D)
            for k in range(8):
                t1 = tmp("t1")[:, 0:n]
                P.att(t1, V(xn)[:, k, 0:n], r, ALU.mult)
                o_ = tmp("fo")[:, 0:n]
                P.ats(o_, t1, V(prms)[:, 2, k:k + 1], ALU.mult)
                P.astore(outT[:, k, t0:t0 + n], o_)
        else:
            for k in range(8):
                P.astore(outT[:, k, t0:t0 + n], V(xn)[:, k, 0:n])
    return P.finish()


def core_tokens(qtr, has_ctx):
    lat = CTX + np.arange(qtr * 2048, (qtr + 1) * 2048)
    if has_ctx:
        return np.concatenate([np.arange(qtr * 64, (qtr + 1) * 64), lat])
    return lat


def stage_merge(nc, has_ctx, proj_res, delta_res, mlstm_res, xT_b, mod_l, inp, l, final):
    full = []
    for b in range(BATCH):
        A = {nm: np.empty((D, TT), np.float32) for nm in O_ARR if nm != "x"}
        for g in range(4):
            j = b * 4 + g
            pT = proj_res[j]["pT"]
            for hh in range(2):
                h = 2 * g + hh
                A["az"][h * 128:(h + 1) * 128] = pT[CH_AZ[hh] * 128:(CH_AZ[hh] + 1) * 128]
                for d, nm in ((0, "oaf"), (1, "oab")):
                    o = delta_res[j]["o"][d * 2 + hh]
                    nat = np.empty_like(o)
                    nat[prob_order(d)] = o
                    A[nm][h * 128:(h + 1) * 128] = nat.T
            for nm, ch in (("bo", CH_BO), ("bz", CH_BZ), ("ga", CH_GA), ("gb", CH_GB)):
                A[nm][g * 256:(g + 1) * 256] = pT[ch[0] * 128:(ch[1] + 1) * 128]
            for d, nm in ((0, "hbf"), (1, "hbb")):
                hv = mlstm_res[j]["h"][d]
                nat = np.empty_like(hv)
                nat[mlstm_order(d)] = hv
                A[nm][g * 256:(g + 1) * 256] = nat.T
        A["x"] = xT_b[b]
        full.append(A)
    maps = []
    for j in range(NCORES):
        b, qtr = divmod(j, 4)
        tok = core_tokens(qtr, has_ctx)
        mp = {nm: np.ascontiguousarray(full[b][nm][:, tok]) for nm in O_ARR}
        prm = np.zeros((128, 4, 8), np.float32)
        prm[:, 0, :] = feat_major(mod_l[b][2 * D:3 * D])
        prm[:, 1, :] = feat_major(mod_l[2][2 * D:3 * D])
        prm[:, 2, :] = feat_major(inp["final_g"])
        prm[:, 3, 0] = inp["norm_a_g"][l]
        prm[:, 3, 1] = inp["norm_b_g"][l][0:128]
        prm[:, 3, 2] = inp["norm_b_g"][l][128:256]
        mp["prm"] = prm
        for nm in ("w_pa", "w_pb", "w_out"):
            mp[nm] = np.ascontiguousarray(inp[nm][l])
        maps.append(mp)
    res = run(nc, maps)
    out = [np.array(xT_b[b]) for b in range(BATCH)]
    for j in range(NCORES):
        b, qtr = divmod(j, 4)
        tok = core_tokens(qtr, has_ctx)
        out[b][:, tok] = res[j]["outT"]
    return out


def kernel(x, c, ctx, c_ctx, ada_w, ada_b, norm_g, w_in, conv_w, a_log, dt_bias, norm_a_g,
           i_bias, f_bias, norm_b_g, w_pa, w_pb, w_out, final_g):
    inp = {k: np.asarray(v, dtype=np.float32) for k, v in dict(
        x=x, c=c, ctx=ctx, c_ctx=c_ctx, ada_w=ada_w, ada_b=ada_b, norm_g=norm_g, w_in=w_in,
        conv_w=conv_w, a_log=a_log, dt_bias=dt_bias, norm_a_g=norm_a_g, i_bias=i_bias,
        f_bias=f_bias, norm_b_g=norm_b_g, w_pa=w_pa, w_pb=w_pb, w_out=w_out, final_g=final_g).items()}
    mod = stage_mod(inp)
    xT_b = [np.ascontiguousarray(np.concatenate([inp["ctx"][b], inp["x"][b]], axis=0).T)
            for b in range(BATCH)]
    nc_pc = build_projconv()
    nc_scan = build_scan()
    for l in range(2):
        proj_res, conv_res = stage_projconv(nc_pc, xT_b, inp["w_in"][l], inp["norm_g"][l], mod[l],
                                            inp["conv_w"][l], inp["a_log"][l], inp["dt_bias"][l],
                                            inp["i_bias"][l], inp["f_bias"][l])
        delta_res, mlstm_res = stage_scan(nc_scan, proj_res, conv_res)
        last = l == 1
        nc_merge = build_merge(not last, last)
        xT_b = stage_merge(nc_merge, not last, proj_res, delta_res, mlstm_res, xT_b, mod[l], inp, l, last)
        del proj_res, conv_res, delta_res, mlstm_res
    out = np.stack([np.ascontiguousarray(xT_b[b][:, CTX:].T) for b in range(BATCH)])
    return out.astype(np.float32)


NWPC = NCH * 128 + 128


def build_projconv(ntiles=None):
    P = Prog()
    xT = P.dram("xT", [D, TT], F32, "ExternalInput")
    Wg = P.dram("Wg", [D, NWPC], F32, "ExternalInput")
    prm = P.dram("prm", [128, 5, 8], F32, "ExternalInput")
    cw = P.dram("cw", [128, 6, 5], F32, "ExternalInput")
    gprm = P.dram("gprm", [128, 4], F32, "ExternalInput")
    pT = P.dram("pT", [14 * 128, TT], F32, "ExternalOutput")
    qkv = P.dram("qkv", [6, 128, TT], F32, "ExternalOutput")
    gout = {nm: P.dram(nm, [r, TT], F32, "ExternalOutput") for nm, r in
            (("beta", 4), ("g", 4), ("ig", 2), ("lf", 2))}
    P.alloc_psum_banks(8)
    W16 = P.sbuf("W16", [128, 8, NWPC], BF16)
    ones = P.sbuf("ones", [128, 128])
    P.memset(ones[:], 1.0, [ones])
    prms = P.sbuf("prms", [128, 5, 8])
    P.dma(prms[:], prm[:, :, :], dst=prms)
    cws = P.sbuf("cws", [128, 6, 5])
    P.dma(cws[:], cw[:, :, :], dst=cws)
    gp = P.sbuf("gp", [128, 4])
    P.dma(gp[:], gprm[:, :], dst=gp)
    gp2 = P.sbuf("gp2", [128, 4])
    P.act(gp2[32:36, 0:1], gp[32:36, 0:1], AF.Exp, [gp], [gp2])
    P.ts(gp2[32:36, 1:2], gp2[32:36, 0:1], -1.0, None, ALU.mult, None, [gp2], [gp2])
    P.ts(gp2[96:98, 2:3], gp[96:98, 3:4], -1.0, None, ALU.mult, None, [gp], [gp2])
    epsb = P.sbuf("epsb", [128, 1])
    P.memset(epsb[:], EPS, [epsb])
    GS = P.sbuf("GS", [128, 4, 8])
    P.stt(GS[:, 0, :], prms[:, 1, :], 1.0, prms[:, 0, :], ALU.add, ALU.mult, [prms], [GS])
    P.copy(GS[:, 1, :], prms[:, 2, :], [prms], [GS])
    P.stt(GS[:, 2, :], prms[:, 3, :], 1.0, prms[:, 0, :], ALU.add, ALU.mult, [prms], [GS])
    P.copy(GS[:, 3, :], prms[:, 4, :], [prms], [GS])
    Wv = Wg.rearrange("(k p) c -> p k c", p=128)
    stg = [P.sbuf(f"wst{i}", [128, 8, 256]) for i in range(2)]
    c0 = 0
    i = 0
    while c0 < NWPC:
        c1 = min(NWPC, c0 + 256)
        s = stg[i % 2]
        P.dma(s[:, :, 0:c1 - c0], Wv[:, :, c0:c1], dst=s)
        P.copy(W16[:, :, c0:c1], s[:, :, 0:c1 - c0], [s], [W16], eng=("dve" if i % 2 == 0 else "pool"))
        c0 = c1
        i += 1
    xv = xT.rearrange("(k p) t -> p k t", p=128)
    xts = [P.sbuf(f"xt{i}", [128, 8, 512]) for i in range(2)]
    sqk = [P.sbuf(f"sqk{i}", [128, 512]) for i in range(2)]
    hTs = [P.sbuf(f"hT{i}", [128, 8, 512], BF16) for i in range(2)]
    rstd = P.sbuf("rstd", [128, 512])
    tmp = [P.sbuf(f"tmp{i}", [128, 512]) for i in range(2)]
    evs = [P.sbuf(f"ev{i}", [128, 512]) for i in range(4)]
    ga = P.sbuf("ga", [128, 512])
    gb = P.sbuf("gb", [128, 512])
    XB = [[P.sbuf(f"xb{c}_{i}", [128, 516]) for i in range(3)] for c in range(6)]
    acc = [P.sbuf(f"acc{i}", [128, 512]) for i in range(3)]
    ys = [P.sbuf(f"y{i}", [128, 512]) for i in range(4)]
    sqs = [P.sbuf(f"sq{i}", [128, 512]) for i in range(2)]
    rs = [P.sbuf(f"r{i}", [128, 512]) for i in range(2)]
    outs = [P.sbuf(f"o{i}", [128, 512]) for i in range(4)]
    cnt = {"acc": 0, "out": 0, "ev": 0}
    tiles = P_TILES[:ntiles] if ntiles else P_TILES
    last_ti = len(tiles) - 1

    def conv_tile(tj):
        t0, t1 = tiles[tj]
        n = t1 - t0
        for c in range(6):
            xi = XB[c][tj % 3]
            a = acc[cnt["acc"] % 3]
            cnt["acc"] += 1
            P.ts(a[:, 0:n], xi[:, 0:n], cws[:, c, 0:1], None, ALU.mult, None, [xi, cws], [a])
            for s_ in range(1, 5):
                P.stt(a[:, 0:n], xi[:, s_:s_ + n], cws[:, c, s_:s_ + 1], a[:, 0:n], ALU.mult, ALU.add,
                      [xi, cws, a], [a])
            if c < 4:
                P.act(ys[c][:, 0:n], a[:, 0:n], AF.Silu, [a], [ys[c]])
            else:
                o = outs[cnt["out"] % 4]
                cnt["out"] += 1
                P.act(o[:, 0:n], a[:, 0:n], AF.Silu, [a], [o])
                P.dma(qkv[c, :, t0:t1], o[:, 0:n], src=o)
        for c in range(4):
            y = ys[c]
            sq = sqs[c % 2]
            r = rs[c % 2]
            o = outs[cnt["out"] % 4]
            cnt["out"] += 1
            P.tt(sq[:, 0:n], y[:, 0:n], y[:, 0:n], ALU.mult, [y], [sq], eng="pool")
            ps = P.bank()
            P.mm(ps[:, 0:n], ones[:], sq[:, 0:n], True, True, [ones, sq], [ps])
            P.act(r[:, 0:n], ps[:, 0:n], AF.Ln, [ps, epsb], [r], bias=epsb[:, 0:1])
            P.act(r[:, 0:n], r[:, 0:n], AF.Exp, [r], [r], scale=-0.5)
            sc = (DK_A ** -0.5) if c < 2 else 1.0
            P.stt(o[:, 0:n], y[:, 0:n], sc, r[:, 0:n], ALU.mult, ALU.mult, [y, r], [o])
            P.dma(qkv[c, :, t0:t1], o[:, 0:n], src=o)

    for ti, (t0, t1) in enumerate(tiles):
        n = t1 - t0
        xt = xts[ti % 2]
        hT = hTs[ti % 2]
        P.dma(xt[:, :, 0:n], xv[:, :, t0:t1], dst=xt)
        ss = P.bank()
        for k in range(8):
            sq = sqk[k % 2]
            P.act(sq[:, 0:n], xt[:, k, 0:n], AF.Square, [xt], [sq])
            P.mm(ss[:, 0:n], ones[:], sq[:, 0:n], k == 0, k == 7, [ones, sq], [ss])
        P.act(rstd[:, 0:n], ss[:, 0:n], AF.Ln, [ss, epsb], [rstd], bias=epsb[:, 0:1], scale=1.0 / D)
        P.act(rstd[:, 0:n], rstd[:, 0:n], AF.Exp, [rstd], [rstd], scale=-0.5)
        gi = 2 if ti == 0 else 0
        for k in range(8):
            tm = tmp[k % 2]
            P.tt(tm[:, 0:n], xt[:, k, 0:n], rstd[:, 0:n], ALU.mult, [xt, rstd], [tm])
            P.act(hT[:, k, 0:n], tm[:, 0:n], AF.Identity, [tm, GS], [hT],
                  scale=GS[:, gi, k:k + 1], bias=GS[:, gi + 1, k:k + 1])
        for c in range(NCH):
            ps = P.bank()
            for k in range(8):
                P.mm(ps[:, 0:n], W16[:, k, c * 128:(c + 1) * 128], hT[:, k, 0:n], k == 0, k == 7,
                     [W16, hT], [ps])
            if c < 6:
                xb = XB[c][ti % 3]
                P.copy(xb[:, 2:2 + n], ps[:, 0:n], [ps], [xb], eng=("act" if c % 2 else "dve"))
                if ti in (0, 1):
                    P.memset(xb[:, 0:2], 0.0, [xb], eng="pool")
                if ti == 0 or ti == last_ti:
                    P.memset(xb[:, 2 + n:4 + n], 0.0, [xb], eng="pool")
                if ti >= 2:
                    pv = XB[c][(ti - 1) % 3]
                    P.copy(pv[:, 514:516], xb[:, 2:4], [xb], [pv], eng="pool")
                if 1 <= ti < last_ti:
                    nx = XB[c][(ti + 1) % 3]
                    P.copy(nx[:, 0:2], xb[:, n:n + 2], [xb], [nx], eng="pool")
            else:
                ev = evs[cnt["ev"] % 4]
                P.copy(ev[:, 0:n], ps[:, 0:n], [ps], [ev], eng=("act" if cnt["ev"] % 2 else "dve"))
                cnt["ev"] += 1
                P.dma(pT[(c - 6) * 128:(c - 5) * 128, t0:t1], ev[:, 0:n], src=ev)
        ps = P.bank()
        for k in range(8):
            P.mm(ps[:, 0:n], W16[:, k, NCH * 128:NCH * 128 + 128], hT[:, k, 0:n], k == 0, k == 7,
                 [W16, hT], [ps])
        P.act(gb[0:4, 0:n], ps[0:4, 0:n], AF.Sigmoid, [ps], [gb])
        P.dma(gout["beta"][:, t0:t1], gb[0:4, 0:n], src=gb)
        P.act(ga[32:36, 0:n], ps[32:36, 0:n], AF.Exp, [ps, gp], [ga], bias=gp[32:36, 1:2])
        P.act(ga[32:36, 0:n], ga[32:36, 0:n], AF.Ln, [ga], [ga], bias=1.0)
        P.ts(gb[32:36, 0:n], ga[32:36, 0:n], gp2[32:36, 1:2], None, ALU.mult, None, [ga, gp2], [gb])
        P.dma(gout["g"][:, t0:t1], gb[32:36, 0:n], src=gb)
        P.ts(gb[64:66, 0:n], ps[64:66, 0:n], gp[64:66, 2:3], None, ALU.add, None, [ps, gp], [gb])
        P.dma(gout["ig"][:, t0:t1], gb[64:66, 0:n], src=gb)
        P.act(ga[96:98, 0:n], ps[96:98, 0:n], AF.Exp, [ps, gp2], [ga], bias=gp2[96:98, 2:3], scale=-1.0)
        P.act(ga[96:98, 0:n], ga[96:98, 0:n], AF.Ln, [ga], [ga], bias=1.0)
        P.ts(gb[96:98, 0:n], ga[96:98, 0:n], -1.0, None, ALU.mult, None, [ga], [gb])
        P.dma(gout["lf"][:, t0:t1], gb[96:98, 0:n], src=gb)
        if ti == 0:
            conv_tile(0)
        elif ti >= 2:
            conv_tile(ti - 1)
    if last_ti >= 1:
        conv_tile(last_ti)
    return P.finish()


def stage_projconv(nc, xT_b, w_in_l, norm_g_l, mod_l, conv_w_l, a_log_l, dt_bias_l, i_bias_l, f_bias_l):
    maps = []
    for j in range(NCORES):
        b, g = divmod(j, 4)
        cols, gates = core_cols(g)
        Wg = np.zeros((D, NWPC), np.float32)
        Wg[:, :NCH * 128] = w_in_l[:, cols]
        gb0 = NCH * 128
        Wg[:, gb0 + 0:gb0 + 4] = w_in_l[:, gates[0:4]]
        Wg[:, gb0 + 32:gb0 + 36] = w_in_l[:, gates[4:8]]
        Wg[:, gb0 + 64:gb0 + 66] = w_in_l[:, gates[8:10]]
        Wg[:, gb0 + 96:gb0 + 98] = w_in_l[:, gates[10:12]]
        shift, scale = mod_l[b][0:D], mod_l[b][D:2 * D]
        shift_c, scale_c = mod_l[2][0:D], mod_l[2][D:2 * D]
        prm = np.stack([feat_major(norm_g_l), feat_major(scale), feat_major(shift),
                        feat_major(scale_c), feat_major(shift_c)], axis=1)
        hA = (2 * g, 2 * g + 1)
        chans = []
        for base in (0, 1024, 2048):
            for h in hA:
                chans.append(np.arange(base + h * 128, base + (h + 1) * 128))
        cw = np.stack([conv_w_l[:, ch].T for ch in chans], axis=1)
        gprm = np.zeros((128, 4), np.float32)
        k = 0
        for d in range(2):
            for h in hA:
                gprm[32 + k, 0] = a_log_l[d, h]
                gprm[32 + k, 1] = dt_bias_l[d, h]
                k += 1
        for d in range(2):
            gprm[64 + d, 2] = i_bias_l[d, g]
            gprm[96 + d, 3] = f_bias_l[d, g]
        maps.append({"xT": xT_b[b], "Wg": Wg, "prm": np.ascontiguousarray(prm),
                     "cw": np.ascontiguousarray(cw), "gprm": gprm})
    if nc is None:
        return maps
    res = run(nc, maps)
    proj_res, conv_res = [], []
    for j in range(NCORES):
        r = res[j]
        pT_full = np.concatenate([np.zeros((6 * 128, TT), np.float32), r["pT"]], axis=0)
        proj_res.append({"pT": pT_full})
        conv_res.append({k: r[k] for k in ("qkv", "beta", "g", "ig", "lf")})
    return proj_res, conv_res
```
